# Optimizing a Trainium2 kernel written in Bass

```python
import math
import jax, jax.numpy as jnp
from jax import lax
import numpy as np

D_MODEL = 1024
BATCH = 8
SEQ = 2048
DEPTH = 1

D_CONV = 1024
CONV_K = 31
D_RNN = 1280
LRU_HEADS = 16
LRU_BLOCK = D_RNN // LRU_HEADS
LRU_CONV_K = 4
LRU_C = 8.0
D_FF = ((int(math.ceil(8 * D_MODEL / 3)) + 255) // 256) * 256
IN_SPLITS = (2 * D_CONV, D_RNN, D_RNN, D_MODEL, D_MODEL)
D_IN = sum(IN_SPLITS)
EPS = 1e-6

kernel_name = "hybrid_conformer_conv_rglru_gated_block"


def rmsnorm(x, g):
    xf = x.astype(jnp.float32)
    y = xf * lax.rsqrt(jnp.mean(xf * xf, axis=-1, keepdims=True) + EPS)
    return (y * g.astype(jnp.float32)).astype(x.dtype)


def layernorm(x, g, b):
    xf = x.astype(jnp.float32)
    mu = jnp.mean(xf, axis=-1, keepdims=True)
    xc = xf - mu
    var = jnp.mean(xc * xc, axis=-1, keepdims=True)
    y = xc * lax.rsqrt(var + EPS) * g.astype(jnp.float32) + b.astype(jnp.float32)
    return y.astype(x.dtype)


def causal_depthwise_conv(u, w, b):
    k, c = w.shape
    u_pad = jnp.pad(u, ((0, 0), (k - 1, 0), (0, 0)))
    y = lax.conv_general_dilated(
        u_pad, w[:, None, :].astype(u.dtype), window_strides=(1,), padding='VALID',
        dimension_numbers=('NWC', 'WIO', 'NWC'), feature_group_count=c)
    return y + b.astype(u.dtype)


def _linear_recurrence_combine(left, right):
    a1, b1 = left
    a2, b2 = right
    return a1 * a2, a2 * b1 + b2


def rg_lru(xb, wa, ba, wx, bx, lam):
    bsz, s, d = xb.shape
    xf = xb.astype(jnp.float32)
    xh = xf.reshape(bsz, s, LRU_HEADS, LRU_BLOCK)
    r = jax.nn.sigmoid(jnp.einsum('bshi,hij->bshj', xh, wa.astype(jnp.float32)).reshape(bsz, s, d)
                       + ba.astype(jnp.float32))
    i = jax.nn.sigmoid(jnp.einsum('bshi,hij->bshj', xh, wx.astype(jnp.float32)).reshape(bsz, s, d)
                       + bx.astype(jnp.float32))
    log_a = -LRU_C * r * jax.nn.softplus(-lam.astype(jnp.float32))
    a = jnp.exp(log_a)
    mult = jnp.sqrt(-jnp.expm1(2.0 * log_a))
    bterm = mult * (i * xf)
    _, h = lax.associative_scan(_linear_recurrence_combine, (a, bterm), axis=1)
    return h.astype(xb.dtype)


def setup_inputs(seed: int = 0) -> dict:
    key = jax.random.key(seed)
    ks = jax.random.split(key, 32)
    f32 = jnp.float32
    L = DEPTH

    def nrm(k, shape, fan_in, scale=1.0):
        return jax.random.normal(k, shape, f32) * (scale * fan_in ** -0.5)

    def small(k, shape, s=0.02):
        return jax.random.normal(k, shape, f32) * s

    x = jax.random.normal(ks[0], (BATCH, SEQ, D_MODEL), f32)
    a0 = jax.random.uniform(ks[16], (L, D_RNN), f32, 0.9, 0.999)
    lru_lambda = jnp.log(a0) - jnp.log1p(-a0)
    return {
        "x": x,
        "norm1_g": 1.0 + small(ks[1], (L, D_MODEL)),
        "w_in": nrm(ks[2], (L, D_MODEL, D_IN), D_MODEL),
        "b_in": small(ks[3], (L, D_IN)),
        "conv_dw_w": nrm(ks[4], (L, CONV_K, D_CONV), CONV_K),
        "conv_dw_b": small(ks[5], (L, D_CONV)),
        "conv_ln_g": 1.0 + small(ks[6], (L, D_CONV)),
        "conv_ln_b": small(ks[7], (L, D_CONV)),
        "conv_w_out": nrm(ks[8], (L, D_CONV, D_MODEL), D_CONV),
        "conv_b_out": small(ks[9], (L, D_MODEL)),
        "lru_conv_w": nrm(ks[10], (L, LRU_CONV_K, D_RNN), LRU_CONV_K),
        "lru_conv_b": small(ks[11], (L, D_RNN)),
        "lru_wa": nrm(ks[12], (L, LRU_HEADS, LRU_BLOCK, LRU_BLOCK), LRU_BLOCK),
        "lru_ba": small(ks[13], (L, D_RNN)),
        "lru_wx": nrm(ks[14], (L, LRU_HEADS, LRU_BLOCK, LRU_BLOCK), LRU_BLOCK),
        "lru_bx": small(ks[15], (L, D_RNN)),
        "lru_lambda": lru_lambda,
        "lru_w_out": nrm(ks[17], (L, D_RNN, D_MODEL), D_RNN),
        "w_mix_out": nrm(ks[18], (L, D_MODEL, D_MODEL), D_MODEL),
        "norm2_g": 1.0 + small(ks[19], (L, D_MODEL)),
        "ffn_w1": nrm(ks[20], (L, D_MODEL, D_FF), D_MODEL),
        "ffn_w3": nrm(ks[21], (L, D_MODEL, D_FF), D_MODEL),
        "ffn_w2": nrm(ks[22], (L, D_FF, D_MODEL), D_FF),
        "norm_f_g": 1.0 + small(ks[23], (D_MODEL,)),
    }


def reference(x, norm1_g, w_in, b_in, conv_dw_w, conv_dw_b, conv_ln_g, conv_ln_b,
              conv_w_out, conv_b_out, lru_conv_w, lru_conv_b, lru_wa, lru_ba, lru_wx,
              lru_bx, lru_lambda, lru_w_out, w_mix_out, norm2_g, ffn_w1, ffn_w3, ffn_w2,
              norm_f_g):
    split_idx = list(np.cumsum(IN_SPLITS)[:-1])
    for l in range(DEPTH):
        h = rmsnorm(x, norm1_g[l])
        z = h @ w_in[l] + b_in[l]
        z_glu, z_rx, z_rg, z_ga, z_gb = jnp.split(z, split_idx, axis=-1)

        u = z_glu[..., :D_CONV] * jax.nn.sigmoid(z_glu[..., D_CONV:])
        u = causal_depthwise_conv(u, conv_dw_w[l], conv_dw_b[l])
        u = jax.nn.silu(layernorm(u, conv_ln_g[l], conv_ln_b[l]))
        y_a = u @ conv_w_out[l] + conv_b_out[l]

        xb = causal_depthwise_conv(z_rx, lru_conv_w[l], lru_conv_b[l])
        hb = rg_lru(xb, lru_wa[l], lru_ba[l], lru_wx[l], lru_bx[l], lru_lambda[l])
        y_b = (hb * jax.nn.gelu(z_rg, approximate=True)) @ lru_w_out[l]

        m = jax.nn.sigmoid(z_ga) * y_a + jax.nn.sigmoid(z_gb) * y_b
        x = x + m @ w_mix_out[l]

        h = rmsnorm(x, norm2_g[l])
        x = x + (jax.nn.silu(h @ ffn_w1[l]) * (h @ ffn_w3[l])) @ ffn_w2[l]
    return rmsnorm(x, norm_f_g)
```

```python
import numpy as np
from contextlib import ExitStack
import concourse.bass as bass
import concourse.mybir as mybir
from concourse.bass_utils import run_bass_kernel_spmd

F32 = mybir.dt.float32
BF16 = mybir.dt.bfloat16
AF = mybir.ActivationFunctionType
ALU = mybir.AluOpType

D = 1024
S = 2048
NT = 1024
NS = S // NT
DR = 1280
DFF = 2816
DIN = 6656
OFF_A, OFF_B, OFF_RX, OFF_RG, OFF_GA, OFF_GB = 0, 1024, 2048, 3328, 4608, 5632
EPS = 1e-6

P_BIN, P_LCB, P_BA, P_BX, P_LNG, P_LNB, P_CW31 = 0, 52, 62, 72, 82, 90, 98
NHALF = 354
P_G1, P_CDB, P_CBO, P_LAM, P_G2, P_GF, P_CW4 = 354, 362, 370, 378, 388, 396, 404
NPP = 444
D_C, D_CH = 354, 364
D_BQ = 374
NDP = 384


class Buf:
    def __init__(self, name, ap):
        self.name = name
        self.ap = ap
        self.w = []
        self.r = []
        self.sem = None


class Prog:
    ENGS = ("pe", "act", "dve", "pool", "sp")

    def __init__(self, nc, es):
        self.nc = nc
        self.es = es
        self.sems = {}
        self.semval = {}
        self.ops = {e: [] for e in self.ENGS}
        self.waited = {e: {} for e in self.ENGS}
        for e in self.ENGS:
            self.newsem("c_" + e)

    def newsem(self, key):
        self.sems[key] = self.es.enter_context(self.nc.semaphore(key))
        self.semval[key] = 0
        return key

    def bufsem(self, b):
        if b.sem is None:
            b.sem = self.newsem("s_" + b.name)
        return b.sem

    def _waits(self, eng, reads, writes):
        own = "c_" + eng
        need = {}

        def add(d, same_ok):
            k, v = d
            if k == own and (eng == "pe" or not same_ok):
                return
            if v > need.get(k, 0):
                need[k] = v
        for b in reads:
            for d in b.w:
                add(d, True)
        for b in writes:
            for d in b.w:
                add(d, True)
            for d in b.r:
                add(d, True)
        out = []
        wd = self.waited[eng]
        for k, v in need.items():
            if wd.get(k, 0) >= v:
                continue
            wd[k] = v
            out.append((k, v))
        return out

    def _mark(self, dep, reads, writes):
        for b in reads:
            b.r.append(dep)
        for b in writes:
            b.w = [dep]
            b.r = []

    def op(self, eng, fn, reads=(), writes=()):
        waits = self._waits(eng, reads, writes)
        key = "c_" + eng
        self.semval[key] += 1
        dep = (key, self.semval[key])
        self.ops[eng].append((waits, fn, (key, 1)))
        self._mark(dep, reads, writes)
        return dep

    def dma(self, eng, fn, semkey, reads=(), writes=(), n=1):
        waits = self._waits(eng, reads, writes)
        self.semval[semkey] += 16 * n
        dep = (semkey, self.semval[semkey])
        self.ops[eng].append((waits, fn, (semkey, 16)))
        self._mark(dep, reads, writes)
        return dep

    def wait_all(self, eng, deps):
        waits = []
        wd = self.waited[eng]
        for k, v in deps:
            if wd.get(k, 0) < v:
                wd[k] = v
                waits.append((k, v))
        self.ops[eng].append((waits, None, None))

    def emit(self):
        nc = self.nc
        with nc.Block() as block:
            def run(name):
                def f(e):
                    for waits, fn, incs in self.ops[name]:
                        for k, v in waits:
                            e.wait_ge(self.sems[k], v)
                        if fn is None:
                            continue
                        ins = fn(e)
                        if incs is not None:
                            if isinstance(ins, (list, tuple)):
                                for i_ in ins:
                                    i_.then_inc(self.sems[incs[0]], incs[1])
                            else:
                                ins.then_inc(self.sems[incs[0]], incs[1])
                return f
            block.tensor(run("pe"))
            block.scalar(run("act"))
            block.vector(run("dve"))
            block.gpsimd(run("pool"))
            block.sync(run("sp"))


def inherit(news, olds):
    deps = []
    for b in olds:
        deps.extend(b.w)
        deps.extend(b.r)
    for n in news:
        n.r = list(n.r) + deps


class Unit:
    def __init__(self, run, wl=None, dl=None):
        self.run = run
        self.wl = wl
        self.dl = dl


def build_program(max_units=None):
    nc = bass.Bass("TRN2", target_bir_lowering=False)
    es = ExitStack()

    def din(name, shape):
        return nc.dram_tensor(name, shape, F32, kind="ExternalInput").ap()
    xT = din("xT", [D, S])
    w_in = din("w_in", [D, DIN])
    w_co = din("w_co", [D, D])
    w_lo = din("w_lo", [DR, D])
    w_mx = din("w_mx", [D, D])
    w_f1 = din("w_f1", [D, DFF])
    w_f3 = din("w_f3", [D, DFF])
    w_f2 = din("w_f2", [DFF, D])
    wa_bd = din("wa_bd", [DR, DR])
    wx_bd = din("wx_bd", [DR, DR])
    ppd = din("pp", [128, NPP])
    identd = din("ident", [128, 128])
    ematd = din("emat", [128, 32])
    outT = nc.dram_tensor("outT", [D, S], F32, kind="ExternalOutput").ap()

    def wv(w):
        return w.rearrange("(k p) n -> p k n", p=128)
    w_in_v, w_co_v, w_lo_v, w_mx_v = wv(w_in), wv(w_co), wv(w_lo), wv(w_mx)
    w_f1_v, w_f3_v, w_f2_v, wa_v, wx_v = wv(w_f1), wv(w_f3), wv(w_f2), wv(wa_bd), wv(wx_bd)

    with es:
        P = Prog(nc, es)

        def sb(name, shape, dt):
            return es.enter_context(nc.sbuf_tensor(name, shape, dt))

        pp = Buf("pp", sb("pp_sb", [128, NPP], F32))
        dp = Buf("dp", sb("dp_sb", [128, NDP], F32))
        cst = Buf("cst", sb("cst_sb", [128, 4], F32))
        tmp10 = Buf("tmp10", sb("tmp10", [128, 16], F32))
        hstate = Buf("hstate", sb("hstate", [128, 16], F32))
        ident = Buf("ident", sb("ident_sb", [128, 128], BF16))
        ones = Buf("ones", sb("ones_sb", [128, 128], BF16))
        emat = Buf("emat", sb("emat_sb", [128, 32], BF16))
        Ust = [Buf(f"Ust{g}", sb(f"Ust{g}", [128, 1032], BF16)) for g in range(4)]
        uhalo = Buf("uhalo", sb("uhalo", [128, 8 * 32], BF16))
        zhalo = Buf("zhalo", sb("zhalo", [128, 10 * 4], BF16))
        regA = sb("regA", [128, 16384], BF16)
        regB = sb("regB", [128, 13824], BF16)
        regC = sb("regC", [128, 26 * 1024], BF16)
        NT32 = 11
        t32s = [Buf(f"t32_{i}", sb(f"t32_{i}", [128, NT], F32)) for i in range(NT32)]
        t16s = [Buf(f"t16_{i}", sb(f"t16_{i}", [128, NT], BF16)) for i in range(2)]
        rstd_bc = Buf("rstd_bc", sb("rstd_bc", [128, NT], F32))
        mu_bc = Buf("mu_bc", sb("mu_bc", [128, NT], F32))
        NW = 3
        wslots = [Buf(f"ws{i}", sb(f"ws{i}", [128, 2816], BF16)) for i in range(NW)]
        ND = 2
        dslots = [Buf(f"ds{i}", sb(f"ds{i}", [128, 1024], BF16)) for i in range(ND)]
        pss = [Buf(f"ps{i}", es.enter_context(nc.psum_tensor(f"ps{i}", [128, NT], F32))) for i in range(4)]

        ybf = [Buf(f"ybf{j}", regA[:, j * 1024:(j + 1) * 1024]) for j in range(8)]
        zrx = [Buf(f"zrx{j}", regA[:, 8192 + j * 1028:8192 + (j + 1) * 1028]) for j in range(5)]
        x1 = [Buf(f"x1_{j}", regA[:, j * 2048:(j + 1) * 2048].bitcast(F32)) for j in range(8)]
        upad = [Buf(f"upad{j}", regB[:, j * 1056:(j + 1) * 1056]) for j in range(8)]
        tgbs = [Buf(f"tgbs{j}", regA[:, 8192 + j * 1024:8192 + (j + 1) * 1024]) for j in range(8)]
        xbbf = [Buf(f"xbbf{j}", regB[:, 8448 + j * 1024:8448 + (j + 1) * 1024]) for j in range(5)]
        mb = [Buf(f"m{j}", regB[:, j * 1024:(j + 1) * 1024]) for j in range(8)]
        h2 = [Buf(f"h2_{j}", regC[:, 22528 + j * 1024:22528 + (j + 1) * 1024]) for j in range(4)] + \
             [Buf(f"h2_{j}", regB[:, 8448 + (j - 4) * 1024:8448 + (j - 3) * 1024]) for j in range(4, 8)]
        hb = [Buf(f"h{j}", regC[:, j * 1024:(j + 1) * 1024]) for j in range(8)]
        vb = [Buf(f"v{j}", regC[:, (8 + j) * 1024:(9 + j) * 1024]) for j in range(8)]
        grnn = [Buf(f"grnn{j}", regC[:, (16 + j) * 1024:(17 + j) * 1024]) for j in range(10)]
        gfb = [Buf(f"gf{k}", regC[:, k * 1024:(k + 1) * 1024]) for k in range(22)]

        state = {"t32": 0, "t16": 0, "ps": 0, "reserved": set()}

        def t32():
            b = t32s[state["t32"] % NT32]
            state["t32"] += 1
            return b

        def t16():
            b = t16s[state["t16"] % 2]
            state["t16"] += 1
            return b

        def ps_alloc():
            while True:
                i = state["ps"] % 4
                state["ps"] += 1
                if i not in state["reserved"]:
                    return pss[i]

        def ps_reserve():
            b = ps_alloc()
            state["reserved"].add(pss.index(b))
            return b

        def ps_release(b):
            state["reserved"].discard(pss.index(b))

        def ppc(c):
            return pp.ap[:, c:c + 1]

        def dpc(c):
            return dp.ap[:, c:c + 1]

        def mm_group(ps, pairs, reads):
            n = len(pairs)

            def fn(e):
                last = None
                for i, (l, r) in enumerate(pairs):
                    for t in range(2):
                        last = e.matmul(ps.ap[:, t * 512:(t + 1) * 512], l, r[:, t * 512:(t + 1) * 512],
                                        start=(i == 0), stop=(i == n - 1))
                return last
            P.op("pe", fn, reads=reads, writes=[ps])

        def conv_group(ps, dslot, ntap, src, base, reads):
            def fn(e):
                last = None
                for t in range(2):
                    for k in range(ntap):
                        o = base + k + t * 512
                        last = e.matmul(ps.ap[:, t * 512:(t + 1) * 512], dslot.ap[:, k * 128:(k + 1) * 128],
                                        src.ap[:, o:o + 512], start=(k == 0), stop=(k == ntap - 1))
                return last
            P.op("pe", fn, reads=reads, writes=[ps])

        def stat_group(ps, src, first, last_):
            def fn(e):
                l = None
                for t in range(2):
                    l = e.matmul(ps.ap[:, t * 512:(t + 1) * 512], ones.ap[:, :], src.ap[:, t * 512:(t + 1) * 512],
                                 start=first, stop=last_)
                return l
            P.op("pe", fn, reads=[ones, src], writes=[ps])

        def act(out_b, out_ap, in_b, in_ap, func, bias=None, scale=None, extra_reads=()):
            kw = {}
            if bias is not None:
                kw["bias"] = bias
            if scale is not None:
                kw["scale"] = scale
            P.op("act", lambda e: e.activation(out_ap, in_ap, func, **kw),
                 reads=[in_b] + list(extra_reads), writes=[out_b])

        def stt(out_b, out_ap, in0_b, in0_ap, scalar, in1_b, in1_ap, op0, op1, extra_reads=()):
            P.op("dve", lambda e: e.scalar_tensor_tensor(out_ap, in0_ap, scalar, in1_ap, op0, op1),
                 reads=[in0_b, in1_b] + list(extra_reads), writes=[out_b])

        def tt(out_b, out_ap, in0_b, in0_ap, in1_b, in1_ap, op):
            P.op("dve", lambda e: e.tensor_tensor(out_ap, in0_ap, in1_ap, op),
                 reads=[in0_b, in1_b], writes=[out_b])

        def load_x(j, s):
            xb = t32()
            sem = P.bufsem(xb)
            src = xT[j * 128:(j + 1) * 128, s * NT:(s + 1) * NT]
            P.dma("sp", lambda e: e.dma_start(out=xb.ap[:, :], in_=src), sem, writes=[xb])
            return xb

        def wdma(slot, items):
            sem = P.bufsem(slot)

            def fn(e):
                res = []
                for dst, src in items:
                    res.append(e.dma_start(out=dst, in_=src))
                return res
            P.dma("pool", fn, sem, writes=[slot], n=len(items))

        def s256(slot):
            return slot.ap[:, :].rearrange("p (k n) -> p k n", n=256)

        def s128(slot):
            return slot.ap[:, :].rearrange("p (k n) -> p k n", n=128)

        def rstd_from(ps_or_buf, in_ap, scale):
            lnv = t32()
            act(lnv, lnv.ap[:, :], ps_or_buf, in_ap, AF.Ln, bias=cst.ap[:, 0:1], scale=scale, extra_reads=[cst])
            act(rstd_bc, rstd_bc.ap[:, :], lnv, lnv.ap[:, :], AF.Exp, scale=-0.5)

        P.newsem("s_par")
        P.dma("sp", lambda e: e.dma_start(out=pp.ap[:, :], in_=ppd[:, :]), "s_par", writes=[pp])
        P.newsem("s_id")
        P.dma("pool", lambda e: e.dma_start(out=ident.ap[:, :], in_=identd[:, :]), "s_id", writes=[ident])
        P.newsem("s_em")
        P.dma("pool", lambda e: e.dma_start(out=emat.ap[:, :], in_=ematd[:, :]), "s_em", writes=[emat])
        P.op("pool", lambda e: e.memset(ones.ap[:, :], 1.0), writes=[ones])
        P.op("pool", lambda e: e.memset(cst.ap[:, 0:1], EPS), writes=[cst])
        P.op("pool", lambda e: e.memset(cst.ap[:, 1:2], 1.0), writes=[cst])
        P.op("dve", lambda e: e.tensor_scalar(dp.ap[:, 0:NHALF], pp.ap[:, 0:NHALF], 0.5, None, ALU.mult),
             reads=[pp], writes=[dp])
        act(tmp10, tmp10.ap[:, 0:10], pp, pp.ap[:, P_LAM:P_LAM + 10], AF.Exp, scale=-1.0)
        act(tmp10, tmp10.ap[:, 0:10], tmp10, tmp10.ap[:, 0:10], AF.Ln, bias=cst.ap[:, 1:2], extra_reads=[cst])
        P.op("dve", lambda e: e.tensor_scalar(dp.ap[:, D_C:D_C + 10], tmp10.ap[:, 0:10], -8.0, None, ALU.mult),
             reads=[tmp10], writes=[dp])
        P.op("dve", lambda e: e.tensor_scalar(dp.ap[:, D_CH:D_CH + 10], tmp10.ap[:, 0:10], -4.0, None, ALU.mult),
             reads=[tmp10], writes=[dp])
        P.op("dve", lambda e: e.tensor_scalar(dp.ap[:, D_BQ:D_BQ + 10], pp.ap[:, P_BIN + OFF_RG // 128:P_BIN + OFF_RG // 128 + 10],
                                              0.25, None, ALU.mult), reads=[pp], writes=[dp])

        units = []
        out_deps = []

        def add(run, wl=None, dl=None):
            units.append(Unit(run, wl, dl))

        pending_out = {}
        for s in range(NS):
            def norm1_stats(s_, dest, keep=None):
                ps = ps_reserve()
                for j in range(8):
                    xb = load_x(j, s_)
                    if keep is not None:
                        keep.append(xb)
                    sq = t16()
                    act(sq, sq.ap[:, :], xb, xb.ap[:, :], AF.Square)
                    stat_group(ps, sq, j == 0, j == 7)
                lnv = t32()
                act(lnv, lnv.ap[:, :], ps, ps.ap[:, :], AF.Ln, bias=cst.ap[:, 0:1], scale=1.0 / D, extra_reads=[cst])
                act(dest, dest.ap[:, :], lnv, lnv.ap[:, :], AF.Exp, scale=-0.5)
                ps_release(ps)

            def st_norm1(ws, ds, s=s):
                if s > 0:
                    inherit(hb + vb + grnn, gfb + h2)
                    src = mu_bc
                    kept = None
                else:
                    kept = []
                    norm1_stats(s, rstd_bc, kept)
                    src = rstd_bc
                for j in range(8):
                    xb = kept[j] if kept is not None else load_x(j, s)
                    stt(hb[j], hb[j].ap, xb, xb.ap[:, :], ppc(P_G1 + j), src, src.ap[:, :],
                        ALU.mult, ALU.mult, extra_reads=[pp])
            add(st_norm1)
            if "fn" in pending_out:
                add(pending_out.pop("fn"))

            def st_front_pre(ws, ds, s=s):
                if s > 0:
                    inherit(upad + xbbf, h2 + mb)
                    inherit(ybf + zrx, x1)
            add(st_front_pre)

            def mk_glu(j, s=s):
                def wl(slot):
                    v = s256(slot)
                    wdma(slot, [(v[:, 0:8, 0:128], w_in_v[:, 0:8, OFF_A + 128 * j:OFF_A + 128 * (j + 1)]),
                                (v[:, 0:8, 128:256], w_in_v[:, 0:8, OFF_B + 128 * j:OFF_B + 128 * (j + 1)])])

                def run(slot, ds):
                    v = s256(slot)
                    psA = ps_alloc()
                    psB = ps_alloc()
                    mm_group(psA, [(v[:, i, 0:128], hb[i].ap) for i in range(8)], [slot] + hb)
                    mm_group(psB, [(v[:, i, 128:256], hb[i].ap) for i in range(8)], [slot] + hb)
                    tb = t32()
                    act(tb, tb.ap[:, :], psB, psB.ap[:, :], AF.Tanh, bias=dpc(P_BIN + OFF_B // 128 + j), scale=0.5,
                        extra_reads=[dp])
                    za = t32()
                    act(za, za.ap[:, :], psA, psA.ap[:, :], AF.Identity, bias=ppc(P_BIN + OFF_A // 128 + j),
                        extra_reads=[pp])
                    u = upad[j]
                    stt(u, u.ap[:, 32:32 + NT], tb, tb.ap[:, :], 1.0, za, za.ap[:, :], ALU.add, ALU.mult)
                    if s == 0:
                        P.op("pool", lambda e: e.memset(u.ap[:, 0:32], 0.0), writes=[u])
                        P.op("pool", lambda e: e.tensor_copy(uhalo.ap[:, j * 32:(j + 1) * 32], u.ap[:, NT:NT + 32]),
                             reads=[u], writes=[uhalo])
                    else:
                        P.op("pool", lambda e: e.tensor_copy(u.ap[:, 0:32], uhalo.ap[:, j * 32:(j + 1) * 32]),
                             reads=[uhalo], writes=[u])
                return Unit(run, wl=wl)

            cstate = {}

            def mk_conv31(j, cstate=cstate):
                def dl(dslot):
                    o = dslot.ap[:, :].rearrange("p (k n) -> p k n", n=32)
                    i0 = emat.ap[:, :].unsqueeze(1).to_broadcast([128, 32, 32])
                    i1 = dp.ap[:, P_CW31 + j * 32:P_CW31 + (j + 1) * 32].unsqueeze(2).to_broadcast([128, 32, 32])
                    P.op("pool", lambda e: e.tensor_tensor(o, i0, i1, ALU.mult), reads=[emat, dp], writes=[dslot])

                def ustack(jc):
                    u = upad[jc]
                    for g in range(4):
                        def fn(e, g=g):
                            res = []
                            for sh in range(4):
                                res.append(e.dma_start(out=Ust[g].ap[32 * sh:32 * sh + 32, 0:1032],
                                                       in_=u.ap[32 * g:32 * g + 32, 24 - 8 * sh:24 - 8 * sh + 1032]))
                            return res
                        P.dma("sp", fn, P.bufsem(Ust[g]), reads=[u], writes=[Ust[g]], n=4)

                def run(ws, dslot):
                    if j == 0:
                        ustack(0)
                    ps = ps_alloc()

                    def cfn(e):
                        last = None
                        for t in range(2):
                            for jj in range(8):
                                for g in range(4):
                                    o_ = 8 + t * 512 - jj
                                    last = e.matmul(ps.ap[32 * g:32 * g + 32, t * 512:(t + 1) * 512],
                                                    dslot.ap[:, (g * 8 + jj) * 32:(g * 8 + jj + 1) * 32],
                                                    Ust[g].ap[:, o_:o_ + 512], start=(jj == 0), stop=(jj == 7),
                                                    tile_position=(0, 32 * g))
                        return last
                    P.op("pe", cfn, reads=[dslot] + Ust, writes=[ps])
                    if j < 7:
                        ustack(j + 1)
                    P.op("dve", lambda e: e.tensor_scalar(ybf[j].ap, ps.ap[:, :], ppc(P_CDB + j), None, ALU.add),
                         reads=[ps, pp], writes=[ybf[j]])
                return Unit(run, dl=dl)

            def st_ln_stats(ws, ds, cstate=cstate):
                pm = ps_reserve()
                pq = ps_reserve()
                for j in range(8):
                    ysq = t16()
                    act(ysq, ysq.ap[:, :], ybf[j], ybf[j].ap, AF.Square)
                    stat_group(pm, ybf[j], j == 0, j == 7)
                    stat_group(pq, ysq, j == 0, j == 7)
                act(mu_bc, mu_bc.ap[:, :], pm, pm.ap[:, :], AF.Identity, scale=1.0 / D)
                msq = t32()
                tt(msq, msq.ap[:, :], mu_bc, mu_bc.ap[:, :], mu_bc, mu_bc.ap[:, :], ALU.mult)
                var = t32()
                stt(var, var.ap[:, :], pq, pq.ap[:, :], 1.0 / D, msq, msq.ap[:, :], ALU.mult, ALU.subtract)
                ps_release(pm)
                ps_release(pq)
                rstd_from(var, var.ap[:, :], 1.0)
                inherit(mb, upad + xbbf)
                inherit(tgbs, zrx)

            def mk_ln_apply(j):
                def run(ws, ds):
                    d = t32()
                    tt(d, d.ap[:, :], ybf[j], ybf[j].ap, mu_bc, mu_bc.ap[:, :], ALU.subtract)
                    tt(d, d.ap[:, :], d, d.ap[:, :], rstd_bc, rstd_bc.ap[:, :], ALU.mult)
                    n = t32()
                    act(n, n.ap[:, :], d, d.ap[:, :], AF.Identity, bias=ppc(P_LNB + j), scale=ppc(P_LNG + j),
                        extra_reads=[pp])
                    tn = t32()
                    act(tn, tn.ap[:, :], d, d.ap[:, :], AF.Tanh, bias=dpc(P_LNB + j), scale=dpc(P_LNG + j),
                        extra_reads=[dp])
                    stt(vb[j], vb[j].ap, tn, tn.ap[:, :], 1.0, n, n.ap[:, :], ALU.add, ALU.mult)
                return Unit(run)

            def mk_r1(j, s=s):
                c = j % 5

                def wl(slot):
                    v = s256(slot)
                    wdma(slot, [(v[:, 0:8, 0:128], w_in_v[:, 0:8, OFF_RX + 128 * j:OFF_RX + 128 * (j + 1)])])

                def run(slot, ds):
                    v = s256(slot)
                    ps = ps_alloc()
                    mm_group(ps, [(v[:, i, 0:128], hb[i].ap) for i in range(8)], [slot] + hb)
                    z = zrx[c]
                    P.op("dve", lambda e: e.tensor_scalar(z.ap[:, 4:4 + NT], ps.ap[:, :], ppc(P_BIN + OFF_RX // 128 + j),
                                                          None, ALU.add), reads=[ps, pp], writes=[z])
                    if s == 0:
                        P.op("pool", lambda e: e.memset(z.ap[:, 0:4], 0.0), writes=[z])
                        P.op("pool", lambda e: e.tensor_copy(zhalo.ap[:, j * 4:(j + 1) * 4], z.ap[:, NT:NT + 4]),
                             reads=[z], writes=[zhalo])
                    else:
                        P.op("pool", lambda e: e.tensor_copy(z.ap[:, 0:4], zhalo.ap[:, j * 4:(j + 1) * 4]),
                             reads=[zhalo], writes=[z])
                return Unit(run, wl=wl)

            rstate = {}

            def mk_conv4(j, s=s):
                c = j % 5

                def dl(dslot):
                    o = dslot.ap[:, 0:512].rearrange("p (k n) -> p k n", n=128)
                    i0 = ident.ap[:, :].unsqueeze(1).to_broadcast([128, 4, 128])
                    i1 = pp.ap[:, P_CW4 + j * 4:P_CW4 + (j + 1) * 4].unsqueeze(2).to_broadcast([128, 4, 128])
                    P.op("pool", lambda e: e.tensor_tensor(o, i0, i1, ALU.mult), reads=[ident, pp], writes=[dslot])

                def run(ws, dslot):
                    ps = ps_alloc()
                    conv_group(ps, dslot, 4, zrx[c], 1, [dslot, zrx[c]])
                    act(xbbf[c], xbbf[c].ap, ps, ps.ap[:, :], AF.Identity, bias=ppc(P_LCB + j), extra_reads=[pp])
                return Unit(run, dl=dl)

            def mk_gate(j, s=s, rstate=rstate):
                g0 = (j // 5) * 5
                nb = list(range(max(g0, j - 1), min(g0 + 5, j + 2)))

                def wl(slot):
                    v = s256(slot)
                    n = len(nb)
                    wdma(slot, [(v[:, 0:n, 0:128], wa_v[:, nb[0]:nb[0] + n, 128 * j:128 * (j + 1)]),
                                (v[:, 0:n, 128:256], wx_v[:, nb[0]:nb[0] + n, 128 * j:128 * (j + 1)])])

                def run(slot, ds):
                    v = s256(slot)
                    psR = ps_alloc()
                    psI = ps_alloc()
                    rd = [slot] + [xbbf[i % 5] for i in nb]
                    mm_group(psR, [(v[:, ii, 0:128], xbbf[i % 5].ap) for ii, i in enumerate(nb)], rd)
                    mm_group(psI, [(v[:, ii, 128:256], xbbf[i % 5].ap) for ii, i in enumerate(nb)], rd)
                    tr = t32()
                    act(tr, tr.ap[:, :], psR, psR.ap[:, :], AF.Tanh, bias=dpc(P_BA + j), scale=0.5, extra_reads=[dp])
                    ti = t32()
                    act(ti, ti.ap[:, :], psI, psI.ap[:, :], AF.Tanh, bias=dpc(P_BX + j), scale=0.5, extra_reads=[dp])
                    a = t32()
                    act(a, a.ap[:, :], tr, tr.ap[:, :], AF.Exp, bias=dpc(D_CH + j), scale=dpc(D_CH + j),
                        extra_reads=[dp])
                    a2 = t32()
                    act(a2, a2.ap[:, :], tr, tr.ap[:, :], AF.Exp, bias=dpc(D_C + j), scale=dpc(D_C + j),
                        extra_reads=[dp])
                    act(a2, a2.ap[:, :], a2, a2.ap[:, :], AF.Sqrt, bias=cst.ap[:, 1:2], scale=-1.0, extra_reads=[cst])
                    xbc = xbbf[j % 5]
                    stt(ti, ti.ap[:, :], ti, ti.ap[:, :], 1.0, xbc, xbc.ap, ALU.add, ALU.mult)
                    tt(ti, ti.ap[:, :], ti, ti.ap[:, :], a2, a2.ap[:, :], ALU.mult)
                    hs = tr
                    init = 0.0 if s == 0 else hstate.ap[:, j:j + 1]
                    rds = [a, ti] + ([hstate] if s > 0 else [])
                    P.op("dve", lambda e: e.tensor_tensor_scan(hs.ap[:, :], a.ap[:, :], ti.ap[:, :], init,
                                                               ALU.mult, ALU.add), reads=rds, writes=[hs])
                    if s == 0:
                        P.op("dve", lambda e: e.tensor_copy(hstate.ap[:, j:j + 1], hs.ap[:, NT - 1:NT]),
                             reads=[hs], writes=[hstate])
                    rstate[j] = hs
                return Unit(run, wl=wl)

            gstate = {}

            def mk_gelu_a(j, gstate=gstate):
                def wl(slot):
                    v = s256(slot)
                    wdma(slot, [(v[:, 0:8, 0:128], w_in_v[:, 0:8, OFF_RG + 128 * j:OFF_RG + 128 * (j + 1)])])

                def run(slot, ds):
                    v = s256(slot)
                    ps = ps_alloc()
                    mm_group(ps, [(v[:, i, 0:128], hb[i].ap) for i in range(8)], [slot] + hb)
                    c = OFF_RG // 128 + j
                    zgq = t32()
                    act(zgq, zgq.ap[:, :], ps, ps.ap[:, :], AF.Identity, bias=dpc(D_BQ + j), scale=0.25, extra_reads=[dp])
                    q = t32()
                    act(q, q.ap[:, :], ps, ps.ap[:, :], AF.Square, bias=ppc(P_BIN + c), extra_reads=[pp])
                    P.op("dve", lambda e: e.tensor_scalar(q.ap[:, :], q.ap[:, :], 0.044715, 1.0, ALU.mult, ALU.add),
                         reads=[q], writes=[q])
                    tt(q, q.ap[:, :], q, q.ap[:, :], zgq, zgq.ap[:, :], ALU.mult)
                    gstate[j] = (zgq, q)
                return Unit(run, wl=wl)

            def mk_gelu_b(j, rstate=rstate, gstate=gstate):
                def run(ws, ds):
                    zgq, q = gstate[j]
                    act(q, q.ap[:, :], q, q.ap[:, :], AF.Tanh, scale=4.0 * 0.7978845608028654)
                    stt(q, q.ap[:, :], q, q.ap[:, :], 1.0, zgq, zgq.ap[:, :], ALU.add, ALU.mult)
                    hs = rstate[j]
                    tt(grnn[j], grnn[j].ap, q, q.ap[:, :], hs, hs.ap[:, :], ALU.mult)
                return Unit(run)

            def front_seq(g0, fillers):
                out = [mk_conv4(g0)]
                per = [2, 2, 2, 1, 1]
                fi = 0
                for k in range(5):
                    j = g0 + k
                    if k < 4:
                        out.append(mk_conv4(j + 1))
                    out.append(mk_gelu_a(j))
                    if per[k] >= 1 and fi < len(fillers):
                        out.append(fillers[fi]); fi += 1
                    out.append(mk_gate(j))
                    if per[k] >= 2 and fi < len(fillers):
                        out.append(fillers[fi]); fi += 1
                    out.append(mk_gelu_b(j))
                out += fillers[fi:]
                return out

            units.extend([mk_r1(j) for j in range(5)])
            units.extend(front_seq(0, [mk_glu(j) for j in range(8)]))
            units.extend([mk_r1(j) for j in range(5, 10)])
            units.extend(front_seq(5, [mk_conv31(j) for j in range(8)]))
            add(st_ln_stats)

            def mk_g(j):
                def wlg(slot):
                    v = s256(slot)
                    wdma(slot, [(v[:, 0:8, 0:128], w_in_v[:, 0:8, OFF_GA + 128 * j:OFF_GA + 128 * (j + 1)]),
                                (v[:, 0:8, 128:256], w_in_v[:, 0:8, OFF_GB + 128 * j:OFF_GB + 128 * (j + 1)])])

                def rung(slot, ds):
                    v = s256(slot)
                    psa = ps_alloc()
                    psb = ps_alloc()
                    mm_group(psa, [(v[:, i, 0:128], hb[i].ap) for i in range(8)], [slot] + hb)
                    mm_group(psb, [(v[:, i, 128:256], hb[i].ap) for i in range(8)], [slot] + hb)
                    act(mb[j], mb[j].ap, psa, psa.ap[:, :], AF.Tanh, bias=dpc(P_BIN + OFF_GA // 128 + j), scale=0.5,
                        extra_reads=[dp])
                    act(tgbs[j], tgbs[j].ap, psb, psb.ap[:, :], AF.Tanh, bias=dpc(P_BIN + OFF_GB // 128 + j),
                        scale=0.5, extra_reads=[dp])
                return Unit(rung, wl=wlg)

            def mk_y(j):
                def wly(slot):
                    v = s256(slot)
                    wdma(slot, [(v[:, 0:8, 0:128], w_co_v[:, 0:8, 128 * j:128 * (j + 1)]),
                                (v[:, 0:10, 128:256], w_lo_v[:, 0:10, 128 * j:128 * (j + 1)])])

                def runy(slot, ds):
                    v = s256(slot)
                    psa = ps_alloc()
                    psb = ps_alloc()
                    mm_group(psa, [(v[:, i, 0:128], vb[i].ap) for i in range(8)], [slot] + vb)
                    mm_group(psb, [(v[:, i, 128:256], grnn[i].ap) for i in range(10)], [slot] + grnn)
                    ya = t32()
                    act(ya, ya.ap[:, :], psa, psa.ap[:, :], AF.Identity, bias=ppc(P_CBO + j), scale=0.5, extra_reads=[pp])
                    stt(ya, ya.ap[:, :], mb[j], mb[j].ap, 1.0, ya, ya.ap[:, :], ALU.add, ALU.mult)
                    t2 = t32()
                    stt(t2, t2.ap[:, :], tgbs[j], tgbs[j].ap, 1.0, psb, psb.ap[:, :], ALU.add, ALU.mult)
                    tt(mb[j], mb[j].ap, ya, ya.ap[:, :], t2, t2.ap[:, :], ALU.add)
                return Unit(runy, wl=wly)

            for j in range(8):
                units.append(mk_ln_apply(j))
                units.append(mk_g(j))
            for j in range(8):
                units.append(mk_y(j))

            xstate = {}

            def st_mix_pre(ws, ds, xstate=xstate):
                inherit(x1, zrx + ybf + tgbs)
                inherit(h2, grnn + xbbf + gfb)
                xstate["ps"] = ps_reserve()
            add(st_mix_pre)
            for j in range(8):
                def wl(slot, j=j):
                    v = s256(slot)
                    wdma(slot, [(v[:, 0:8, 0:128], w_mx_v[:, 0:8, 128 * j:128 * (j + 1)])])

                def run(slot, ds, j=j, s=s, xstate=xstate):
                    v = s256(slot)
                    ps = ps_alloc()
                    mm_group(ps, [(v[:, i, 0:128], mb[i].ap) for i in range(8)], [slot] + mb)
                    if j > 0:
                        stat_group(xstate["ps"], xstate["sq"], j - 1 == 0, False)
                    xb = load_x(j, s)
                    stt(x1[j], x1[j].ap, ps, ps.ap[:, :], 0.5, xb, xb.ap[:, :], ALU.mult, ALU.add)
                    sq = t16()
                    act(sq, sq.ap[:, :], x1[j], x1[j].ap, AF.Square)
                    xstate["sq"] = sq
                    P.op("dve", lambda e: e.tensor_scalar(h2[j].ap, x1[j].ap, ppc(P_G2 + j), None, ALU.mult),
                         reads=[x1[j], pp], writes=[h2[j]])
                add(run, wl=wl)

            def st_norm2(ws, ds, xstate=xstate):
                ps = xstate["ps"]
                stat_group(ps, xstate["sq"], False, True)
                rstd_from(ps, ps.ap[:, :], 1.0 / D)
                ps_release(ps)
                inherit(gfb, hb + vb + grnn)
            add(st_norm2)

            for k in range(22):
                def wl(slot, k=k):
                    v = s256(slot)
                    wdma(slot, [(v[:, 0:8, 0:128], w_f1_v[:, 0:8, 128 * k:128 * (k + 1)]),
                                (v[:, 0:8, 128:256], w_f3_v[:, 0:8, 128 * k:128 * (k + 1)])])

                def run(slot, ds, k=k):
                    v = s256(slot)
                    p1 = ps_alloc()
                    p3 = ps_alloc()
                    mm_group(p1, [(v[:, i, 0:128], h2[i].ap) for i in range(8)], [slot] + h2)
                    mm_group(p3, [(v[:, i, 128:256], h2[i].ap) for i in range(8)], [slot] + h2)
                    s1 = t32()
                    tt(s1, s1.ap[:, :], p1, p1.ap[:, :], rstd_bc, rstd_bc.ap[:, :], ALU.mult)
                    act(s1, s1.ap[:, :], s1, s1.ap[:, :], AF.Silu)
                    t3 = t32()
                    tt(t3, t3.ap[:, :], p3, p3.ap[:, :], rstd_bc, rstd_bc.ap[:, :], ALU.mult)
                    tt(gfb[k], gfb[k].ap, s1, s1.ap[:, :], t3, t3.ap[:, :], ALU.mult)
                add(run, wl=wl)

            fstate = {}

            def st_dn_pre(ws, ds, fstate=fstate, s=s):
                if s + 1 < NS:
                    norm1_stats(s + 1, mu_bc)
                fstate["ps"] = ps_reserve()
            add(st_dn_pre)
            for j in range(8):
                def wl(slot, j=j):
                    v = s128(slot)
                    wdma(slot, [(v[:, 0:22, 0:128], w_f2_v[:, 0:22, 128 * j:128 * (j + 1)])])

                def run(slot, ds, j=j, fstate=fstate):
                    v = s128(slot)
                    ps = ps_alloc()
                    mm_group(ps, [(v[:, k, 0:128], gfb[k].ap) for k in range(22)], [slot] + gfb)
                    if j > 0:
                        stat_group(fstate["ps"], fstate["sq"], j - 1 == 0, False)
                    tt(x1[j], x1[j].ap, ps, ps.ap[:, :], x1[j], x1[j].ap, ALU.add)
                    sq = t16()
                    act(sq, sq.ap[:, :], x1[j], x1[j].ap, AF.Square)
                    fstate["sq"] = sq
                add(run, wl=wl)

            def st_out(ws, ds, s=s, fstate=fstate):
                ps = fstate["ps"]
                stat_group(ps, fstate["sq"], False, True)
                rstd_from(ps, ps.ap[:, :], 1.0 / D)
                ps_release(ps)
                for j in range(8):
                    o = t32()
                    stt(o, o.ap[:, :], x1[j], x1[j].ap, ppc(P_GF + j), rstd_bc, rstd_bc.ap[:, :],
                        ALU.mult, ALU.mult, extra_reads=[pp])
                    sem = P.bufsem(o)
                    dst = outT[j * 128:(j + 1) * 128, s * NT:(s + 1) * NT]
                    out_deps.append(P.dma("sp", lambda e, o=o, dst=dst: e.dma_start(out=dst, in_=o.ap[:, :]),
                                          sem, reads=[o]))
            pending_out["fn"] = st_out

        add(pending_out.pop("fn"))

        wl_idx = [i for i, u in enumerate(units) if u.wl is not None]
        dl_idx = [i for i, u in enumerate(units) if u.dl is not None]
        wslot_of, dslot_of = {}, {}
        wp = dp_ = 0
        wdone = ddone = 0
        if max_units is not None:
            units = units[:max_units]
            wl_idx = [i for i in wl_idx if i < max_units]
            dl_idx = [i for i in dl_idx if i < max_units]
        for idx, u in enumerate(units):
            while wp < len(wl_idx) and (wp - wdone) < NW and wl_idx[wp] <= idx + 6:
                ui = wl_idx[wp]
                slot = wslots[wp % NW]
                wslot_of[ui] = slot
                units[ui].wl(slot)
                wp += 1
            while dp_ < len(dl_idx) and (dp_ - ddone) < ND and dl_idx[dp_] <= idx + 4:
                ui = dl_idx[dp_]
                slot = dslots[dp_ % ND]
                dslot_of[ui] = slot
                units[ui].dl(slot)
                dp_ += 1
            u.run(wslot_of.get(idx), dslot_of.get(idx))
            if u.wl is not None:
                wdone += 1
            if u.dl is not None:
                ddone += 1

        P.wait_all("sp", out_deps)
        P.emit()
    return nc


def _pack_params(inp):
    def cols(v):
        v = np.asarray(v, np.float32).reshape(-1)
        return v.reshape(-1, 128).T
    cw31 = np.asarray(inp["conv_dw_w"][0], np.float32)
    cw31c = np.zeros((128, 256), np.float32)
    for jc in range(8):
        for g in range(4):
            for jj in range(8):
                for sft in range(4):
                    d = 8 * sft + jj
                    if d <= 30:
                        ch0 = jc * 128 + 32 * g
                        cw31c[32 * sft:32 * sft + 32, jc * 32 + g * 8 + jj] = cw31[30 - d, ch0:ch0 + 32]
    cw4 = np.asarray(inp["lru_conv_w"][0], np.float32)
    cw4c = np.concatenate([cw4[:, j * 128:(j + 1) * 128].T for j in range(10)], axis=1)
    parts = [cols(inp["b_in"][0]), cols(inp["lru_conv_b"][0]), cols(inp["lru_ba"][0]), cols(inp["lru_bx"][0]),
             cols(inp["conv_ln_g"][0]), cols(inp["conv_ln_b"][0]), cw31c,
             cols(inp["norm1_g"][0]), cols(inp["conv_dw_b"][0]), cols(inp["conv_b_out"][0]),
             cols(inp["lru_lambda"][0]), cols(inp["norm2_g"][0]), cols(inp["norm_f_g"]), cw4c]
    pp = np.ascontiguousarray(np.concatenate(parts, axis=1), dtype=np.float32)
    assert pp.shape == (128, NPP), pp.shape
    return pp


def _blockdiag(w):
    w = np.asarray(w, np.float32)
    out = np.zeros((DR, DR), np.float32)
    for h in range(16):
        out[h * 80:(h + 1) * 80, h * 80:(h + 1) * 80] = w[h]
    return out


_NC_CACHE = {}


def kernel(**inp):
    x = np.asarray(inp["x"], np.float32)
    nb = x.shape[0]
    if "nc" not in _NC_CACHE:
        _NC_CACHE["nc"] = build_program()
    nc = _NC_CACHE["nc"]
    shared = {
        "w_in": np.ascontiguousarray(inp["w_in"][0], dtype=np.float32),
        "w_co": np.ascontiguousarray(inp["conv_w_out"][0], dtype=np.float32),
        "w_lo": np.ascontiguousarray(inp["lru_w_out"][0], dtype=np.float32),
        "w_mx": np.ascontiguousarray(inp["w_mix_out"][0], dtype=np.float32),
        "w_f1": np.ascontiguousarray(inp["ffn_w1"][0], dtype=np.float32),
        "w_f3": np.ascontiguousarray(inp["ffn_w3"][0], dtype=np.float32),
        "w_f2": np.ascontiguousarray(inp["ffn_w2"][0], dtype=np.float32),
        "wa_bd": _blockdiag(inp["lru_wa"][0]),
        "wx_bd": _blockdiag(inp["lru_wx"][0]),
        "pp": _pack_params(inp),
        "ident": np.eye(128, dtype=np.float32),
        "emat": np.ascontiguousarray(np.tile(np.eye(32, dtype=np.float32), (4, 1))),
    }
    in_maps = []
    for b in range(nb):
        m = dict(shared)
        m["xT"] = np.ascontiguousarray(x[b].T)
        in_maps.append(m)
    res = run_bass_kernel_spmd(nc, in_maps, core_ids=list(range(nb)))
    out = np.stack([np.ascontiguousarray(r["outT"].T) for r in res.results], axis=0)
    return out.astype(np.float32)
```

```python
import numpy as np
from contextlib import ExitStack
import concourse.bass as bass
import concourse.mybir as mybir
from concourse.bass_utils import run_bass_kernel_spmd

F32 = mybir.dt.float32
BF16 = mybir.dt.bfloat16
AF = mybir.ActivationFunctionType
ALU = mybir.AluOpType

D = 1024
S = 2048
NT = 1024
NS = S // NT
DR = 1280
DFF = 2816
DIN = 6656
OFF_A, OFF_B, OFF_RX, OFF_RG, OFF_GA, OFF_GB = 0, 1024, 2048, 3328, 4608, 5632
EPS = 1e-6

P_BIN, P_LCB, P_BA, P_BX, P_LNG, P_LNB, P_CW31 = 0, 52, 62, 72, 82, 90, 98
NHALF = 354
P_G1, P_CDB, P_CBO, P_LAM, P_G2, P_GF, P_CW4 = 354, 362, 370, 378, 388, 396, 404
NPP = 444
D_C, D_CH = 354, 364
D_BQ = 374
NDP = 384


class Buf:
    def __init__(self, name, ap):
        self.name = name
        self.ap = ap
        self.w = []
        self.r = []
        self.sem = None


class Prog:
    ENGS = ("pe", "act", "dve", "pool", "sp")

    def __init__(self, nc, es):
        self.nc = nc
        self.es = es
        self.sems = {}
        self.semval = {}
        self.ops = {e: [] for e in self.ENGS}
        self.waited = {e: {} for e in self.ENGS}
        for e in self.ENGS:
            self.newsem("c_" + e)

    def newsem(self, key):
        self.sems[key] = self.es.enter_context(self.nc.semaphore(key))
        self.semval[key] = 0
        return key

    def bufsem(self, b):
        if b.sem is None:
            b.sem = self.newsem("s_" + b.name)
        return b.sem

    def _waits(self, eng, reads, writes):
        own = "c_" + eng
        need = {}

        def add(d, same_ok):
            k, v = d
            if k == own and (eng == "pe" or not same_ok):
                return
            if v > need.get(k, 0):
                need[k] = v
        for b in reads:
            for d in b.w:
                add(d, True)
        for b in writes:
            for d in b.w:
                add(d, True)
            for d in b.r:
                add(d, True)
        out = []
        wd = self.waited[eng]
        for k, v in need.items():
            if wd.get(k, 0) >= v:
                continue
            wd[k] = v
            out.append((k, v))
        return out

    def _mark(self, dep, reads, writes):
        for b in reads:
            b.r.append(dep)
        for b in writes:
            b.w = [dep]
            b.r = []

    def op(self, eng, fn, reads=(), writes=()):
        waits = self._waits(eng, reads, writes)
        key = "c_" + eng
        self.semval[key] += 1
        dep = (key, self.semval[key])
        self.ops[eng].append((waits, fn, (key, 1)))
        self._mark(dep, reads, writes)
        return dep

    def dma(self, eng, fn, semkey, reads=(), writes=(), n=1):
        waits = self._waits(eng, reads, writes)
        self.semval[semkey] += 16 * n
        dep = (semkey, self.semval[semkey])
        self.ops[eng].append((waits, fn, (semkey, 16)))
        self._mark(dep, reads, writes)
        return dep

    def wait_all(self, eng, deps):
        waits = []
        wd = self.waited[eng]
        for k, v in deps:
            if wd.get(k, 0) < v:
                wd[k] = v
                waits.append((k, v))
        self.ops[eng].append((waits, None, None))

    def emit(self):
        nc = self.nc
        with nc.Block() as block:
            def run(name):
                def f(e):
                    for waits, fn, incs in self.ops[name]:
                        for k, v in waits:
                            e.wait_ge(self.sems[k], v)
                        if fn is None:
                            continue
                        ins = fn(e)
                        if incs is not None:
                            if isinstance(ins, (list, tuple)):
                                for i_ in ins:
                                    i_.then_inc(self.sems[incs[0]], incs[1])
                            else:
                                ins.then_inc(self.sems[incs[0]], incs[1])
                return f
            block.tensor(run("pe"))
            block.scalar(run("act"))
            block.vector(run("dve"))
            block.gpsimd(run("pool"))
            block.sync(run("sp"))


def inherit(news, olds):
    deps = []
    for b in olds:
        deps.extend(b.w)
        deps.extend(b.r)
    for n in news:
        n.r = list(n.r) + deps


class Unit:
    def __init__(self, run, wl=None, dl=None):
        self.run = run
        self.wl = wl
        self.dl = dl


def build_program(max_units=None):
    nc = bass.Bass("TRN2", target_bir_lowering=False)
    es = ExitStack()

    def din(name, shape):
        return nc.dram_tensor(name, shape, F32, kind="ExternalInput").ap()
    xT = din("xT", [D, S])
    w_in = din("w_in", [D, DIN])
    w_co = din("w_co", [D, D])
    w_lo = din("w_lo", [DR, D])
    w_mx = din("w_mx", [D, D])
    w_f1 = din("w_f1", [D, DFF])
    w_f3 = din("w_f3", [D, DFF])
    w_f2 = din("w_f2", [DFF, D])
    wa_bd = din("wa_bd", [DR, DR])
    wx_bd = din("wx_bd", [DR, DR])
    ppd = din("pp", [128, NPP])
    identd = din("ident", [128, 128])
    ematd = din("emat", [128, 32])
    outT = nc.dram_tensor("outT", [D, S], F32, kind="ExternalOutput").ap()

    def wv(w):
        return w.rearrange("(k p) n -> p k n", p=128)
    w_in_v, w_co_v, w_lo_v, w_mx_v = wv(w_in), wv(w_co), wv(w_lo), wv(w_mx)
    w_f1_v, w_f3_v, w_f2_v, wa_v, wx_v = wv(w_f1), wv(w_f3), wv(w_f2), wv(wa_bd), wv(wx_bd)

    with es:
        P = Prog(nc, es)

        def sb(name, shape, dt):
            return es.enter_context(nc.sbuf_tensor(name, shape, dt))

        pp = Buf("pp", sb("pp_sb", [128, NPP], F32))
        dp = Buf("dp", sb("dp_sb", [128, NDP], F32))
        cst = Buf("cst", sb("cst_sb", [128, 4], F32))
        tmp10 = Buf("tmp10", sb("tmp10", [128, 16], F32))
        hstate = Buf("hstate", sb("hstate", [128, 16], F32))
        ident = Buf("ident", sb("ident_sb", [128, 128], BF16))
        ones = Buf("ones", sb("ones_sb", [128, 128], BF16))
        emat = Buf("emat", sb("emat_sb", [128, 32], BF16))
        Ust = [Buf(f"Ust{g}", sb(f"Ust{g}", [128, 1032], BF16)) for g in range(4)]
        uhalo = Buf("uhalo", sb("uhalo", [128, 8 * 32], BF16))
        zhalo = Buf("zhalo", sb("zhalo", [128, 10 * 4], BF16))
        regA = sb("regA", [128, 16384], BF16)
        regB = sb("regB", [128, 13824], BF16)
        regC = sb("regC", [128, 26 * 1024], BF16)
        NT32 = 11
        t32s = [Buf(f"t32_{i}", sb(f"t32_{i}", [128, NT], F32)) for i in range(NT32)]
        t16s = [Buf(f"t16_{i}", sb(f"t16_{i}", [128, NT], BF16)) for i in range(2)]
        rstd_bc = Buf("rstd_bc", sb("rstd_bc", [128, NT], F32))
        mu_bc = Buf("mu_bc", sb("mu_bc", [128, NT], F32))
        NW = 3
        wslots = [Buf(f"ws{i}", sb(f"ws{i}", [128, 2816], BF16)) for i in range(NW)]
        ND = 2
        dslots = [Buf(f"ds{i}", sb(f"ds{i}", [128, 1024], BF16)) for i in range(ND)]
        pss = [Buf(f"ps{i}", es.enter_context(nc.psum_tensor(f"ps{i}", [128, NT], F32))) for i in range(4)]

        ybf = [Buf(f"ybf{j}", regA[:, j * 1024:(j + 1) * 1024]) for j in range(8)]
        zrx = [Buf(f"zrx{j}", regA[:, 8192 + j * 1028:8192 + (j + 1) * 1028]) for j in range(5)]
        x1 = [Buf(f"x1_{j}", regA[:, j * 2048:(j + 1) * 2048].bitcast(F32)) for j in range(8)]
        upad = [Buf(f"upad{j}", regB[:, j * 1056:(j + 1) * 1056]) for j in range(8)]
        tgbs = [Buf(f"tgbs{j}", regA[:, 8192 + j * 1024:8192 + (j + 1) * 1024]) for j in range(8)]
        xbbf = [Buf(f"xbbf{j}", regB[:, 8448 + j * 1024:8448 + (j + 1) * 1024]) for j in range(5)]
        mb = [Buf(f"m{j}", regB[:, j * 1024:(j + 1) * 1024]) for j in range(8)]
        h2 = [Buf(f"h2_{j}", regC[:, 22528 + j * 1024:22528 + (j + 1) * 1024]) for j in range(4)] + \
             [Buf(f"h2_{j}", regB[:, 8448 + (j - 4) * 1024:8448 + (j - 3) * 1024]) for j in range(4, 8)]
        hb = [Buf(f"h{j}", regC[:, j * 1024:(j + 1) * 1024]) for j in range(8)]
        vb = [Buf(f"v{j}", regC[:, (8 + j) * 1024:(9 + j) * 1024]) for j in range(8)]
        grnn = [Buf(f"grnn{j}", regC[:, (16 + j) * 1024:(17 + j) * 1024]) for j in range(10)]
        gfb = [Buf(f"gf{k}", regC[:, k * 1024:(k + 1) * 1024]) for k in range(22)]

        state = {"t32": 0, "t16": 0, "ps": 0, "reserved": set()}

        def t32():
            b = t32s[state["t32"] % NT32]
            state["t32"] += 1
            return b

        def t16():
            b = t16s[state["t16"] % 2]
            state["t16"] += 1
            return b

        def ps_alloc():
            while True:
                i = state["ps"] % 4
                state["ps"] += 1
                if i not in state["reserved"]:
                    return pss[i]

        def ps_reserve():
            b = ps_alloc()
            state["reserved"].add(pss.index(b))
            return b

        def ps_release(b):
            state["reserved"].discard(pss.index(b))

        def ppc(c):
            return pp.ap[:, c:c + 1]

        def dpc(c):
            return dp.ap[:, c:c + 1]

        def mm_group(ps, pairs, reads):
            n = len(pairs)

            def fn(e):
                last = None
                for i, (l, r) in enumerate(pairs):
                    for t in range(2):
                        last = e.matmul(ps.ap[:, t * 512:(t + 1) * 512], l, r[:, t * 512:(t + 1) * 512],
                                        start=(i == 0), stop=(i == n - 1))
                return last
            P.op("pe", fn, reads=reads, writes=[ps])

        def conv_group(ps, dslot, ntap, src, base, reads):
            def fn(e):
                last = None
                for t in range(2):
                    for k in range(ntap):
                        o = base + k + t * 512
                        last = e.matmul(ps.ap[:, t * 512:(t + 1) * 512], dslot.ap[:, k * 128:(k + 1) * 128],
                                        src.ap[:, o:o + 512], start=(k == 0), stop=(k == ntap - 1))
                return last
            P.op("pe", fn, reads=reads, writes=[ps])

        def stat_group(ps, src, first, last_):
            def fn(e):
                l = None
                for t in range(2):
                    l = e.matmul(ps.ap[:, t * 512:(t + 1) * 512], ones.ap[:, :], src.ap[:, t * 512:(t + 1) * 512],
                                 start=first, stop=last_)
                return l
            P.op("pe", fn, reads=[ones, src], writes=[ps])

        def act(out_b, out_ap, in_b, in_ap, func, bias=None, scale=None, extra_reads=()):
            kw = {}
            if bias is not None:
                kw["bias"] = bias
            if scale is not None:
                kw["scale"] = scale
            P.op("act", lambda e: e.activation(out_ap, in_ap, func, **kw),
                 reads=[in_b] + list(extra_reads), writes=[out_b])

        def stt(out_b, out_ap, in0_b, in0_ap, scalar, in1_b, in1_ap, op0, op1, extra_reads=()):
            P.op("dve", lambda e: e.scalar_tensor_tensor(out_ap, in0_ap, scalar, in1_ap, op0, op1),
                 reads=[in0_b, in1_b] + list(extra_reads), writes=[out_b])

        def tt(out_b, out_ap, in0_b, in0_ap, in1_b, in1_ap, op):
            P.op("dve", lambda e: e.tensor_tensor(out_ap, in0_ap, in1_ap, op),
                 reads=[in0_b, in1_b], writes=[out_b])

        def load_x(j, s):
            xb = t32()
            sem = P.bufsem(xb)
            src = xT[j * 128:(j + 1) * 128, s * NT:(s + 1) * NT]
            P.dma("sp", lambda e: e.dma_start(out=xb.ap[:, :], in_=src), sem, writes=[xb])
            return xb

        def wdma(slot, items):
            sem = P.bufsem(slot)

            def fn(e):
                res = []
                for dst, src in items:
                    res.append(e.dma_start(out=dst, in_=src))
                return res
            P.dma("pool", fn, sem, writes=[slot], n=len(items))

        def s256(slot):
            return slot.ap[:, :].rearrange("p (k n) -> p k n", n=256)

        def s128(slot):
            return slot.ap[:, :].rearrange("p (k n) -> p k n", n=128)

        def rstd_from(ps_or_buf, in_ap, scale):
            lnv = t32()
            act(lnv, lnv.ap[:, :], ps_or_buf, in_ap, AF.Ln, bias=cst.ap[:, 0:1], scale=scale, extra_reads=[cst])
            act(rstd_bc, rstd_bc.ap[:, :], lnv, lnv.ap[:, :], AF.Exp, scale=-0.5)

        P.newsem("s_par")
        P.dma("sp", lambda e: e.dma_start(out=pp.ap[:, :], in_=ppd[:, :]), "s_par", writes=[pp])
        P.newsem("s_id")
        P.dma("pool", lambda e: e.dma_start(out=ident.ap[:, :], in_=identd[:, :]), "s_id", writes=[ident])
        P.newsem("s_em")
        P.dma("pool", lambda e: e.dma_start(out=emat.ap[:, :], in_=ematd[:, :]), "s_em", writes=[emat])
        P.op("pool", lambda e: e.memset(ones.ap[:, :], 1.0), writes=[ones])
        P.op("pool", lambda e: e.memset(cst.ap[:, 0:1], EPS), writes=[cst])
        P.op("pool", lambda e: e.memset(cst.ap[:, 1:2], 1.0), writes=[cst])
        P.op("dve", lambda e: e.tensor_scalar(dp.ap[:, 0:NHALF], pp.ap[:, 0:NHALF], 0.5, None, ALU.mult),
             reads=[pp], writes=[dp])
        act(tmp10, tmp10.ap[:, 0:10], pp, pp.ap[:, P_LAM:P_LAM + 10], AF.Exp, scale=-1.0)
        act(tmp10, tmp10.ap[:, 0:10], tmp10, tmp10.ap[:, 0:10], AF.Ln, bias=cst.ap[:, 1:2], extra_reads=[cst])
        P.op("dve", lambda e: e.tensor_scalar(dp.ap[:, D_C:D_C + 10], tmp10.ap[:, 0:10], -8.0, None, ALU.mult),
             reads=[tmp10], writes=[dp])
        P.op("dve", lambda e: e.tensor_scalar(dp.ap[:, D_CH:D_CH + 10], tmp10.ap[:, 0:10], -4.0, None, ALU.mult),
             reads=[tmp10], writes=[dp])
        P.op("dve", lambda e: e.tensor_scalar(dp.ap[:, D_BQ:D_BQ + 10], pp.ap[:, P_BIN + OFF_RG // 128:P_BIN + OFF_RG // 128 + 10],
                                              0.25, None, ALU.mult), reads=[pp], writes=[dp])

        units = []
        out_deps = []

        def add(run, wl=None, dl=None):
            units.append(Unit(run, wl, dl))

        pending_out = {}
        for s in range(NS):
            def norm1_stats(s_, dest, keep=None):
                ps = ps_reserve()
                for j in range(8):
                    xb = load_x(j, s_)
                    if keep is not None:
                        keep.append(xb)
                    sq = t16()
                    act(sq, sq.ap[:, :], xb, xb.ap[:, :], AF.Square)
                    stat_group(ps, sq, j == 0, j == 7)
                lnv = t32()
                act(lnv, lnv.ap[:, :], ps, ps.ap[:, :], AF.Ln, bias=cst.ap[:, 0:1], scale=1.0 / D, extra_reads=[cst])
                act(dest, dest.ap[:, :], lnv, lnv.ap[:, :], AF.Exp, scale=-0.5)
                ps_release(ps)

            def st_norm1(ws, ds, s=s):
                if s > 0:
                    inherit(hb + vb + grnn, gfb + h2)
                    src = mu_bc
                    kept = None
                else:
                    kept = []
                    norm1_stats(s, rstd_bc, kept)
                    src = rstd_bc
                for j in range(8):
                    xb = kept[j] if kept is not None else load_x(j, s)
                    stt(hb[j], hb[j].ap, xb, xb.ap[:, :], ppc(P_G1 + j), src, src.ap[:, :],
                        ALU.mult, ALU.mult, extra_reads=[pp])
            add(st_norm1)
            if "fn" in pending_out:
                add(pending_out.pop("fn"))

            def st_front_pre(ws, ds, s=s):
                if s > 0:
                    inherit(upad + xbbf, h2 + mb)
                    inherit(ybf + zrx, x1)
            add(st_front_pre)

            def mk_glu(j, s=s):
                def wl(slot):
                    v = s256(slot)
                    wdma(slot, [(v[:, 0:8, 0:128], w_in_v[:, 0:8, OFF_A + 128 * j:OFF_A + 128 * (j + 1)]),
                                (v[:, 0:8, 128:256], w_in_v[:, 0:8, OFF_B + 128 * j:OFF_B + 128 * (j + 1)])])

                def run(slot, ds):
                    v = s256(slot)
                    psA = ps_alloc()
                    psB = ps_alloc()
                    mm_group(psA, [(v[:, i, 0:128], hb[i].ap) for i in range(8)], [slot] + hb)
                    mm_group(psB, [(v[:, i, 128:256], hb[i].ap) for i in range(8)], [slot] + hb)
                    tb = t32()
                    act(tb, tb.ap[:, :], psB, psB.ap[:, :], AF.Tanh, bias=dpc(P_BIN + OFF_B // 128 + j), scale=0.5,
                        extra_reads=[dp])
                    za = t32()
                    act(za, za.ap[:, :], psA, psA.ap[:, :], AF.Identity, bias=ppc(P_BIN + OFF_A // 128 + j),
                        extra_reads=[pp])
                    u = upad[j]
                    stt(u, u.ap[:, 32:32 + NT], tb, tb.ap[:, :], 1.0, za, za.ap[:, :], ALU.add, ALU.mult)
                    if s == 0:
                        P.op("pool", lambda e: e.memset(u.ap[:, 0:32], 0.0), writes=[u])
                        P.op("pool", lambda e: e.tensor_copy(uhalo.ap[:, j * 32:(j + 1) * 32], u.ap[:, NT:NT + 32]),
                             reads=[u], writes=[uhalo])
                    else:
                        P.op("pool", lambda e: e.tensor_copy(u.ap[:, 0:32], uhalo.ap[:, j * 32:(j + 1) * 32]),
                             reads=[uhalo], writes=[u])
                return Unit(run, wl=wl)

            cstate = {}

            def ustack(jc):
                u = upad[jc]
                for g in range(4):
                    def fn(e, g=g):
                        res = []
                        for sh in range(4):
                            res.append(e.dma_start(out=Ust[g].ap[32 * sh:32 * sh + 32, 0:1032],
                                                   in_=u.ap[32 * g:32 * g + 32, 24 - 8 * sh:24 - 8 * sh + 1032]))
                        return res
                    P.dma("sp", fn, P.bufsem(Ust[g]), reads=[u], writes=[Ust[g]], n=4)

            def mk_conv31(j, cstate=cstate):
                def dl(dslot):
                    o = dslot.ap[:, :].rearrange("p (k n) -> p k n", n=32)
                    i0 = emat.ap[:, :].unsqueeze(1).to_broadcast([128, 32, 32])
                    i1 = dp.ap[:, P_CW31 + j * 32:P_CW31 + (j + 1) * 32].unsqueeze(2).to_broadcast([128, 32, 32])
                    P.op("pool", lambda e: e.tensor_tensor(o, i0, i1, ALU.mult), reads=[emat, dp], writes=[dslot])

                def run(ws, dslot):
                    ps = ps_alloc()

                    def cfn(e):
                        last = None
                        for t in range(2):
                            for jj in range(8):
                                for g in range(4):
                                    o_ = 8 + t * 512 - jj
                                    last = e.matmul(ps.ap[32 * g:32 * g + 32, t * 512:(t + 1) * 512],
                                                    dslot.ap[:, (g * 8 + jj) * 32:(g * 8 + jj + 1) * 32],
                                                    Ust[g].ap[:, o_:o_ + 512], start=(jj == 0), stop=(jj == 7),
                                                    tile_position=(0, 32 * g))
                        return last
                    P.op("pe", cfn, reads=[dslot] + Ust, writes=[ps])
                    if j < 7:
                        ustack(j + 1)
                    P.op("dve", lambda e: e.tensor_scalar(ybf[j].ap, ps.ap[:, :], ppc(P_CDB + j), None, ALU.add),
                         reads=[ps, pp], writes=[ybf[j]])
                return Unit(run, dl=dl)

            def st_ln_stats(ws, ds, cstate=cstate):
                pm = ps_reserve()
                pq = ps_reserve()
                for j in range(8):
                    ysq = t16()
                    act(ysq, ysq.ap[:, :], ybf[j], ybf[j].ap, AF.Square)
                    stat_group(pm, ybf[j], j == 0, j == 7)
                    stat_group(pq, ysq, j == 0, j == 7)
                act(mu_bc, mu_bc.ap[:, :], pm, pm.ap[:, :], AF.Identity, scale=1.0 / D)
                msq = t32()
                tt(msq, msq.ap[:, :], mu_bc, mu_bc.ap[:, :], mu_bc, mu_bc.ap[:, :], ALU.mult)
                var = t32()
                stt(var, var.ap[:, :], pq, pq.ap[:, :], 1.0 / D, msq, msq.ap[:, :], ALU.mult, ALU.subtract)
                ps_release(pm)
                ps_release(pq)
                rstd_from(var, var.ap[:, :], 1.0)
                inherit(mb, upad + xbbf)
                inherit(tgbs, zrx)

            def mk_ln_apply(j):
                def run(ws, ds):
                    d = t32()
                    tt(d, d.ap[:, :], ybf[j], ybf[j].ap, mu_bc, mu_bc.ap[:, :], ALU.subtract)
                    tt(d, d.ap[:, :], d, d.ap[:, :], rstd_bc, rstd_bc.ap[:, :], ALU.mult)
                    n = t32()
                    act(n, n.ap[:, :], d, d.ap[:, :], AF.Identity, bias=ppc(P_LNB + j), scale=ppc(P_LNG + j),
                        extra_reads=[pp])
                    tn = t32()
                    act(tn, tn.ap[:, :], d, d.ap[:, :], AF.Tanh, bias=dpc(P_LNB + j), scale=dpc(P_LNG + j),
                        extra_reads=[dp])
                    stt(vb[j], vb[j].ap, tn, tn.ap[:, :], 1.0, n, n.ap[:, :], ALU.add, ALU.mult)
                return Unit(run)

            def mk_r1(j, s=s):
                c = j % 5

                def wl(slot):
                    v = s256(slot)
                    wdma(slot, [(v[:, 0:8, 0:128], w_in_v[:, 0:8, OFF_RX + 128 * j:OFF_RX + 128 * (j + 1)])])

                def run(slot, ds):
                    v = s256(slot)
                    ps = ps_alloc()
                    mm_group(ps, [(v[:, i, 0:128], hb[i].ap) for i in range(8)], [slot] + hb)
                    z = zrx[c]
                    P.op("dve", lambda e: e.tensor_scalar(z.ap[:, 4:4 + NT], ps.ap[:, :], ppc(P_BIN + OFF_RX // 128 + j),
                                                          None, ALU.add), reads=[ps, pp], writes=[z])
                    if s == 0:
                        P.op("pool", lambda e: e.memset(z.ap[:, 0:4], 0.0), writes=[z])
                        P.op("pool", lambda e: e.tensor_copy(zhalo.ap[:, j * 4:(j + 1) * 4], z.ap[:, NT:NT + 4]),
                             reads=[z], writes=[zhalo])
                    else:
                        P.op("pool", lambda e: e.tensor_copy(z.ap[:, 0:4], zhalo.ap[:, j * 4:(j + 1) * 4]),
                             reads=[zhalo], writes=[z])
                return Unit(run, wl=wl)

            rstate = {}

            def mk_conv4(j, s=s):
                c = j % 5

                def dl(dslot):
                    o = dslot.ap[:, 0:512].rearrange("p (k n) -> p k n", n=128)
                    i0 = ident.ap[:, :].unsqueeze(1).to_broadcast([128, 4, 128])
                    i1 = pp.ap[:, P_CW4 + j * 4:P_CW4 + (j + 1) * 4].unsqueeze(2).to_broadcast([128, 4, 128])
                    P.op("pool", lambda e: e.tensor_tensor(o, i0, i1, ALU.mult), reads=[ident, pp], writes=[dslot])

                def run(ws, dslot):
                    ps = ps_alloc()
                    conv_group(ps, dslot, 4, zrx[c], 1, [dslot, zrx[c]])
                    act(xbbf[c], xbbf[c].ap, ps, ps.ap[:, :], AF.Identity, bias=ppc(P_LCB + j), extra_reads=[pp])
                return Unit(run, dl=dl)

            def mk_gate(j, s=s, rstate=rstate):
                g0 = (j // 5) * 5
                nb = list(range(max(g0, j - 1), min(g0 + 5, j + 2)))

                def wl(slot):
                    v = s256(slot)
                    n = len(nb)
                    wdma(slot, [(v[:, 0:n, 0:128], wa_v[:, nb[0]:nb[0] + n, 128 * j:128 * (j + 1)]),
                                (v[:, 0:n, 128:256], wx_v[:, nb[0]:nb[0] + n, 128 * j:128 * (j + 1)])])

                def run(slot, ds):
                    v = s256(slot)
                    psR = ps_alloc()
                    psI = ps_alloc()
                    rd = [slot] + [xbbf[i % 5] for i in nb]
                    mm_group(psR, [(v[:, ii, 0:128], xbbf[i % 5].ap) for ii, i in enumerate(nb)], rd)
                    mm_group(psI, [(v[:, ii, 128:256], xbbf[i % 5].ap) for ii, i in enumerate(nb)], rd)
                    tr = t32()
                    act(tr, tr.ap[:, :], psR, psR.ap[:, :], AF.Tanh, bias=dpc(P_BA + j), scale=0.5, extra_reads=[dp])
                    ti = t32()
                    act(ti, ti.ap[:, :], psI, psI.ap[:, :], AF.Tanh, bias=dpc(P_BX + j), scale=0.5, extra_reads=[dp])
                    a = t32()
                    act(a, a.ap[:, :], tr, tr.ap[:, :], AF.Exp, bias=dpc(D_CH + j), scale=dpc(D_CH + j),
                        extra_reads=[dp])
                    a2 = t32()
                    act(a2, a2.ap[:, :], tr, tr.ap[:, :], AF.Exp, bias=dpc(D_C + j), scale=dpc(D_C + j),
                        extra_reads=[dp])
                    act(a2, a2.ap[:, :], a2, a2.ap[:, :], AF.Sqrt, bias=cst.ap[:, 1:2], scale=-1.0, extra_reads=[cst])
                    xbc = xbbf[j % 5]
                    stt(ti, ti.ap[:, :], ti, ti.ap[:, :], 1.0, xbc, xbc.ap, ALU.add, ALU.mult)
                    tt(ti, ti.ap[:, :], ti, ti.ap[:, :], a2, a2.ap[:, :], ALU.mult)
                    hs = tr
                    init = 0.0 if s == 0 else hstate.ap[:, j:j + 1]
                    rds = [a, ti] + ([hstate] if s > 0 else [])
                    P.op("dve", lambda e: e.tensor_tensor_scan(hs.ap[:, :], a.ap[:, :], ti.ap[:, :], init,
                                                               ALU.mult, ALU.add), reads=rds, writes=[hs])
                    if s == 0:
                        P.op("dve", lambda e: e.tensor_copy(hstate.ap[:, j:j + 1], hs.ap[:, NT - 1:NT]),
                             reads=[hs], writes=[hstate])
                    rstate[j] = hs
                return Unit(run, wl=wl)

            gstate = {}

            def mk_gelu_a(j, gstate=gstate):
                def wl(slot):
                    v = s256(slot)
                    wdma(slot, [(v[:, 0:8, 0:128], w_in_v[:, 0:8, OFF_RG + 128 * j:OFF_RG + 128 * (j + 1)])])

                def run(slot, ds):
                    v = s256(slot)
                    ps = ps_alloc()
                    mm_group(ps, [(v[:, i, 0:128], hb[i].ap) for i in range(8)], [slot] + hb)
                    c = OFF_RG // 128 + j
                    zgq = t32()
                    act(zgq, zgq.ap[:, :], ps, ps.ap[:, :], AF.Identity, bias=dpc(D_BQ + j), scale=0.25, extra_reads=[dp])
                    q = t32()
                    act(q, q.ap[:, :], ps, ps.ap[:, :], AF.Square, bias=ppc(P_BIN + c), extra_reads=[pp])
                    P.op("dve", lambda e: e.tensor_scalar(q.ap[:, :], q.ap[:, :], 0.044715, 1.0, ALU.mult, ALU.add),
                         reads=[q], writes=[q])
                    tt(q, q.ap[:, :], q, q.ap[:, :], zgq, zgq.ap[:, :], ALU.mult)
                    gstate[j] = (zgq, q)
                return Unit(run, wl=wl)

            def mk_gelu_b(j, rstate=rstate, gstate=gstate):
                def run(ws, ds):
                    zgq, q = gstate[j]
                    act(q, q.ap[:, :], q, q.ap[:, :], AF.Tanh, scale=4.0 * 0.7978845608028654)
                    stt(q, q.ap[:, :], q, q.ap[:, :], 1.0, zgq, zgq.ap[:, :], ALU.add, ALU.mult)
                    hs = rstate[j]
                    tt(grnn[j], grnn[j].ap, q, q.ap[:, :], hs, hs.ap[:, :], ALU.mult)
                return Unit(run)

            def front_seq(g0, fillers, per):
                out = [mk_conv4(g0)]
                fi = 0
                for k in range(5):
                    j = g0 + k
                    if k < 4:
                        out.append(mk_conv4(j + 1))
                    out.append(mk_gelu_a(j))
                    if per[k] >= 1 and fi < len(fillers):
                        out.append(fillers[fi]); fi += 1
                    out.append(mk_gate(j))
                    if per[k] >= 2 and fi < len(fillers):
                        out.append(fillers[fi]); fi += 1
                    if per[k] >= 3 and fi < len(fillers):
                        out.append(fillers[fi]); fi += 1
                    out.append(mk_gelu_b(j))
                out += fillers[fi:]
                return out

            G_ = [mk_glu(j) for j in range(8)]
            C_ = [mk_conv31(j) for j in range(8)]
            U0 = Unit(lambda ws, ds: ustack(0))
            fl = [G_[0], U0, G_[1], G_[2], C_[0], G_[3], C_[1], G_[4], C_[2], G_[5], C_[3],
                  G_[6], C_[4], G_[7], C_[5], C_[6], C_[7]]
            units.extend([mk_r1(j) for j in range(5)])
            units.extend(front_seq(0, fl[:11], [3, 2, 2, 2, 2]))
            units.extend([mk_r1(j) for j in range(5, 10)])
            units.extend(front_seq(5, fl[11:], [2, 1, 1, 1, 1]))
            add(st_ln_stats)

            def mk_g(j):
                def wlg(slot):
                    v = s256(slot)
                    wdma(slot, [(v[:, 0:8, 0:128], w_in_v[:, 0:8, OFF_GA + 128 * j:OFF_GA + 128 * (j + 1)]),
                                (v[:, 0:8, 128:256], w_in_v[:, 0:8, OFF_GB + 128 * j:OFF_GB + 128 * (j + 1)])])

                def rung(slot, ds):
                    v = s256(slot)
                    psa = ps_alloc()
                    psb = ps_alloc()
                    mm_group(psa, [(v[:, i, 0:128], hb[i].ap) for i in range(8)], [slot] + hb)
                    mm_group(psb, [(v[:, i, 128:256], hb[i].ap) for i in range(8)], [slot] + hb)
                    act(mb[j], mb[j].ap, psa, psa.ap[:, :], AF.Tanh, bias=dpc(P_BIN + OFF_GA // 128 + j), scale=0.5,
                        extra_reads=[dp])
                    act(tgbs[j], tgbs[j].ap, psb, psb.ap[:, :], AF.Tanh, bias=dpc(P_BIN + OFF_GB // 128 + j),
                        scale=0.5, extra_reads=[dp])
                return Unit(rung, wl=wlg)

            def mk_y(j):
                def wly(slot):
                    v = s256(slot)
                    wdma(slot, [(v[:, 0:8, 0:128], w_co_v[:, 0:8, 128 * j:128 * (j + 1)]),
                                (v[:, 0:10, 128:256], w_lo_v[:, 0:10, 128 * j:128 * (j + 1)])])

                def runy(slot, ds):
                    v = s256(slot)
                    psa = ps_alloc()
                    psb = ps_alloc()
                    mm_group(psa, [(v[:, i, 0:128], vb[i].ap) for i in range(8)], [slot] + vb)
                    mm_group(psb, [(v[:, i, 128:256], grnn[i].ap) for i in range(10)], [slot] + grnn)
                    ya = t32()
                    act(ya, ya.ap[:, :], psa, psa.ap[:, :], AF.Identity, bias=ppc(P_CBO + j), scale=0.5, extra_reads=[pp])
                    stt(ya, ya.ap[:, :], mb[j], mb[j].ap, 1.0, ya, ya.ap[:, :], ALU.add, ALU.mult)
                    t2 = t32()
                    stt(t2, t2.ap[:, :], tgbs[j], tgbs[j].ap, 1.0, psb, psb.ap[:, :], ALU.add, ALU.mult)
                    tt(mb[j], mb[j].ap, ya, ya.ap[:, :], t2, t2.ap[:, :], ALU.add)
                return Unit(runy, wl=wly)

            for j in range(8):
                units.append(mk_ln_apply(j))
                units.append(mk_g(j))
            for j in range(8):
                units.append(mk_y(j))

            xstate = {}

            def st_mix_pre(ws, ds, xstate=xstate):
                inherit(x1, zrx + ybf + tgbs)
                inherit(h2, grnn + xbbf + gfb)
                xstate["ps"] = ps_reserve()
            add(st_mix_pre)
            for j in range(8):
                def wl(slot, j=j):
                    v = s256(slot)
                    wdma(slot, [(v[:, 0:8, 0:128], w_mx_v[:, 0:8, 128 * j:128 * (j + 1)])])

                def run(slot, ds, j=j, s=s, xstate=xstate):
                    v = s256(slot)
                    ps = ps_alloc()
                    mm_group(ps, [(v[:, i, 0:128], mb[i].ap) for i in range(8)], [slot] + mb)
                    if j > 0:
                        stat_group(xstate["ps"], xstate["sq"], j - 1 == 0, False)
                    xb = load_x(j, s)
                    stt(x1[j], x1[j].ap, ps, ps.ap[:, :], 0.5, xb, xb.ap[:, :], ALU.mult, ALU.add)
                    sq = t16()
                    act(sq, sq.ap[:, :], x1[j], x1[j].ap, AF.Square)
                    xstate["sq"] = sq
                    P.op("dve", lambda e: e.tensor_scalar(h2[j].ap, x1[j].ap, ppc(P_G2 + j), None, ALU.mult),
                         reads=[x1[j], pp], writes=[h2[j]])
                add(run, wl=wl)

            def st_norm2(ws, ds, xstate=xstate):
                ps = xstate["ps"]
                stat_group(ps, xstate["sq"], False, True)
                rstd_from(ps, ps.ap[:, :], 1.0 / D)
                ps_release(ps)
                inherit(gfb, hb + vb + grnn)
            add(st_norm2)

            for k in range(22):
                def wl(slot, k=k):
                    v = s256(slot)
                    wdma(slot, [(v[:, 0:8, 0:128], w_f1_v[:, 0:8, 128 * k:128 * (k + 1)]),
                                (v[:, 0:8, 128:256], w_f3_v[:, 0:8, 128 * k:128 * (k + 1)])])

                def run(slot, ds, k=k):
                    v = s256(slot)
                    p1 = ps_alloc()
                    p3 = ps_alloc()
                    mm_group(p1, [(v[:, i, 0:128], h2[i].ap) for i in range(8)], [slot] + h2)
                    mm_group(p3, [(v[:, i, 128:256], h2[i].ap) for i in range(8)], [slot] + h2)
                    s1 = t32()
                    tt(s1, s1.ap[:, :], p1, p1.ap[:, :], rstd_bc, rstd_bc.ap[:, :], ALU.mult)
                    act(s1, s1.ap[:, :], s1, s1.ap[:, :], AF.Silu)
                    t3 = t32()
                    tt(t3, t3.ap[:, :], p3, p3.ap[:, :], rstd_bc, rstd_bc.ap[:, :], ALU.mult)
                    tt(gfb[k], gfb[k].ap, s1, s1.ap[:, :], t3, t3.ap[:, :], ALU.mult)
                add(run, wl=wl)

            fstate = {}

            def st_dn_pre(ws, ds, fstate=fstate, s=s):
                if s + 1 < NS:
                    norm1_stats(s + 1, mu_bc)
                fstate["ps"] = ps_reserve()
            add(st_dn_pre)
            for j in range(8):
                def wl(slot, j=j):
                    v = s128(slot)
                    wdma(slot, [(v[:, 0:22, 0:128], w_f2_v[:, 0:22, 128 * j:128 * (j + 1)])])

                def run(slot, ds, j=j, fstate=fstate):
                    v = s128(slot)
                    ps = ps_alloc()
                    mm_group(ps, [(v[:, k, 0:128], gfb[k].ap) for k in range(22)], [slot] + gfb)
                    if j > 0:
                        stat_group(fstate["ps"], fstate["sq"], j - 1 == 0, False)
                    tt(x1[j], x1[j].ap, ps, ps.ap[:, :], x1[j], x1[j].ap, ALU.add)
                    sq = t16()
                    act(sq, sq.ap[:, :], x1[j], x1[j].ap, AF.Square)
                    fstate["sq"] = sq
                add(run, wl=wl)

            def st_out(ws, ds, s=s, fstate=fstate):
                ps = fstate["ps"]
                stat_group(ps, fstate["sq"], False, True)
                rstd_from(ps, ps.ap[:, :], 1.0 / D)
                ps_release(ps)
                for j in range(8):
                    o = t32()
                    stt(o, o.ap[:, :], x1[j], x1[j].ap, ppc(P_GF + j), rstd_bc, rstd_bc.ap[:, :],
                        ALU.mult, ALU.mult, extra_reads=[pp])
                    sem = P.bufsem(o)
                    dst = outT[j * 128:(j + 1) * 128, s * NT:(s + 1) * NT]
                    out_deps.append(P.dma("sp", lambda e, o=o, dst=dst: e.dma_start(out=dst, in_=o.ap[:, :]),
                                          sem, reads=[o]))
            pending_out["fn"] = st_out

        add(pending_out.pop("fn"))

        wl_idx = [i for i, u in enumerate(units) if u.wl is not None]
        dl_idx = [i for i, u in enumerate(units) if u.dl is not None]
        wslot_of, dslot_of = {}, {}
        wp = dp_ = 0
        wdone = ddone = 0
        if max_units is not None:
            units = units[:max_units]
            wl_idx = [i for i in wl_idx if i < max_units]
            dl_idx = [i for i in dl_idx if i < max_units]
        for idx, u in enumerate(units):
            while wp < len(wl_idx) and (wp - wdone) < NW and wl_idx[wp] <= idx + 6:
                ui = wl_idx[wp]
                slot = wslots[wp % NW]
                wslot_of[ui] = slot
                units[ui].wl(slot)
                wp += 1
            while dp_ < len(dl_idx) and (dp_ - ddone) < ND and dl_idx[dp_] <= idx + 4:
                ui = dl_idx[dp_]
                slot = dslots[dp_ % ND]
                dslot_of[ui] = slot
                units[ui].dl(slot)
                dp_ += 1
            u.run(wslot_of.get(idx), dslot_of.get(idx))
            if u.wl is not None:
                wdone += 1
            if u.dl is not None:
                ddone += 1

        P.wait_all("sp", out_deps)
        P.emit()
    return nc


def _pack_params(inp):
    def cols(v):
        v = np.asarray(v, np.float32).reshape(-1)
        return v.reshape(-1, 128).T
    cw31 = np.asarray(inp["conv_dw_w"][0], np.float32)
    cw31c = np.zeros((128, 256), np.float32)
    for jc in range(8):
        for g in range(4):
            for jj in range(8):
                for sft in range(4):
                    d = 8 * sft + jj
                    if d <= 30:
                        ch0 = jc * 128 + 32 * g
                        cw31c[32 * sft:32 * sft + 32, jc * 32 + g * 8 + jj] = cw31[30 - d, ch0:ch0 + 32]
    cw4 = np.asarray(inp["lru_conv_w"][0], np.float32)
    cw4c = np.concatenate([cw4[:, j * 128:(j + 1) * 128].T for j in range(10)], axis=1)
    parts = [cols(inp["b_in"][0]), cols(inp["lru_conv_b"][0]), cols(inp["lru_ba"][0]), cols(inp["lru_bx"][0]),
             cols(inp["conv_ln_g"][0]), cols(inp["conv_ln_b"][0]), cw31c,
             cols(inp["norm1_g"][0]), cols(inp["conv_dw_b"][0]), cols(inp["conv_b_out"][0]),
             cols(inp["lru_lambda"][0]), cols(inp["norm2_g"][0]), cols(inp["norm_f_g"]), cw4c]
    pp = np.ascontiguousarray(np.concatenate(parts, axis=1), dtype=np.float32)
    assert pp.shape == (128, NPP), pp.shape
    return pp


def _blockdiag(w):
    w = np.asarray(w, np.float32)
    out = np.zeros((DR, DR), np.float32)
    for h in range(16):
        out[h * 80:(h + 1) * 80, h * 80:(h + 1) * 80] = w[h]
    return out


_NC_CACHE = {}


def kernel(**inp):
    x = np.asarray(inp["x"], np.float32)
    nb = x.shape[0]
    if "nc" not in _NC_CACHE:
        _NC_CACHE["nc"] = build_program()
    nc = _NC_CACHE["nc"]
    shared = {
        "w_in": np.ascontiguousarray(inp["w_in"][0], dtype=np.float32),
        "w_co": np.ascontiguousarray(inp["conv_w_out"][0], dtype=np.float32),
        "w_lo": np.ascontiguousarray(inp["lru_w_out"][0], dtype=np.float32),
        "w_mx": np.ascontiguousarray(inp["w_mix_out"][0], dtype=np.float32),
        "w_f1": np.ascontiguousarray(inp["ffn_w1"][0], dtype=np.float32),
        "w_f3": np.ascontiguousarray(inp["ffn_w3"][0], dtype=np.float32),
        "w_f2": np.ascontiguousarray(inp["ffn_w2"][0], dtype=np.float32),
        "wa_bd": _blockdiag(inp["lru_wa"][0]),
        "wx_bd": _blockdiag(inp["lru_wx"][0]),
        "pp": _pack_params(inp),
        "ident": np.eye(128, dtype=np.float32),
        "emat": np.ascontiguousarray(np.tile(np.eye(32, dtype=np.float32), (4, 1))),
    }
    in_maps = []
    for b in range(nb):
        m = dict(shared)
        m["xT"] = np.ascontiguousarray(x[b].T)
        in_maps.append(m)
    res = run_bass_kernel_spmd(nc, in_maps, core_ids=list(range(nb)))
    out = np.stack([np.ascontiguousarray(r["outT"].T) for r in res.results], axis=0)
    return out.astype(np.float32)
```

```python
import numpy as np
from contextlib import ExitStack
import concourse.bass as bass
import concourse.mybir as mybir
from concourse.bass_utils import run_bass_kernel_spmd

F32 = mybir.dt.float32
BF16 = mybir.dt.bfloat16
AF = mybir.ActivationFunctionType
ALU = mybir.AluOpType

D = 1024
S = 2048
NT = 1024
NS = S // NT
DR = 1280
DFF = 2816
DIN = 6656
OFF_A, OFF_B, OFF_RX, OFF_RG, OFF_GA, OFF_GB = 0, 1024, 2048, 3328, 4608, 5632
EPS = 1e-6

P_BIN, P_LCB, P_BA, P_BX, P_LNG, P_LNB, P_CW31 = 0, 52, 62, 72, 82, 90, 98
NHALF = 354
P_G1, P_CDB, P_CBO, P_LAM, P_G2, P_GF, P_CW4 = 354, 362, 370, 378, 388, 396, 404
NPP = 444
D_C, D_CH = 354, 364
D_BQ = 374
NDP = 384


class Buf:
    def __init__(self, name, ap):
        self.name = name
        self.ap = ap
        self.w = []
        self.r = []
        self.sem = None


class Prog:
    ENGS = ("pe", "act", "dve", "pool", "sp")

    def __init__(self, nc, es):
        self.nc = nc
        self.es = es
        self.sems = {}
        self.semval = {}
        self.ops = {e: [] for e in self.ENGS}
        self.waited = {e: {} for e in self.ENGS}
        for e in self.ENGS:
            self.newsem("c_" + e)

    def newsem(self, key):
        self.sems[key] = self.es.enter_context(self.nc.semaphore(key))
        self.semval[key] = 0
        return key

    def bufsem(self, b):
        if b.sem is None:
            b.sem = self.newsem("s_" + b.name)
        return b.sem

    def _waits(self, eng, reads, writes):
        own = "c_" + eng
        need = {}

        def add(d, same_ok):
            k, v = d
            if k == own and (eng == "pe" or not same_ok):
                return
            if v > need.get(k, 0):
                need[k] = v
        for b in reads:
            for d in b.w:
                add(d, True)
        for b in writes:
            for d in b.w:
                add(d, True)
            for d in b.r:
                add(d, True)
        out = []
        wd = self.waited[eng]
        for k, v in need.items():
            if wd.get(k, 0) >= v:
                continue
            wd[k] = v
            out.append((k, v))
        return out

    def _mark(self, dep, reads, writes):
        for b in reads:
            b.r.append(dep)
        for b in writes:
            b.w = [dep]
            b.r = []

    def op(self, eng, fn, reads=(), writes=()):
        waits = self._waits(eng, reads, writes)
        key = "c_" + eng
        self.semval[key] += 1
        dep = (key, self.semval[key])
        self.ops[eng].append((waits, fn, (key, 1)))
        self._mark(dep, reads, writes)
        return dep

    def dma(self, eng, fn, semkey, reads=(), writes=(), n=1):
        waits = self._waits(eng, reads, writes)
        self.semval[semkey] += 16 * n
        dep = (semkey, self.semval[semkey])
        self.ops[eng].append((waits, fn, (semkey, 16)))
        self._mark(dep, reads, writes)
        return dep

    def wait_all(self, eng, deps):
        waits = []
        wd = self.waited[eng]
        for k, v in deps:
            if wd.get(k, 0) < v:
                wd[k] = v
                waits.append((k, v))
        self.ops[eng].append((waits, None, None))

    def emit(self):
        nc = self.nc
        with nc.Block() as block:
            def run(name):
                def f(e):
                    for waits, fn, incs in self.ops[name]:
                        for k, v in waits:
                            e.wait_ge(self.sems[k], v)
                        if fn is None:
                            continue
                        ins = fn(e)
                        if incs is not None:
                            if isinstance(ins, (list, tuple)):
                                for i_ in ins:
                                    i_.then_inc(self.sems[incs[0]], incs[1])
                            else:
                                ins.then_inc(self.sems[incs[0]], incs[1])
                return f
            block.tensor(run("pe"))
            block.scalar(run("act"))
            block.vector(run("dve"))
            block.gpsimd(run("pool"))
            block.sync(run("sp"))


def inherit(news, olds):
    deps = []
    for b in olds:
        deps.extend(b.w)
        deps.extend(b.r)
    for n in news:
        n.r = list(n.r) + deps


class Unit:
    def __init__(self, run, wl=None, dl=None):
        self.run = run
        self.wl = wl
        self.dl = dl


def build_program(max_units=None):
    nc = bass.Bass("TRN2", target_bir_lowering=False)
    es = ExitStack()

    def din(name, shape):
        return nc.dram_tensor(name, shape, F32, kind="ExternalInput").ap()
    xT = din("xT", [D, S])
    w_in = din("w_in", [D, DIN])
    w_co = din("w_co", [D, D])
    w_lo = din("w_lo", [DR, D])
    w_mx = din("w_mx", [D, D])
    w_f1 = din("w_f1", [D, DFF])
    w_f3 = din("w_f3", [D, DFF])
    w_f2 = din("w_f2", [DFF, D])
    wa_bd = din("wa_bd", [DR, DR])
    wx_bd = din("wx_bd", [DR, DR])
    ppd = din("pp", [128, NPP])
    identd = din("ident", [128, 128])
    ematd = din("emat", [128, 32])
    outT = nc.dram_tensor("outT", [D, S], F32, kind="ExternalOutput").ap()

    def wv(w):
        return w.rearrange("(k p) n -> p k n", p=128)
    w_in_v, w_co_v, w_lo_v, w_mx_v = wv(w_in), wv(w_co), wv(w_lo), wv(w_mx)
    w_f1_v, w_f3_v, w_f2_v, wa_v, wx_v = wv(w_f1), wv(w_f3), wv(w_f2), wv(wa_bd), wv(wx_bd)

    with es:
        P = Prog(nc, es)

        def sb(name, shape, dt):
            return es.enter_context(nc.sbuf_tensor(name, shape, dt))

        pp = Buf("pp", sb("pp_sb", [128, NPP], F32))
        dp = Buf("dp", sb("dp_sb", [128, NDP], F32))
        cst = Buf("cst", sb("cst_sb", [128, 4], F32))
        tmp10 = Buf("tmp10", sb("tmp10", [128, 16], F32))
        hstate = Buf("hstate", sb("hstate", [128, 16], F32))
        ident = Buf("ident", sb("ident_sb", [128, 128], BF16))
        ones = Buf("ones", sb("ones_sb", [128, 128], BF16))
        emat = Buf("emat", sb("emat_sb", [128, 32], BF16))
        Ust = [Buf(f"Ust{g}", sb(f"Ust{g}", [128, 1032], BF16)) for g in range(4)]
        uhalo = Buf("uhalo", sb("uhalo", [128, 8 * 32], BF16))
        zhalo = Buf("zhalo", sb("zhalo", [128, 10 * 4], BF16))
        regA = sb("regA", [128, 16384], BF16)
        regB = sb("regB", [128, 13824], BF16)
        regC = sb("regC", [128, 26 * 1024], BF16)
        NT32 = 11
        t32s = [Buf(f"t32_{i}", sb(f"t32_{i}", [128, NT], F32)) for i in range(NT32)]
        t16s = [Buf(f"t16_{i}", sb(f"t16_{i}", [128, NT], BF16)) for i in range(2)]
        rstd_bc = Buf("rstd_bc", sb("rstd_bc", [128, NT], F32))
        mu_bc = Buf("mu_bc", sb("mu_bc", [128, NT], F32))
        NW = 3
        wslots = [Buf(f"ws{i}", sb(f"ws{i}", [128, 2816], BF16)) for i in range(NW)]
        ND = 2
        dslots = [Buf(f"ds{i}", sb(f"ds{i}", [128, 1024], BF16)) for i in range(ND)]
        pss = [Buf(f"ps{i}", es.enter_context(nc.psum_tensor(f"ps{i}", [128, NT], F32))) for i in range(4)]

        ybf = [Buf(f"ybf{j}", regA[:, j * 1024:(j + 1) * 1024]) for j in range(8)]
        zrx = [Buf(f"zrx{j}", regA[:, 8192 + j * 1028:8192 + (j + 1) * 1028]) for j in range(5)]
        x1 = [Buf(f"x1_{j}", regA[:, j * 2048:(j + 1) * 2048].bitcast(F32)) for j in range(8)]
        upad = [Buf(f"upad{j}", regB[:, j * 1056:(j + 1) * 1056]) for j in range(8)]
        tgbs = [Buf(f"tgbs{j}", regA[:, 8192 + j * 1024:8192 + (j + 1) * 1024]) for j in range(8)]
        xbbf = [Buf(f"xbbf{j}", regB[:, 8448 + j * 1024:8448 + (j + 1) * 1024]) for j in range(5)]
        mb = [Buf(f"m{j}", regB[:, j * 1024:(j + 1) * 1024]) for j in range(8)]
        h2 = [Buf(f"h2_{j}", regC[:, 22528 + j * 1024:22528 + (j + 1) * 1024]) for j in range(4)] + \
             [Buf(f"h2_{j}", regB[:, 8448 + (j - 4) * 1024:8448 + (j - 3) * 1024]) for j in range(4, 8)]
        hb = [Buf(f"h{j}", regC[:, j * 1024:(j + 1) * 1024]) for j in range(8)]
        vb = [Buf(f"v{j}", regC[:, (8 + j) * 1024:(9 + j) * 1024]) for j in range(8)]
        grnn = [Buf(f"grnn{j}", regC[:, (16 + j) * 1024:(17 + j) * 1024]) for j in range(10)]
        gfb = [Buf(f"gf{k}", regC[:, k * 1024:(k + 1) * 1024]) for k in range(22)]

        state = {"t32": 0, "t16": 0, "ps": 0, "reserved": set()}

        def t32():
            b = t32s[state["t32"] % NT32]
            state["t32"] += 1
            return b

        def t16():
            b = t16s[state["t16"] % 2]
            state["t16"] += 1
            return b

        def ps_alloc():
            while True:
                i = state["ps"] % 4
                state["ps"] += 1
                if i not in state["reserved"]:
                    return pss[i]

        def ps_reserve():
            b = ps_alloc()
            state["reserved"].add(pss.index(b))
            return b

        def ps_release(b):
            state["reserved"].discard(pss.index(b))

        def ppc(c):
            return pp.ap[:, c:c + 1]

        def dpc(c):
            return dp.ap[:, c:c + 1]

        def mm_group(ps, pairs, reads):
            n = len(pairs)

            def fn(e):
                last = None
                for i, (l, r) in enumerate(pairs):
                    for t in range(2):
                        last = e.matmul(ps.ap[:, t * 512:(t + 1) * 512], l, r[:, t * 512:(t + 1) * 512],
                                        start=(i == 0), stop=(i == n - 1))
                return last
            P.op("pe", fn, reads=reads, writes=[ps])

        def conv_group(ps, dslot, ntap, src, base, reads):
            def fn(e):
                last = None
                for t in range(2):
                    for k in range(ntap):
                        o = base + k + t * 512
                        last = e.matmul(ps.ap[:, t * 512:(t + 1) * 512], dslot.ap[:, k * 128:(k + 1) * 128],
                                        src.ap[:, o:o + 512], start=(k == 0), stop=(k == ntap - 1))
                return last
            P.op("pe", fn, reads=reads, writes=[ps])

        def stat_group(ps, src, first, last_):
            def fn(e):
                l = None
                for t in range(2):
                    l = e.matmul(ps.ap[:, t * 512:(t + 1) * 512], ones.ap[:, :], src.ap[:, t * 512:(t + 1) * 512],
                                 start=first, stop=last_)
                return l
            P.op("pe", fn, reads=[ones, src], writes=[ps])

        def act(out_b, out_ap, in_b, in_ap, func, bias=None, scale=None, extra_reads=()):
            kw = {}
            if bias is not None:
                kw["bias"] = bias
            if scale is not None:
                kw["scale"] = scale
            P.op("act", lambda e: e.activation(out_ap, in_ap, func, **kw),
                 reads=[in_b] + list(extra_reads), writes=[out_b])

        def stt(out_b, out_ap, in0_b, in0_ap, scalar, in1_b, in1_ap, op0, op1, extra_reads=()):
            P.op("dve", lambda e: e.scalar_tensor_tensor(out_ap, in0_ap, scalar, in1_ap, op0, op1),
                 reads=[in0_b, in1_b] + list(extra_reads), writes=[out_b])

        def tt(out_b, out_ap, in0_b, in0_ap, in1_b, in1_ap, op):
            P.op("dve", lambda e: e.tensor_tensor(out_ap, in0_ap, in1_ap, op),
                 reads=[in0_b, in1_b], writes=[out_b])

        def load_x(j, s):
            xb = t32()
            sem = P.bufsem(xb)
            src = xT[j * 128:(j + 1) * 128, s * NT:(s + 1) * NT]
            P.dma("sp", lambda e: e.dma_start(out=xb.ap[:, :], in_=src), sem, writes=[xb])
            return xb

        def wdma(slot, items):
            sem = P.bufsem(slot)

            def fn(e):
                res = []
                for dst, src in items:
                    res.append(e.dma_start(out=dst, in_=src))
                return res
            P.dma("pool", fn, sem, writes=[slot], n=len(items))

        def s256(slot):
            return slot.ap[:, :].rearrange("p (k n) -> p k n", n=256)

        def s128(slot):
            return slot.ap[:, :].rearrange("p (k n) -> p k n", n=128)

        def rstd_from(ps_or_buf, in_ap, scale):
            lnv = t32()
            act(lnv, lnv.ap[:, :], ps_or_buf, in_ap, AF.Ln, bias=cst.ap[:, 0:1], scale=scale, extra_reads=[cst])
            act(rstd_bc, rstd_bc.ap[:, :], lnv, lnv.ap[:, :], AF.Exp, scale=-0.5)

        P.newsem("s_par")
        P.dma("sp", lambda e: e.dma_start(out=pp.ap[:, :], in_=ppd[:, :]), "s_par", writes=[pp])
        P.newsem("s_id")
        P.dma("pool", lambda e: e.dma_start(out=ident.ap[:, :], in_=identd[:, :]), "s_id", writes=[ident])
        P.newsem("s_em")
        P.dma("pool", lambda e: e.dma_start(out=emat.ap[:, :], in_=ematd[:, :]), "s_em", writes=[emat])
        P.op("pool", lambda e: e.memset(ones.ap[:, :], 1.0), writes=[ones])
        P.op("pool", lambda e: e.memset(cst.ap[:, 0:1], EPS), writes=[cst])
        P.op("pool", lambda e: e.memset(cst.ap[:, 1:2], 1.0), writes=[cst])
        P.op("dve", lambda e: e.tensor_scalar(dp.ap[:, 0:NHALF], pp.ap[:, 0:NHALF], 0.5, None, ALU.mult),
             reads=[pp], writes=[dp])
        act(tmp10, tmp10.ap[:, 0:10], pp, pp.ap[:, P_LAM:P_LAM + 10], AF.Exp, scale=-1.0)
        act(tmp10, tmp10.ap[:, 0:10], tmp10, tmp10.ap[:, 0:10], AF.Ln, bias=cst.ap[:, 1:2], extra_reads=[cst])
        P.op("dve", lambda e: e.tensor_scalar(dp.ap[:, D_C:D_C + 10], tmp10.ap[:, 0:10], -8.0, None, ALU.mult),
             reads=[tmp10], writes=[dp])
        P.op("dve", lambda e: e.tensor_scalar(dp.ap[:, D_CH:D_CH + 10], tmp10.ap[:, 0:10], -4.0, None, ALU.mult),
             reads=[tmp10], writes=[dp])
        P.op("dve", lambda e: e.tensor_scalar(dp.ap[:, D_BQ:D_BQ + 10], pp.ap[:, P_BIN + OFF_RG // 128:P_BIN + OFF_RG // 128 + 10],
                                              0.25, None, ALU.mult), reads=[pp], writes=[dp])

        units = []
        out_deps = []

        def add(run, wl=None, dl=None):
            units.append(Unit(run, wl, dl))

        pending_out = {}
        for s in range(NS):
            def norm1_stats(s_, dest, keep=None):
                ps = ps_reserve()
                for j in range(8):
                    xb = load_x(j, s_)
                    if keep is not None:
                        keep.append(xb)
                    sq = t16()
                    act(sq, sq.ap[:, :], xb, xb.ap[:, :], AF.Square)
                    stat_group(ps, sq, j == 0, j == 7)
                lnv = t32()
                act(lnv, lnv.ap[:, :], ps, ps.ap[:, :], AF.Ln, bias=cst.ap[:, 0:1], scale=1.0 / D, extra_reads=[cst])
                act(dest, dest.ap[:, :], lnv, lnv.ap[:, :], AF.Exp, scale=-0.5)
                ps_release(ps)

            def st_norm1(ws, ds, s=s):
                if s > 0:
                    inherit(hb + vb + grnn, gfb + h2)
                    src = mu_bc
                    kept = None
                else:
                    kept = []
                    norm1_stats(s, rstd_bc, kept)
                    src = rstd_bc
                for j in range(8):
                    xb = kept[j] if kept is not None else load_x(j, s)
                    stt(hb[j], hb[j].ap, xb, xb.ap[:, :], ppc(P_G1 + j), src, src.ap[:, :],
                        ALU.mult, ALU.mult, extra_reads=[pp])
            add(st_norm1)
            if "fn" in pending_out:
                add(pending_out.pop("fn"))

            def st_front_pre(ws, ds, s=s):
                if s > 0:
                    inherit(upad + xbbf, h2 + mb)
                    inherit(ybf + zrx, x1)
            add(st_front_pre)

            def mk_glu(j, s=s):
                def wl(slot):
                    v = s256(slot)
                    wdma(slot, [(v[:, 0:8, 0:128], w_in_v[:, 0:8, OFF_A + 128 * j:OFF_A + 128 * (j + 1)]),
                                (v[:, 0:8, 128:256], w_in_v[:, 0:8, OFF_B + 128 * j:OFF_B + 128 * (j + 1)])])

                def run(slot, ds):
                    v = s256(slot)
                    psA = ps_alloc()
                    psB = ps_alloc()
                    mm_group(psA, [(v[:, i, 0:128], hb[i].ap) for i in range(8)], [slot] + hb)
                    mm_group(psB, [(v[:, i, 128:256], hb[i].ap) for i in range(8)], [slot] + hb)
                    tb = t32()
                    act(tb, tb.ap[:, :], psB, psB.ap[:, :], AF.Tanh, bias=dpc(P_BIN + OFF_B // 128 + j), scale=0.5,
                        extra_reads=[dp])
                    za = t32()
                    P.op("dve", lambda e: e.tensor_scalar(za.ap[:, :], psA.ap[:, :], ppc(P_BIN + OFF_A // 128 + j), None,
                                                          ALU.add), reads=[psA, pp], writes=[za])
                    u = upad[j]
                    stt(u, u.ap[:, 32:32 + NT], tb, tb.ap[:, :], 1.0, za, za.ap[:, :], ALU.add, ALU.mult)
                    if s == 0:
                        P.op("pool", lambda e: e.memset(u.ap[:, 0:32], 0.0), writes=[u])
                        P.op("pool", lambda e: e.tensor_copy(uhalo.ap[:, j * 32:(j + 1) * 32], u.ap[:, NT:NT + 32]),
                             reads=[u], writes=[uhalo])
                    else:
                        P.op("pool", lambda e: e.tensor_copy(u.ap[:, 0:32], uhalo.ap[:, j * 32:(j + 1) * 32]),
                             reads=[uhalo], writes=[u])
                return Unit(run, wl=wl)

            cstate = {}

            def ustack(jc):
                u = upad[jc]
                for g in range(4):
                    def fn(e, g=g):
                        res = []
                        for sh in range(4):
                            res.append(e.dma_start(out=Ust[g].ap[32 * sh:32 * sh + 32, 0:1032],
                                                   in_=u.ap[32 * g:32 * g + 32, 24 - 8 * sh:24 - 8 * sh + 1032]))
                        return res
                    P.dma("sp", fn, P.bufsem(Ust[g]), reads=[u], writes=[Ust[g]], n=4)

            def mk_conv31(j, cstate=cstate):
                def dl(dslot):
                    o = dslot.ap[:, :].rearrange("p (k n) -> p k n", n=32)
                    i0 = emat.ap[:, :].unsqueeze(1).to_broadcast([128, 32, 32])
                    i1 = dp.ap[:, P_CW31 + j * 32:P_CW31 + (j + 1) * 32].unsqueeze(2).to_broadcast([128, 32, 32])
                    P.op("pool", lambda e: e.tensor_tensor(o, i0, i1, ALU.mult), reads=[emat, dp], writes=[dslot])

                def run(ws, dslot):
                    ps = ps_alloc()

                    def cfn(e):
                        last = None
                        for t in range(2):
                            for jj in range(8):
                                for g in range(4):
                                    o_ = 8 + t * 512 - jj
                                    last = e.matmul(ps.ap[32 * g:32 * g + 32, t * 512:(t + 1) * 512],
                                                    dslot.ap[:, (g * 8 + jj) * 32:(g * 8 + jj + 1) * 32],
                                                    Ust[g].ap[:, o_:o_ + 512], start=(jj == 0), stop=(jj == 7),
                                                    tile_position=(0, 32 * g))
                        return last
                    P.op("pe", cfn, reads=[dslot] + Ust, writes=[ps])
                    if j < 7:
                        ustack(j + 1)
                    P.op("dve", lambda e: e.tensor_scalar(ybf[j].ap, ps.ap[:, :], ppc(P_CDB + j), None, ALU.add),
                         reads=[ps, pp], writes=[ybf[j]])
                return Unit(run, dl=dl)

            def st_ln_stats(ws, ds, cstate=cstate):
                pm = ps_reserve()
                pq = ps_reserve()
                for j in range(8):
                    ysq = t16()
                    act(ysq, ysq.ap[:, :], ybf[j], ybf[j].ap, AF.Square)
                    stat_group(pm, ybf[j], j == 0, j == 7)
                    stat_group(pq, ysq, j == 0, j == 7)
                act(mu_bc, mu_bc.ap[:, :], pm, pm.ap[:, :], AF.Identity, scale=1.0 / D)
                msq = t32()
                tt(msq, msq.ap[:, :], mu_bc, mu_bc.ap[:, :], mu_bc, mu_bc.ap[:, :], ALU.mult)
                var = t32()
                stt(var, var.ap[:, :], pq, pq.ap[:, :], 1.0 / D, msq, msq.ap[:, :], ALU.mult, ALU.subtract)
                ps_release(pm)
                ps_release(pq)
                rstd_from(var, var.ap[:, :], 1.0)
                inherit(mb, upad + xbbf)
                inherit(tgbs, zrx)

            def mk_ln_apply(j):
                def run(ws, ds):
                    d = t32()
                    tt(d, d.ap[:, :], ybf[j], ybf[j].ap, mu_bc, mu_bc.ap[:, :], ALU.subtract)
                    tt(d, d.ap[:, :], d, d.ap[:, :], rstd_bc, rstd_bc.ap[:, :], ALU.mult)
                    n = t32()
                    act(n, n.ap[:, :], d, d.ap[:, :], AF.Identity, bias=ppc(P_LNB + j), scale=ppc(P_LNG + j),
                        extra_reads=[pp])
                    tn = t32()
                    act(tn, tn.ap[:, :], d, d.ap[:, :], AF.Tanh, bias=dpc(P_LNB + j), scale=dpc(P_LNG + j),
                        extra_reads=[dp])
                    stt(vb[j], vb[j].ap, tn, tn.ap[:, :], 1.0, n, n.ap[:, :], ALU.add, ALU.mult)
                return Unit(run)

            def mk_r1(j, s=s):
                c = j % 5

                def wl(slot):
                    v = s256(slot)
                    wdma(slot, [(v[:, 0:8, 0:128], w_in_v[:, 0:8, OFF_RX + 128 * j:OFF_RX + 128 * (j + 1)])])

                def run(slot, ds):
                    v = s256(slot)
                    ps = ps_alloc()
                    mm_group(ps, [(v[:, i, 0:128], hb[i].ap) for i in range(8)], [slot] + hb)
                    z = zrx[c]
                    P.op("dve", lambda e: e.tensor_scalar(z.ap[:, 4:4 + NT], ps.ap[:, :], ppc(P_BIN + OFF_RX // 128 + j),
                                                          None, ALU.add), reads=[ps, pp], writes=[z])
                    if s == 0:
                        P.op("pool", lambda e: e.memset(z.ap[:, 0:4], 0.0), writes=[z])
                        P.op("pool", lambda e: e.tensor_copy(zhalo.ap[:, j * 4:(j + 1) * 4], z.ap[:, NT:NT + 4]),
                             reads=[z], writes=[zhalo])
                    else:
                        P.op("pool", lambda e: e.tensor_copy(z.ap[:, 0:4], zhalo.ap[:, j * 4:(j + 1) * 4]),
                             reads=[zhalo], writes=[z])
                return Unit(run, wl=wl)

            rstate = {}

            def mk_conv4(j, s=s):
                c = j % 5

                def dl(dslot):
                    o = dslot.ap[:, 0:512].rearrange("p (k n) -> p k n", n=128)
                    i0 = ident.ap[:, :].unsqueeze(1).to_broadcast([128, 4, 128])
                    i1 = pp.ap[:, P_CW4 + j * 4:P_CW4 + (j + 1) * 4].unsqueeze(2).to_broadcast([128, 4, 128])
                    P.op("pool", lambda e: e.tensor_tensor(o, i0, i1, ALU.mult), reads=[ident, pp], writes=[dslot])

                def run(ws, dslot):
                    ps = ps_alloc()
                    conv_group(ps, dslot, 4, zrx[c], 1, [dslot, zrx[c]])
                    P.op("dve", lambda e: e.tensor_scalar(xbbf[c].ap, ps.ap[:, :], ppc(P_LCB + j), None, ALU.add),
                         reads=[ps, pp], writes=[xbbf[c]])
                return Unit(run, dl=dl)

            def mk_gate(j, s=s, rstate=rstate):
                g0 = (j // 5) * 5
                nb = list(range(max(g0, j - 1), min(g0 + 5, j + 2)))

                def wl(slot):
                    v = s256(slot)
                    n = len(nb)
                    wdma(slot, [(v[:, 0:n, 0:128], wa_v[:, nb[0]:nb[0] + n, 128 * j:128 * (j + 1)]),
                                (v[:, 0:n, 128:256], wx_v[:, nb[0]:nb[0] + n, 128 * j:128 * (j + 1)])])

                def run(slot, ds):
                    v = s256(slot)
                    psR = ps_alloc()
                    psI = ps_alloc()
                    rd = [slot] + [xbbf[i % 5] for i in nb]
                    mm_group(psR, [(v[:, ii, 0:128], xbbf[i % 5].ap) for ii, i in enumerate(nb)], rd)
                    mm_group(psI, [(v[:, ii, 128:256], xbbf[i % 5].ap) for ii, i in enumerate(nb)], rd)
                    tr = t32()
                    act(tr, tr.ap[:, :], psR, psR.ap[:, :], AF.Tanh, bias=dpc(P_BA + j), scale=0.5, extra_reads=[dp])
                    ti = t32()
                    act(ti, ti.ap[:, :], psI, psI.ap[:, :], AF.Tanh, bias=dpc(P_BX + j), scale=0.5, extra_reads=[dp])
                    a = t32()
                    act(a, a.ap[:, :], tr, tr.ap[:, :], AF.Exp, bias=dpc(D_CH + j), scale=dpc(D_CH + j),
                        extra_reads=[dp])
                    a2 = t32()
                    act(a2, a2.ap[:, :], tr, tr.ap[:, :], AF.Exp, bias=dpc(D_C + j), scale=dpc(D_C + j),
                        extra_reads=[dp])
                    act(a2, a2.ap[:, :], a2, a2.ap[:, :], AF.Sqrt, bias=cst.ap[:, 1:2], scale=-1.0, extra_reads=[cst])
                    xbc = xbbf[j % 5]
                    stt(ti, ti.ap[:, :], ti, ti.ap[:, :], 1.0, xbc, xbc.ap, ALU.add, ALU.mult)
                    tt(ti, ti.ap[:, :], ti, ti.ap[:, :], a2, a2.ap[:, :], ALU.mult)
                    hs = tr
                    init = 0.0 if s == 0 else hstate.ap[:, j:j + 1]
                    rds = [a, ti] + ([hstate] if s > 0 else [])
                    P.op("dve", lambda e: e.tensor_tensor_scan(hs.ap[:, :], a.ap[:, :], ti.ap[:, :], init,
                                                               ALU.mult, ALU.add), reads=rds, writes=[hs])
                    if s == 0:
                        P.op("dve", lambda e: e.tensor_copy(hstate.ap[:, j:j + 1], hs.ap[:, NT - 1:NT]),
                             reads=[hs], writes=[hstate])
                    rstate[j] = hs
                return Unit(run, wl=wl)

            gstate = {}

            def mk_gelu_a(j, gstate=gstate):
                def wl(slot):
                    v = s256(slot)
                    wdma(slot, [(v[:, 0:8, 0:128], w_in_v[:, 0:8, OFF_RG + 128 * j:OFF_RG + 128 * (j + 1)])])

                def run(slot, ds):
                    v = s256(slot)
                    ps = ps_alloc()
                    mm_group(ps, [(v[:, i, 0:128], hb[i].ap) for i in range(8)], [slot] + hb)
                    c = OFF_RG // 128 + j
                    zgq = t32()
                    act(zgq, zgq.ap[:, :], ps, ps.ap[:, :], AF.Identity, bias=dpc(D_BQ + j), scale=0.25, extra_reads=[dp])
                    q = t32()
                    act(q, q.ap[:, :], ps, ps.ap[:, :], AF.Square, bias=ppc(P_BIN + c), extra_reads=[pp])
                    P.op("dve", lambda e: e.tensor_scalar(q.ap[:, :], q.ap[:, :], 0.044715, 1.0, ALU.mult, ALU.add),
                         reads=[q], writes=[q])
                    tt(q, q.ap[:, :], q, q.ap[:, :], zgq, zgq.ap[:, :], ALU.mult)
                    gstate[j] = (zgq, q)
                return Unit(run, wl=wl)

            def mk_gelu_b(j, rstate=rstate, gstate=gstate):
                def run(ws, ds):
                    zgq, q = gstate[j]
                    act(q, q.ap[:, :], q, q.ap[:, :], AF.Tanh, scale=4.0 * 0.7978845608028654)
                    stt(q, q.ap[:, :], q, q.ap[:, :], 1.0, zgq, zgq.ap[:, :], ALU.add, ALU.mult)
                    hs = rstate[j]
                    tt(grnn[j], grnn[j].ap, q, q.ap[:, :], hs, hs.ap[:, :], ALU.mult)
                return Unit(run)

            def front_seq(g0, fillers, per):
                out = [mk_conv4(g0)]
                fi = 0
                for k in range(5):
                    j = g0 + k
                    if k < 4:
                        out.append(mk_conv4(j + 1))
                    out.append(mk_gelu_a(j))
                    if per[k] >= 1 and fi < len(fillers):
                        out.append(fillers[fi]); fi += 1
                    out.append(mk_gate(j))
                    if per[k] >= 2 and fi < len(fillers):
                        out.append(fillers[fi]); fi += 1
                    if per[k] >= 3 and fi < len(fillers):
                        out.append(fillers[fi]); fi += 1
                    out.append(mk_gelu_b(j))
                out += fillers[fi:]
                return out

            G_ = [mk_glu(j) for j in range(8)]
            C_ = [mk_conv31(j) for j in range(8)]
            U0 = Unit(lambda ws, ds: ustack(0))
            fl = [G_[0], U0, G_[1], G_[2], C_[0], G_[3], C_[1], G_[4], C_[2], G_[5], C_[3],
                  G_[6], C_[4], G_[7], C_[5], C_[6], C_[7]]
            units.extend([mk_r1(j) for j in range(5)])
            units.extend(front_seq(0, fl[:11], [3, 2, 2, 2, 2]))
            units.extend([mk_r1(j) for j in range(5, 10)])
            units.extend(front_seq(5, fl[11:], [2, 1, 1, 1, 1]))
            add(st_ln_stats)

            def mk_g(j):
                def wlg(slot):
                    v = s256(slot)
                    wdma(slot, [(v[:, 0:8, 0:128], w_in_v[:, 0:8, OFF_GA + 128 * j:OFF_GA + 128 * (j + 1)]),
                                (v[:, 0:8, 128:256], w_in_v[:, 0:8, OFF_GB + 128 * j:OFF_GB + 128 * (j + 1)])])

                def rung(slot, ds):
                    v = s256(slot)
                    psa = ps_alloc()
                    psb = ps_alloc()
                    mm_group(psa, [(v[:, i, 0:128], hb[i].ap) for i in range(8)], [slot] + hb)
                    mm_group(psb, [(v[:, i, 128:256], hb[i].ap) for i in range(8)], [slot] + hb)
                    act(mb[j], mb[j].ap, psa, psa.ap[:, :], AF.Tanh, bias=dpc(P_BIN + OFF_GA // 128 + j), scale=0.5,
                        extra_reads=[dp])
                    act(tgbs[j], tgbs[j].ap, psb, psb.ap[:, :], AF.Tanh, bias=dpc(P_BIN + OFF_GB // 128 + j),
                        scale=0.5, extra_reads=[dp])
                return Unit(rung, wl=wlg)

            def mk_y(j):
                def wly(slot):
                    v = s256(slot)
                    wdma(slot, [(v[:, 0:8, 0:128], w_co_v[:, 0:8, 128 * j:128 * (j + 1)]),
                                (v[:, 0:10, 128:256], w_lo_v[:, 0:10, 128 * j:128 * (j + 1)])])

                def runy(slot, ds):
                    v = s256(slot)
                    psa = ps_alloc()
                    psb = ps_alloc()
                    mm_group(psa, [(v[:, i, 0:128], vb[i].ap) for i in range(8)], [slot] + vb)
                    mm_group(psb, [(v[:, i, 128:256], grnn[i].ap) for i in range(10)], [slot] + grnn)
                    ya = t32()
                    act(ya, ya.ap[:, :], psa, psa.ap[:, :], AF.Identity, bias=ppc(P_CBO + j), scale=0.5, extra_reads=[pp])
                    stt(ya, ya.ap[:, :], mb[j], mb[j].ap, 1.0, ya, ya.ap[:, :], ALU.add, ALU.mult)
                    t2 = t32()
                    stt(t2, t2.ap[:, :], tgbs[j], tgbs[j].ap, 1.0, psb, psb.ap[:, :], ALU.add, ALU.mult)
                    tt(mb[j], mb[j].ap, ya, ya.ap[:, :], t2, t2.ap[:, :], ALU.add)
                return Unit(runy, wl=wly)

            for j in range(8):
                units.append(mk_ln_apply(j))
                units.append(mk_g(j))
            for j in range(8):
                units.append(mk_y(j))

            xstate = {}

            def st_mix_pre(ws, ds, xstate=xstate):
                inherit(x1, zrx + ybf + tgbs)
                inherit(h2, grnn + xbbf + gfb)
                xstate["ps"] = ps_reserve()
            add(st_mix_pre)
            for j in range(8):
                def wl(slot, j=j):
                    v = s256(slot)
                    wdma(slot, [(v[:, 0:8, 0:128], w_mx_v[:, 0:8, 128 * j:128 * (j + 1)])])

                def run(slot, ds, j=j, s=s, xstate=xstate):
                    v = s256(slot)
                    ps = ps_alloc()
                    mm_group(ps, [(v[:, i, 0:128], mb[i].ap) for i in range(8)], [slot] + mb)
                    if j > 0:
                        stat_group(xstate["ps"], xstate["sq"], j - 1 == 0, False)
                    xb = load_x(j, s)
                    stt(x1[j], x1[j].ap, ps, ps.ap[:, :], 0.5, xb, xb.ap[:, :], ALU.mult, ALU.add)
                    sq = t16()
                    act(sq, sq.ap[:, :], x1[j], x1[j].ap, AF.Square)
                    xstate["sq"] = sq
                    P.op("dve", lambda e: e.tensor_scalar(h2[j].ap, x1[j].ap, ppc(P_G2 + j), None, ALU.mult),
                         reads=[x1[j], pp], writes=[h2[j]])
                add(run, wl=wl)

            def st_norm2(ws, ds, xstate=xstate):
                ps = xstate["ps"]
                stat_group(ps, xstate["sq"], False, True)
                rstd_from(ps, ps.ap[:, :], 1.0 / D)
                ps_release(ps)
                inherit(gfb, hb + vb + grnn)
            add(st_norm2)

            for k in range(22):
                def wl(slot, k=k):
                    v = s256(slot)
                    wdma(slot, [(v[:, 0:8, 0:128], w_f1_v[:, 0:8, 128 * k:128 * (k + 1)]),
                                (v[:, 0:8, 128:256], w_f3_v[:, 0:8, 128 * k:128 * (k + 1)])])

                def run(slot, ds, k=k):
                    v = s256(slot)
                    p1 = ps_alloc()
                    p3 = ps_alloc()
                    mm_group(p1, [(v[:, i, 0:128], h2[i].ap) for i in range(8)], [slot] + h2)
                    mm_group(p3, [(v[:, i, 128:256], h2[i].ap) for i in range(8)], [slot] + h2)
                    s1 = t32()
                    tt(s1, s1.ap[:, :], p1, p1.ap[:, :], rstd_bc, rstd_bc.ap[:, :], ALU.mult)
                    act(s1, s1.ap[:, :], s1, s1.ap[:, :], AF.Silu)
                    t3 = t32()
                    tt(t3, t3.ap[:, :], p3, p3.ap[:, :], rstd_bc, rstd_bc.ap[:, :], ALU.mult)
                    tt(gfb[k], gfb[k].ap, s1, s1.ap[:, :], t3, t3.ap[:, :], ALU.mult)
                add(run, wl=wl)

            fstate = {}

            def st_dn_pre(ws, ds, fstate=fstate, s=s):
                if s + 1 < NS:
                    norm1_stats(s + 1, mu_bc)
                fstate["ps"] = ps_reserve()
            add(st_dn_pre)
            for j in range(8):
                def wl(slot, j=j):
                    v = s128(slot)
                    wdma(slot, [(v[:, 0:22, 0:128], w_f2_v[:, 0:22, 128 * j:128 * (j + 1)])])

                def run(slot, ds, j=j, fstate=fstate):
                    v = s128(slot)
                    ps = ps_alloc()
                    mm_group(ps, [(v[:, k, 0:128], gfb[k].ap) for k in range(22)], [slot] + gfb)
                    if j > 0:
                        stat_group(fstate["ps"], fstate["sq"], j - 1 == 0, False)
                    tt(x1[j], x1[j].ap, ps, ps.ap[:, :], x1[j], x1[j].ap, ALU.add)
                    sq = t16()
                    act(sq, sq.ap[:, :], x1[j], x1[j].ap, AF.Square)
                    fstate["sq"] = sq
                add(run, wl=wl)

            def st_out(ws, ds, s=s, fstate=fstate):
                ps = fstate["ps"]
                stat_group(ps, fstate["sq"], False, True)
                rstd_from(ps, ps.ap[:, :], 1.0 / D)
                ps_release(ps)
                for j in range(8):
                    o = t32()
                    stt(o, o.ap[:, :], x1[j], x1[j].ap, ppc(P_GF + j), rstd_bc, rstd_bc.ap[:, :],
                        ALU.mult, ALU.mult, extra_reads=[pp])
                    sem = P.bufsem(o)
                    dst = outT[j * 128:(j + 1) * 128, s * NT:(s + 1) * NT]
                    out_deps.append(P.dma("sp", lambda e, o=o, dst=dst: e.dma_start(out=dst, in_=o.ap[:, :]),
                                          sem, reads=[o]))
            pending_out["fn"] = st_out

        add(pending_out.pop("fn"))

        wl_idx = [i for i, u in enumerate(units) if u.wl is not None]
        dl_idx = [i for i, u in enumerate(units) if u.dl is not None]
        wslot_of, dslot_of = {}, {}
        wp = dp_ = 0
        wdone = ddone = 0
        if max_units is not None:
            units = units[:max_units]
            wl_idx = [i for i in wl_idx if i < max_units]
            dl_idx = [i for i in dl_idx if i < max_units]
        for idx, u in enumerate(units):
            while wp < len(wl_idx) and (wp - wdone) < NW and wl_idx[wp] <= idx + 6:
                ui = wl_idx[wp]
                slot = wslots[wp % NW]
                wslot_of[ui] = slot
                units[ui].wl(slot)
                wp += 1
            while dp_ < len(dl_idx) and (dp_ - ddone) < ND and dl_idx[dp_] <= idx + 4:
                ui = dl_idx[dp_]
                slot = dslots[dp_ % ND]
                dslot_of[ui] = slot
                units[ui].dl(slot)
                dp_ += 1
            u.run(wslot_of.get(idx), dslot_of.get(idx))
            if u.wl is not None:
                wdone += 1
            if u.dl is not None:
                ddone += 1

        P.wait_all("sp", out_deps)
        P.emit()
    return nc


def _pack_params(inp):
    def cols(v):
        v = np.asarray(v, np.float32).reshape(-1)
        return v.reshape(-1, 128).T
    cw31 = np.asarray(inp["conv_dw_w"][0], np.float32)
    cw31c = np.zeros((128, 256), np.float32)
    for jc in range(8):
        for g in range(4):
            for jj in range(8):
                for sft in range(4):
                    d = 8 * sft + jj
                    if d <= 30:
                        ch0 = jc * 128 + 32 * g
                        cw31c[32 * sft:32 * sft + 32, jc * 32 + g * 8 + jj] = cw31[30 - d, ch0:ch0 + 32]
    cw4 = np.asarray(inp["lru_conv_w"][0], np.float32)
    cw4c = np.concatenate([cw4[:, j * 128:(j + 1) * 128].T for j in range(10)], axis=1)
    parts = [cols(inp["b_in"][0]), cols(inp["lru_conv_b"][0]), cols(inp["lru_ba"][0]), cols(inp["lru_bx"][0]),
             cols(inp["conv_ln_g"][0]), cols(inp["conv_ln_b"][0]), cw31c,
             cols(inp["norm1_g"][0]), cols(inp["conv_dw_b"][0]), cols(inp["conv_b_out"][0]),
             cols(inp["lru_lambda"][0]), cols(inp["norm2_g"][0]), cols(inp["norm_f_g"]), cw4c]
    pp = np.ascontiguousarray(np.concatenate(parts, axis=1), dtype=np.float32)
    assert pp.shape == (128, NPP), pp.shape
    return pp


def _blockdiag(w):
    w = np.asarray(w, np.float32)
    out = np.zeros((DR, DR), np.float32)
    for h in range(16):
        out[h * 80:(h + 1) * 80, h * 80:(h + 1) * 80] = w[h]
    return out


_NC_CACHE = {}


def kernel(**inp):
    x = np.asarray(inp["x"], np.float32)
    nb = x.shape[0]
    if "nc" not in _NC_CACHE:
        _NC_CACHE["nc"] = build_program()
    nc = _NC_CACHE["nc"]
    shared = {
        "w_in": np.ascontiguousarray(inp["w_in"][0], dtype=np.float32),
        "w_co": np.ascontiguousarray(inp["conv_w_out"][0], dtype=np.float32),
        "w_lo": np.ascontiguousarray(inp["lru_w_out"][0], dtype=np.float32),
        "w_mx": np.ascontiguousarray(inp["w_mix_out"][0], dtype=np.float32),
        "w_f1": np.ascontiguousarray(inp["ffn_w1"][0], dtype=np.float32),
        "w_f3": np.ascontiguousarray(inp["ffn_w3"][0], dtype=np.float32),
        "w_f2": np.ascontiguousarray(inp["ffn_w2"][0], dtype=np.float32),
        "wa_bd": _blockdiag(inp["lru_wa"][0]),
        "wx_bd": _blockdiag(inp["lru_wx"][0]),
        "pp": _pack_params(inp),
        "ident": np.eye(128, dtype=np.float32),
        "emat": np.ascontiguousarray(np.tile(np.eye(32, dtype=np.float32), (4, 1))),
    }
    in_maps = []
    for b in range(nb):
        m = dict(shared)
        m["xT"] = np.ascontiguousarray(x[b].T)
        in_maps.append(m)
    res = run_bass_kernel_spmd(nc, in_maps, core_ids=list(range(nb)))
    out = np.stack([np.ascontiguousarray(r["outT"].T) for r in res.results], axis=0)
    return out.astype(np.float32)
```

```python
import numpy as np
from contextlib import ExitStack
import concourse.bass as bass
import concourse.mybir as mybir
from concourse.bass_utils import run_bass_kernel_spmd

F32 = mybir.dt.float32
BF16 = mybir.dt.bfloat16
AF = mybir.ActivationFunctionType
ALU = mybir.AluOpType

D = 1024
S = 2048
NT = 1024
NS = S // NT
DR = 1280
DFF = 2816
DIN = 6656
OFF_A, OFF_B, OFF_RX, OFF_RG, OFF_GA, OFF_GB = 0, 1024, 2048, 3328, 4608, 5632
EPS = 1e-6

P_BIN, P_LCB, P_BA, P_BX, P_LNG, P_LNB, P_CW31 = 0, 52, 62, 72, 82, 90, 98
NHALF = 354
P_G1, P_CDB, P_CBO, P_LAM, P_G2, P_GF, P_CW4 = 354, 362, 370, 378, 388, 396, 404
NPP = 444
D_C, D_CH = 354, 364
D_BQ = 374
NDP = 384


class Buf:
    def __init__(self, name, ap):
        self.name = name
        self.ap = ap
        self.w = []
        self.r = []
        self.sem = None


class Prog:
    ENGS = ("pe", "act", "dve", "pool", "sp")

    def __init__(self, nc, es):
        self.nc = nc
        self.es = es
        self.sems = {}
        self.semval = {}
        self.ops = {e: [] for e in self.ENGS}
        self.waited = {e: {} for e in self.ENGS}
        for e in self.ENGS:
            self.newsem("c_" + e)

    def newsem(self, key):
        self.sems[key] = self.es.enter_context(self.nc.semaphore(key))
        self.semval[key] = 0
        return key

    def bufsem(self, b):
        if b.sem is None:
            b.sem = self.newsem("s_" + b.name)
        return b.sem

    def _waits(self, eng, reads, writes):
        own = "c_" + eng
        need = {}

        def add(d, same_ok):
            k, v = d
            if k == own and (eng == "pe" or not same_ok):
                return
            if v > need.get(k, 0):
                need[k] = v
        for b in reads:
            for d in b.w:
                add(d, True)
        for b in writes:
            for d in b.w:
                add(d, True)
            for d in b.r:
                add(d, True)
        out = []
        wd = self.waited[eng]
        for k, v in need.items():
            if wd.get(k, 0) >= v:
                continue
            wd[k] = v
            out.append((k, v))
        return out

    def _mark(self, dep, reads, writes):
        for b in reads:
            b.r.append(dep)
        for b in writes:
            b.w = [dep]
            b.r = []

    def op(self, eng, fn, reads=(), writes=()):
        waits = self._waits(eng, reads, writes)
        key = "c_" + eng
        self.semval[key] += 1
        dep = (key, self.semval[key])
        self.ops[eng].append((waits, fn, (key, 1)))
        self._mark(dep, reads, writes)
        return dep

    def dma(self, eng, fn, semkey, reads=(), writes=(), n=1):
        waits = self._waits(eng, reads, writes)
        self.semval[semkey] += 16 * n
        dep = (semkey, self.semval[semkey])
        self.ops[eng].append((waits, fn, (semkey, 16)))
        self._mark(dep, reads, writes)
        return dep

    def wait_all(self, eng, deps):
        waits = []
        wd = self.waited[eng]
        for k, v in deps:
            if wd.get(k, 0) < v:
                wd[k] = v
                waits.append((k, v))
        self.ops[eng].append((waits, None, None))

    def emit(self):
        nc = self.nc
        with nc.Block() as block:
            def run(name):
                def f(e):
                    for waits, fn, incs in self.ops[name]:
                        for k, v in waits:
                            e.wait_ge(self.sems[k], v)
                        if fn is None:
                            continue
                        ins = fn(e)
                        if incs is not None:
                            if isinstance(ins, (list, tuple)):
                                for i_ in ins:
                                    i_.then_inc(self.sems[incs[0]], incs[1])
                            else:
                                ins.then_inc(self.sems[incs[0]], incs[1])
                return f
            block.tensor(run("pe"))
            block.scalar(run("act"))
            block.vector(run("dve"))
            block.gpsimd(run("pool"))
            block.sync(run("sp"))


def inherit(news, olds):
    deps = []
    for b in olds:
        deps.extend(b.w)
        deps.extend(b.r)
    for n in news:
        n.r = list(n.r) + deps


class Unit:
    def __init__(self, run, wl=None, dl=None):
        self.run = run
        self.wl = wl
        self.dl = dl


def build_program(max_units=None):
    nc = bass.Bass("TRN2", target_bir_lowering=False)
    es = ExitStack()

    def din(name, shape):
        return nc.dram_tensor(name, shape, F32, kind="ExternalInput").ap()
    xT = din("xT", [D, S])
    w_in = din("w_in", [D, DIN])
    w_co = din("w_co", [D, D])
    w_lo = din("w_lo", [DR, D])
    w_mx = din("w_mx", [D, D])
    w_f1 = din("w_f1", [D, DFF])
    w_f3 = din("w_f3", [D, DFF])
    w_f2 = din("w_f2", [DFF, D])
    wa_bd = din("wa_bd", [DR, DR])
    wx_bd = din("wx_bd", [DR, DR])
    ppd = din("pp", [128, NPP])
    identd = din("ident", [128, 128])
    ematd = din("emat", [128, 32])
    outT = nc.dram_tensor("outT", [D, S], F32, kind="ExternalOutput").ap()

    def wv(w):
        return w.rearrange("(k p) n -> p k n", p=128)
    w_in_v, w_co_v, w_lo_v, w_mx_v = wv(w_in), wv(w_co), wv(w_lo), wv(w_mx)
    w_f1_v, w_f3_v, w_f2_v, wa_v, wx_v = wv(w_f1), wv(w_f3), wv(w_f2), wv(wa_bd), wv(wx_bd)

    with es:
        P = Prog(nc, es)

        def sb(name, shape, dt):
            return es.enter_context(nc.sbuf_tensor(name, shape, dt))

        pp = Buf("pp", sb("pp_sb", [128, NPP], F32))
        dp = Buf("dp", sb("dp_sb", [128, NDP], F32))
        cst = Buf("cst", sb("cst_sb", [128, 4], F32))
        tmp10 = Buf("tmp10", sb("tmp10", [128, 16], F32))
        hstate = Buf("hstate", sb("hstate", [128, 16], F32))
        ident = Buf("ident", sb("ident_sb", [128, 128], BF16))
        ones = Buf("ones", sb("ones_sb", [128, 128], BF16))
        emat = Buf("emat", sb("emat_sb", [128, 32], BF16))
        Ust = [Buf(f"Ust{g}", sb(f"Ust{g}", [128, 1032], BF16)) for g in range(4)]
        uhalo = Buf("uhalo", sb("uhalo", [128, 8 * 32], BF16))
        zhalo = Buf("zhalo", sb("zhalo", [128, 10 * 4], BF16))
        regA = sb("regA", [128, 16384], BF16)
        regB = sb("regB", [128, 13824], BF16)
        regC = sb("regC", [128, 26 * 1024], BF16)
        NT32 = 10
        t32s = [Buf(f"t32_{i}", sb(f"t32_{i}", [128, NT], F32)) for i in range(NT32)]
        t16s = [Buf(f"t16_{i}", sb(f"t16_{i}", [128, NT], BF16)) for i in range(2)]
        rstd_bc = Buf("rstd_bc", sb("rstd_bc", [128, NT], F32))
        mu_bc = Buf("mu_bc", sb("mu_bc", [128, NT], F32))
        NW = 4
        wslots = [Buf(f"ws{i}", sb(f"ws{i}", [128, 2816], BF16)) for i in range(NW)]
        ND = 2
        dslots = [Buf(f"ds{i}", sb(f"ds{i}", [128, 1024], BF16)) for i in range(ND)]
        pss = [Buf(f"ps{i}", es.enter_context(nc.psum_tensor(f"ps{i}", [128, NT], F32))) for i in range(4)]

        ybf = [Buf(f"ybf{j}", regA[:, j * 1024:(j + 1) * 1024]) for j in range(8)]
        zrx = [Buf(f"zrx{j}", regA[:, 8192 + j * 1028:8192 + (j + 1) * 1028]) for j in range(5)]
        x1 = [Buf(f"x1_{j}", regA[:, j * 2048:(j + 1) * 2048].bitcast(F32)) for j in range(8)]
        upad = [Buf(f"upad{j}", regB[:, j * 1056:(j + 1) * 1056]) for j in range(8)]
        tgbs = [Buf(f"tgbs{j}", regA[:, 8192 + j * 1024:8192 + (j + 1) * 1024]) for j in range(8)]
        xbbf = [Buf(f"xbbf{j}", regB[:, 8448 + j * 1024:8448 + (j + 1) * 1024]) for j in range(5)]
        mb = [Buf(f"m{j}", regB[:, j * 1024:(j + 1) * 1024]) for j in range(8)]
        h2 = [Buf(f"h2_{j}", regC[:, 22528 + j * 1024:22528 + (j + 1) * 1024]) for j in range(4)] + \
             [Buf(f"h2_{j}", regB[:, 8448 + (j - 4) * 1024:8448 + (j - 3) * 1024]) for j in range(4, 8)]
        hb = [Buf(f"h{j}", regC[:, j * 1024:(j + 1) * 1024]) for j in range(8)]
        vb = [Buf(f"v{j}", regC[:, (8 + j) * 1024:(9 + j) * 1024]) for j in range(8)]
        grnn = [Buf(f"grnn{j}", regC[:, (16 + j) * 1024:(17 + j) * 1024]) for j in range(10)]
        gfb = [Buf(f"gf{k}", regC[:, k * 1024:(k + 1) * 1024]) for k in range(22)]

        state = {"t32": 0, "t16": 0, "ps": 0, "reserved": set()}

        def t32():
            b = t32s[state["t32"] % NT32]
            state["t32"] += 1
            return b

        def t16():
            b = t16s[state["t16"] % 2]
            state["t16"] += 1
            return b

        def ps_alloc():
            while True:
                i = state["ps"] % 4
                state["ps"] += 1
                if i not in state["reserved"]:
                    return pss[i]

        def ps_reserve():
            b = ps_alloc()
            state["reserved"].add(pss.index(b))
            return b

        def ps_release(b):
            state["reserved"].discard(pss.index(b))

        def ppc(c):
            return pp.ap[:, c:c + 1]

        def dpc(c):
            return dp.ap[:, c:c + 1]

        def mm_group(ps, pairs, reads):
            n = len(pairs)

            def fn(e):
                last = None
                for i, (l, r) in enumerate(pairs):
                    for t in range(2):
                        last = e.matmul(ps.ap[:, t * 512:(t + 1) * 512], l, r[:, t * 512:(t + 1) * 512],
                                        start=(i == 0), stop=(i == n - 1))
                return last
            P.op("pe", fn, reads=reads, writes=[ps])

        def conv_group(ps, dslot, ntap, src, base, reads):
            def fn(e):
                last = None
                for t in range(2):
                    for k in range(ntap):
                        o = base + k + t * 512
                        last = e.matmul(ps.ap[:, t * 512:(t + 1) * 512], dslot.ap[:, k * 128:(k + 1) * 128],
                                        src.ap[:, o:o + 512], start=(k == 0), stop=(k == ntap - 1))
                return last
            P.op("pe", fn, reads=reads, writes=[ps])

        def stat_group(ps, src, first, last_):
            def fn(e):
                l = None
                for t in range(2):
                    l = e.matmul(ps.ap[:, t * 512:(t + 1) * 512], ones.ap[:, :], src.ap[:, t * 512:(t + 1) * 512],
                                 start=first, stop=last_)
                return l
            P.op("pe", fn, reads=[ones, src], writes=[ps])

        def act(out_b, out_ap, in_b, in_ap, func, bias=None, scale=None, extra_reads=()):
            kw = {}
            if bias is not None:
                kw["bias"] = bias
            if scale is not None:
                kw["scale"] = scale
            P.op("act", lambda e: e.activation(out_ap, in_ap, func, **kw),
                 reads=[in_b] + list(extra_reads), writes=[out_b])

        def stt(out_b, out_ap, in0_b, in0_ap, scalar, in1_b, in1_ap, op0, op1, extra_reads=()):
            P.op("dve", lambda e: e.scalar_tensor_tensor(out_ap, in0_ap, scalar, in1_ap, op0, op1),
                 reads=[in0_b, in1_b] + list(extra_reads), writes=[out_b])

        def tt(out_b, out_ap, in0_b, in0_ap, in1_b, in1_ap, op):
            P.op("dve", lambda e: e.tensor_tensor(out_ap, in0_ap, in1_ap, op),
                 reads=[in0_b, in1_b], writes=[out_b])

        def load_x(j, s):
            xb = t32()
            sem = P.bufsem(xb)
            src = xT[j * 128:(j + 1) * 128, s * NT:(s + 1) * NT]
            P.dma("sp", lambda e: e.dma_start(out=xb.ap[:, :], in_=src), sem, writes=[xb])
            return xb

        def wdma(slot, items):
            sem = P.bufsem(slot)

            def fn(e):
                res = []
                for dst, src in items:
                    res.append(e.dma_start(out=dst, in_=src))
                return res
            P.dma("pool", fn, sem, writes=[slot], n=len(items))

        def s256(slot):
            return slot.ap[:, :].rearrange("p (k n) -> p k n", n=256)

        def s128(slot):
            return slot.ap[:, :].rearrange("p (k n) -> p k n", n=128)

        def rstd_from(ps_or_buf, in_ap, scale):
            lnv = t32()
            act(lnv, lnv.ap[:, :], ps_or_buf, in_ap, AF.Ln, bias=cst.ap[:, 0:1], scale=scale, extra_reads=[cst])
            act(rstd_bc, rstd_bc.ap[:, :], lnv, lnv.ap[:, :], AF.Exp, scale=-0.5)

        P.newsem("s_par")
        P.dma("sp", lambda e: e.dma_start(out=pp.ap[:, :], in_=ppd[:, :]), "s_par", writes=[pp])
        P.newsem("s_id")
        P.dma("pool", lambda e: e.dma_start(out=ident.ap[:, :], in_=identd[:, :]), "s_id", writes=[ident])
        P.newsem("s_em")
        P.dma("pool", lambda e: e.dma_start(out=emat.ap[:, :], in_=ematd[:, :]), "s_em", writes=[emat])
        P.op("pool", lambda e: e.memset(ones.ap[:, :], 1.0), writes=[ones])
        P.op("pool", lambda e: e.memset(cst.ap[:, 0:1], EPS), writes=[cst])
        P.op("pool", lambda e: e.memset(cst.ap[:, 1:2], 1.0), writes=[cst])
        P.op("dve", lambda e: e.tensor_scalar(dp.ap[:, 0:NHALF], pp.ap[:, 0:NHALF], 0.5, None, ALU.mult),
             reads=[pp], writes=[dp])
        act(tmp10, tmp10.ap[:, 0:10], pp, pp.ap[:, P_LAM:P_LAM + 10], AF.Exp, scale=-1.0)
        act(tmp10, tmp10.ap[:, 0:10], tmp10, tmp10.ap[:, 0:10], AF.Ln, bias=cst.ap[:, 1:2], extra_reads=[cst])
        P.op("dve", lambda e: e.tensor_scalar(dp.ap[:, D_C:D_C + 10], tmp10.ap[:, 0:10], -8.0, None, ALU.mult),
             reads=[tmp10], writes=[dp])
        P.op("dve", lambda e: e.tensor_scalar(dp.ap[:, D_CH:D_CH + 10], tmp10.ap[:, 0:10], -4.0, None, ALU.mult),
             reads=[tmp10], writes=[dp])
        P.op("dve", lambda e: e.tensor_scalar(dp.ap[:, D_BQ:D_BQ + 10], pp.ap[:, P_BIN + OFF_RG // 128:P_BIN + OFF_RG // 128 + 10],
                                              0.25, None, ALU.mult), reads=[pp], writes=[dp])

        units = []
        out_deps = []

        def add(run, wl=None, dl=None):
            units.append(Unit(run, wl, dl))

        pending_out = {}
        for s in range(NS):
            def norm1_stats(s_, dest, keep=None):
                ps = ps_reserve()
                for j in range(8):
                    xb = load_x(j, s_)
                    if keep is not None:
                        keep.append(xb)
                    sq = t16()
                    act(sq, sq.ap[:, :], xb, xb.ap[:, :], AF.Square)
                    stat_group(ps, sq, j == 0, j == 7)
                lnv = t32()
                act(lnv, lnv.ap[:, :], ps, ps.ap[:, :], AF.Ln, bias=cst.ap[:, 0:1], scale=1.0 / D, extra_reads=[cst])
                act(dest, dest.ap[:, :], lnv, lnv.ap[:, :], AF.Exp, scale=-0.5)
                ps_release(ps)

            def st_norm1(ws, ds, s=s):
                if s > 0:
                    inherit(hb + vb + grnn, gfb + h2)
                    src = mu_bc
                    kept = None
                else:
                    kept = []
                    norm1_stats(s, rstd_bc, kept)
                    src = rstd_bc
                for j in range(8):
                    xb = kept[j] if kept is not None else load_x(j, s)
                    stt(hb[j], hb[j].ap, xb, xb.ap[:, :], ppc(P_G1 + j), src, src.ap[:, :],
                        ALU.mult, ALU.mult, extra_reads=[pp])
            add(st_norm1)
            if "fn" in pending_out:
                add(pending_out.pop("fn"))

            def st_front_pre(ws, ds, s=s):
                if s > 0:
                    inherit(upad + xbbf, h2 + mb)
                    inherit(ybf + zrx, x1)
            add(st_front_pre)

            def mk_glu(j, s=s):
                def wl(slot):
                    v = s256(slot)
                    wdma(slot, [(v[:, 0:8, 0:128], w_in_v[:, 0:8, OFF_A + 128 * j:OFF_A + 128 * (j + 1)]),
                                (v[:, 0:8, 128:256], w_in_v[:, 0:8, OFF_B + 128 * j:OFF_B + 128 * (j + 1)])])

                def run(slot, ds):
                    v = s256(slot)
                    psA = ps_alloc()
                    psB = ps_alloc()
                    mm_group(psA, [(v[:, i, 0:128], hb[i].ap) for i in range(8)], [slot] + hb)
                    mm_group(psB, [(v[:, i, 128:256], hb[i].ap) for i in range(8)], [slot] + hb)
                    tb = t32()
                    act(tb, tb.ap[:, :], psB, psB.ap[:, :], AF.Tanh, bias=dpc(P_BIN + OFF_B // 128 + j), scale=0.5,
                        extra_reads=[dp])
                    za = t32()
                    P.op("dve", lambda e: e.tensor_scalar(za.ap[:, :], psA.ap[:, :], ppc(P_BIN + OFF_A // 128 + j), None,
                                                          ALU.add), reads=[psA, pp], writes=[za])
                    u = upad[j]
                    stt(u, u.ap[:, 32:32 + NT], tb, tb.ap[:, :], 1.0, za, za.ap[:, :], ALU.add, ALU.mult)
                    if s == 0:
                        P.op("pool", lambda e: e.memset(u.ap[:, 0:32], 0.0), writes=[u])
                        P.op("pool", lambda e: e.tensor_copy(uhalo.ap[:, j * 32:(j + 1) * 32], u.ap[:, NT:NT + 32]),
                             reads=[u], writes=[uhalo])
                    else:
                        P.op("pool", lambda e: e.tensor_copy(u.ap[:, 0:32], uhalo.ap[:, j * 32:(j + 1) * 32]),
                             reads=[uhalo], writes=[u])
                return Unit(run, wl=wl)

            cstate = {}

            def ustack(jc):
                u = upad[jc]
                for g in range(4):
                    def fn(e, g=g):
                        res = []
                        for sh in range(4):
                            res.append(e.dma_start(out=Ust[g].ap[32 * sh:32 * sh + 32, 0:1032],
                                                   in_=u.ap[32 * g:32 * g + 32, 24 - 8 * sh:24 - 8 * sh + 1032]))
                        return res
                    P.dma("sp", fn, P.bufsem(Ust[g]), reads=[u], writes=[Ust[g]], n=4)

            def mk_conv31(j, cstate=cstate):
                def dl(dslot):
                    o = dslot.ap[:, :].rearrange("p (k n) -> p k n", n=32)
                    i0 = emat.ap[:, :].unsqueeze(1).to_broadcast([128, 32, 32])
                    i1 = dp.ap[:, P_CW31 + j * 32:P_CW31 + (j + 1) * 32].unsqueeze(2).to_broadcast([128, 32, 32])
                    P.op("pool", lambda e: e.tensor_tensor(o, i0, i1, ALU.mult), reads=[emat, dp], writes=[dslot])

                def run(ws, dslot):
                    ps = ps_alloc()

                    def cfn(e):
                        last = None
                        for t in range(2):
                            for jj in range(8):
                                for g in range(4):
                                    o_ = 8 + t * 512 - jj
                                    last = e.matmul(ps.ap[32 * g:32 * g + 32, t * 512:(t + 1) * 512],
                                                    dslot.ap[:, (g * 8 + jj) * 32:(g * 8 + jj + 1) * 32],
                                                    Ust[g].ap[:, o_:o_ + 512], start=(jj == 0), stop=(jj == 7),
                                                    tile_position=(0, 32 * g))
                        return last
                    P.op("pe", cfn, reads=[dslot] + Ust, writes=[ps])
                    if j < 7:
                        ustack(j + 1)
                    P.op("dve", lambda e: e.tensor_scalar(ybf[j].ap, ps.ap[:, :], ppc(P_CDB + j), None, ALU.add),
                         reads=[ps, pp], writes=[ybf[j]])
                return Unit(run, dl=dl)

            def st_ln_stats(ws, ds, cstate=cstate):
                pm = ps_reserve()
                pq = ps_reserve()
                for j in range(8):
                    ysq = t16()
                    act(ysq, ysq.ap[:, :], ybf[j], ybf[j].ap, AF.Square)
                    stat_group(pm, ybf[j], j == 0, j == 7)
                    stat_group(pq, ysq, j == 0, j == 7)
                act(mu_bc, mu_bc.ap[:, :], pm, pm.ap[:, :], AF.Identity, scale=1.0 / D)
                msq = t32()
                tt(msq, msq.ap[:, :], mu_bc, mu_bc.ap[:, :], mu_bc, mu_bc.ap[:, :], ALU.mult)
                var = t32()
                stt(var, var.ap[:, :], pq, pq.ap[:, :], 1.0 / D, msq, msq.ap[:, :], ALU.mult, ALU.subtract)
                ps_release(pm)
                ps_release(pq)
                rstd_from(var, var.ap[:, :], 1.0)
                inherit(mb, upad + xbbf)
                inherit(tgbs, zrx)

            def mk_ln_apply(j):
                def run(ws, ds):
                    d = t32()
                    tt(d, d.ap[:, :], ybf[j], ybf[j].ap, mu_bc, mu_bc.ap[:, :], ALU.subtract)
                    tt(d, d.ap[:, :], d, d.ap[:, :], rstd_bc, rstd_bc.ap[:, :], ALU.mult)
                    n = t32()
                    act(n, n.ap[:, :], d, d.ap[:, :], AF.Identity, bias=ppc(P_LNB + j), scale=ppc(P_LNG + j),
                        extra_reads=[pp])
                    tn = t32()
                    act(tn, tn.ap[:, :], d, d.ap[:, :], AF.Tanh, bias=dpc(P_LNB + j), scale=dpc(P_LNG + j),
                        extra_reads=[dp])
                    stt(vb[j], vb[j].ap, tn, tn.ap[:, :], 1.0, n, n.ap[:, :], ALU.add, ALU.mult)
                return Unit(run)

            def mk_r1(j, s=s):
                c = j % 5

                def wl(slot):
                    v = s256(slot)
                    wdma(slot, [(v[:, 0:8, 0:128], w_in_v[:, 0:8, OFF_RX + 128 * j:OFF_RX + 128 * (j + 1)])])

                def run(slot, ds):
                    v = s256(slot)
                    ps = ps_alloc()
                    mm_group(ps, [(v[:, i, 0:128], hb[i].ap) for i in range(8)], [slot] + hb)
                    z = zrx[c]
                    P.op("dve", lambda e: e.tensor_scalar(z.ap[:, 4:4 + NT], ps.ap[:, :], ppc(P_BIN + OFF_RX // 128 + j),
                                                          None, ALU.add), reads=[ps, pp], writes=[z])
                    if s == 0:
                        P.op("pool", lambda e: e.memset(z.ap[:, 0:4], 0.0), writes=[z])
                        P.op("pool", lambda e: e.tensor_copy(zhalo.ap[:, j * 4:(j + 1) * 4], z.ap[:, NT:NT + 4]),
                             reads=[z], writes=[zhalo])
                    else:
                        P.op("pool", lambda e: e.tensor_copy(z.ap[:, 0:4], zhalo.ap[:, j * 4:(j + 1) * 4]),
                             reads=[zhalo], writes=[z])
                return Unit(run, wl=wl)

            rstate = {}

            def mk_conv4(j, s=s):
                c = j % 5

                def dl(dslot):
                    o = dslot.ap[:, 0:512].rearrange("p (k n) -> p k n", n=128)
                    i0 = ident.ap[:, :].unsqueeze(1).to_broadcast([128, 4, 128])
                    i1 = pp.ap[:, P_CW4 + j * 4:P_CW4 + (j + 1) * 4].unsqueeze(2).to_broadcast([128, 4, 128])
                    P.op("pool", lambda e: e.tensor_tensor(o, i0, i1, ALU.mult), reads=[ident, pp], writes=[dslot])

                def run(ws, dslot):
                    ps = ps_alloc()
                    conv_group(ps, dslot, 4, zrx[c], 1, [dslot, zrx[c]])
                    P.op("dve", lambda e: e.tensor_scalar(xbbf[c].ap, ps.ap[:, :], ppc(P_LCB + j), None, ALU.add),
                         reads=[ps, pp], writes=[xbbf[c]])
                return Unit(run, dl=dl)

            def mk_gate(j, s=s, rstate=rstate):
                g0 = (j // 5) * 5
                nb = list(range(max(g0, j - 1), min(g0 + 5, j + 2)))

                def wl(slot):
                    v = s256(slot)
                    n = len(nb)
                    wdma(slot, [(v[:, 0:n, 0:128], wa_v[:, nb[0]:nb[0] + n, 128 * j:128 * (j + 1)]),
                                (v[:, 0:n, 128:256], wx_v[:, nb[0]:nb[0] + n, 128 * j:128 * (j + 1)])])

                def run(slot, ds):
                    v = s256(slot)
                    psR = ps_alloc()
                    psI = ps_alloc()
                    rd = [slot] + [xbbf[i % 5] for i in nb]
                    mm_group(psR, [(v[:, ii, 0:128], xbbf[i % 5].ap) for ii, i in enumerate(nb)], rd)
                    mm_group(psI, [(v[:, ii, 128:256], xbbf[i % 5].ap) for ii, i in enumerate(nb)], rd)
                    tr = t32()
                    act(tr, tr.ap[:, :], psR, psR.ap[:, :], AF.Tanh, bias=dpc(P_BA + j), scale=0.5, extra_reads=[dp])
                    ti = t32()
                    act(ti, ti.ap[:, :], psI, psI.ap[:, :], AF.Tanh, bias=dpc(P_BX + j), scale=0.5, extra_reads=[dp])
                    a = t32()
                    act(a, a.ap[:, :], tr, tr.ap[:, :], AF.Exp, bias=dpc(D_CH + j), scale=dpc(D_CH + j),
                        extra_reads=[dp])
                    a2 = t32()
                    act(a2, a2.ap[:, :], tr, tr.ap[:, :], AF.Exp, bias=dpc(D_C + j), scale=dpc(D_C + j),
                        extra_reads=[dp])
                    act(a2, a2.ap[:, :], a2, a2.ap[:, :], AF.Sqrt, bias=cst.ap[:, 1:2], scale=-1.0, extra_reads=[cst])
                    xbc = xbbf[j % 5]
                    stt(ti, ti.ap[:, :], ti, ti.ap[:, :], 1.0, xbc, xbc.ap, ALU.add, ALU.mult)
                    tt(ti, ti.ap[:, :], ti, ti.ap[:, :], a2, a2.ap[:, :], ALU.mult)
                    hs = tr
                    init = 0.0 if s == 0 else hstate.ap[:, j:j + 1]
                    rds = [a, ti] + ([hstate] if s > 0 else [])
                    P.op("dve", lambda e: e.tensor_tensor_scan(hs.ap[:, :], a.ap[:, :], ti.ap[:, :], init,
                                                               ALU.mult, ALU.add), reads=rds, writes=[hs])
                    if s == 0:
                        P.op("dve", lambda e: e.tensor_copy(hstate.ap[:, j:j + 1], hs.ap[:, NT - 1:NT]),
                             reads=[hs], writes=[hstate])
                    rstate[j] = hs
                return Unit(run, wl=wl)

            gstate = {}

            def mk_gelu_a(j, gstate=gstate):
                def wl(slot):
                    v = s256(slot)
                    wdma(slot, [(v[:, 0:8, 0:128], w_in_v[:, 0:8, OFF_RG + 128 * j:OFF_RG + 128 * (j + 1)])])

                def run(slot, ds):
                    v = s256(slot)
                    ps = ps_alloc()
                    mm_group(ps, [(v[:, i, 0:128], hb[i].ap) for i in range(8)], [slot] + hb)
                    c = OFF_RG // 128 + j
                    zgq = t32()
                    act(zgq, zgq.ap[:, :], ps, ps.ap[:, :], AF.Identity, bias=dpc(D_BQ + j), scale=0.25, extra_reads=[dp])
                    q = t32()
                    act(q, q.ap[:, :], ps, ps.ap[:, :], AF.Square, bias=ppc(P_BIN + c), extra_reads=[pp])
                    P.op("dve", lambda e: e.tensor_scalar(q.ap[:, :], q.ap[:, :], 0.044715, 1.0, ALU.mult, ALU.add),
                         reads=[q], writes=[q])
                    tt(q, q.ap[:, :], q, q.ap[:, :], zgq, zgq.ap[:, :], ALU.mult)
                    gstate[j] = (zgq, q)
                return Unit(run, wl=wl)

            def mk_gelu_b(j, rstate=rstate, gstate=gstate):
                def run(ws, ds):
                    zgq, q = gstate[j]
                    act(q, q.ap[:, :], q, q.ap[:, :], AF.Tanh, scale=4.0 * 0.7978845608028654)
                    stt(q, q.ap[:, :], q, q.ap[:, :], 1.0, zgq, zgq.ap[:, :], ALU.add, ALU.mult)
                    hs = rstate[j]
                    tt(grnn[j], grnn[j].ap, q, q.ap[:, :], hs, hs.ap[:, :], ALU.mult)
                return Unit(run)

            def front_seq(g0, fillers, per):
                out = [mk_conv4(g0)]
                fi = 0
                for k in range(5):
                    j = g0 + k
                    if k < 4:
                        out.append(mk_conv4(j + 1))
                    out.append(mk_gelu_a(j))
                    if per[k] >= 1 and fi < len(fillers):
                        out.append(fillers[fi]); fi += 1
                    out.append(mk_gate(j))
                    if per[k] >= 2 and fi < len(fillers):
                        out.append(fillers[fi]); fi += 1
                    if per[k] >= 3 and fi < len(fillers):
                        out.append(fillers[fi]); fi += 1
                    out.append(mk_gelu_b(j))
                out += fillers[fi:]
                return out

            G_ = [mk_glu(j) for j in range(8)]
            C_ = [mk_conv31(j) for j in range(8)]
            U0 = Unit(lambda ws, ds: ustack(0))
            fl = [G_[0], U0, G_[1], G_[2], C_[0], G_[3], C_[1], G_[4], C_[2], G_[5], C_[3],
                  G_[6], C_[4], G_[7], C_[5], C_[6], C_[7]]
            units.extend([mk_r1(j) for j in range(5)])
            units.extend(front_seq(0, fl[:11], [3, 2, 2, 2, 2]))
            units.extend([mk_r1(j) for j in range(5, 10)])
            units.extend(front_seq(5, fl[11:], [2, 1, 1, 1, 1]))
            add(st_ln_stats)

            def mk_g(j):
                def wlg(slot):
                    v = s256(slot)
                    wdma(slot, [(v[:, 0:8, 0:128], w_in_v[:, 0:8, OFF_GA + 128 * j:OFF_GA + 128 * (j + 1)]),
                                (v[:, 0:8, 128:256], w_in_v[:, 0:8, OFF_GB + 128 * j:OFF_GB + 128 * (j + 1)])])

                def rung(slot, ds):
                    v = s256(slot)
                    psa = ps_alloc()
                    psb = ps_alloc()
                    mm_group(psa, [(v[:, i, 0:128], hb[i].ap) for i in range(8)], [slot] + hb)
                    mm_group(psb, [(v[:, i, 128:256], hb[i].ap) for i in range(8)], [slot] + hb)
                    act(mb[j], mb[j].ap, psa, psa.ap[:, :], AF.Tanh, bias=dpc(P_BIN + OFF_GA // 128 + j), scale=0.5,
                        extra_reads=[dp])
                    act(tgbs[j], tgbs[j].ap, psb, psb.ap[:, :], AF.Tanh, bias=dpc(P_BIN + OFF_GB // 128 + j),
                        scale=0.5, extra_reads=[dp])
                return Unit(rung, wl=wlg)

            def mk_y(j):
                def wly(slot):
                    v = s256(slot)
                    wdma(slot, [(v[:, 0:8, 0:128], w_co_v[:, 0:8, 128 * j:128 * (j + 1)]),
                                (v[:, 0:10, 128:256], w_lo_v[:, 0:10, 128 * j:128 * (j + 1)])])

                def runy(slot, ds):
                    v = s256(slot)
                    psa = ps_alloc()
                    psb = ps_alloc()
                    mm_group(psa, [(v[:, i, 0:128], vb[i].ap) for i in range(8)], [slot] + vb)
                    mm_group(psb, [(v[:, i, 128:256], grnn[i].ap) for i in range(10)], [slot] + grnn)
                    ya = t32()
                    act(ya, ya.ap[:, :], psa, psa.ap[:, :], AF.Identity, bias=ppc(P_CBO + j), scale=0.5, extra_reads=[pp])
                    stt(ya, ya.ap[:, :], mb[j], mb[j].ap, 1.0, ya, ya.ap[:, :], ALU.add, ALU.mult)
                    t2 = t32()
                    stt(t2, t2.ap[:, :], tgbs[j], tgbs[j].ap, 1.0, psb, psb.ap[:, :], ALU.add, ALU.mult)
                    tt(mb[j], mb[j].ap, ya, ya.ap[:, :], t2, t2.ap[:, :], ALU.add)
                return Unit(runy, wl=wly)

            for j in range(8):
                units.append(mk_ln_apply(j))
                units.append(mk_g(j))
            for j in range(8):
                units.append(mk_y(j))

            xstate = {}

            def st_mix_pre(ws, ds, xstate=xstate):
                inherit(x1, zrx + ybf + tgbs)
                inherit(h2, grnn + xbbf + gfb)
                xstate["ps"] = ps_reserve()
            add(st_mix_pre)
            for j in range(8):
                def wl(slot, j=j):
                    v = s256(slot)
                    wdma(slot, [(v[:, 0:8, 0:128], w_mx_v[:, 0:8, 128 * j:128 * (j + 1)])])

                def run(slot, ds, j=j, s=s, xstate=xstate):
                    v = s256(slot)
                    ps = ps_alloc()
                    mm_group(ps, [(v[:, i, 0:128], mb[i].ap) for i in range(8)], [slot] + mb)
                    if j > 0:
                        stat_group(xstate["ps"], xstate["sq"], j - 1 == 0, False)
                    xb = load_x(j, s)
                    stt(x1[j], x1[j].ap, ps, ps.ap[:, :], 0.5, xb, xb.ap[:, :], ALU.mult, ALU.add)
                    sq = t16()
                    act(sq, sq.ap[:, :], x1[j], x1[j].ap, AF.Square)
                    xstate["sq"] = sq
                    P.op("dve", lambda e: e.tensor_scalar(h2[j].ap, x1[j].ap, ppc(P_G2 + j), None, ALU.mult),
                         reads=[x1[j], pp], writes=[h2[j]])
                add(run, wl=wl)

            def st_norm2(ws, ds, xstate=xstate):
                ps = xstate["ps"]
                stat_group(ps, xstate["sq"], False, True)
                rstd_from(ps, ps.ap[:, :], 1.0 / D)
                ps_release(ps)
                inherit(gfb, hb + vb + grnn)
            add(st_norm2)

            for k in range(22):
                def wl(slot, k=k):
                    v = s256(slot)
                    wdma(slot, [(v[:, 0:8, 0:128], w_f1_v[:, 0:8, 128 * k:128 * (k + 1)]),
                                (v[:, 0:8, 128:256], w_f3_v[:, 0:8, 128 * k:128 * (k + 1)])])

                def run(slot, ds, k=k):
                    v = s256(slot)
                    p1 = ps_alloc()
                    p3 = ps_alloc()
                    mm_group(p1, [(v[:, i, 0:128], h2[i].ap) for i in range(8)], [slot] + h2)
                    mm_group(p3, [(v[:, i, 128:256], h2[i].ap) for i in range(8)], [slot] + h2)
                    s1 = t32()
                    tt(s1, s1.ap[:, :], p1, p1.ap[:, :], rstd_bc, rstd_bc.ap[:, :], ALU.mult)
                    act(s1, s1.ap[:, :], s1, s1.ap[:, :], AF.Silu)
                    t3 = t32()
                    tt(t3, t3.ap[:, :], p3, p3.ap[:, :], rstd_bc, rstd_bc.ap[:, :], ALU.mult)
                    tt(gfb[k], gfb[k].ap, s1, s1.ap[:, :], t3, t3.ap[:, :], ALU.mult)
                add(run, wl=wl)

            fstate = {}

            def st_dn_pre(ws, ds, fstate=fstate, s=s):
                if s + 1 < NS:
                    norm1_stats(s + 1, mu_bc)
                fstate["ps"] = ps_reserve()
            add(st_dn_pre)
            for j in range(8):
                def wl(slot, j=j):
                    v = s128(slot)
                    wdma(slot, [(v[:, 0:22, 0:128], w_f2_v[:, 0:22, 128 * j:128 * (j + 1)])])

                def run(slot, ds, j=j, fstate=fstate):
                    v = s128(slot)
                    ps = ps_alloc()
                    mm_group(ps, [(v[:, k, 0:128], gfb[k].ap) for k in range(22)], [slot] + gfb)
                    if j > 0:
                        stat_group(fstate["ps"], fstate["sq"], j - 1 == 0, False)
                    tt(x1[j], x1[j].ap, ps, ps.ap[:, :], x1[j], x1[j].ap, ALU.add)
                    sq = t16()
                    act(sq, sq.ap[:, :], x1[j], x1[j].ap, AF.Square)
                    fstate["sq"] = sq
                add(run, wl=wl)

            def st_out(ws, ds, s=s, fstate=fstate):
                ps = fstate["ps"]
                stat_group(ps, fstate["sq"], False, True)
                rstd_from(ps, ps.ap[:, :], 1.0 / D)
                ps_release(ps)
                for j in range(8):
                    o = t32()
                    stt(o, o.ap[:, :], x1[j], x1[j].ap, ppc(P_GF + j), rstd_bc, rstd_bc.ap[:, :],
                        ALU.mult, ALU.mult, extra_reads=[pp])
                    sem = P.bufsem(o)
                    dst = outT[j * 128:(j + 1) * 128, s * NT:(s + 1) * NT]
                    out_deps.append(P.dma("sp", lambda e, o=o, dst=dst: e.dma_start(out=dst, in_=o.ap[:, :]),
                                          sem, reads=[o]))
            pending_out["fn"] = st_out

        add(pending_out.pop("fn"))

        wl_idx = [i for i, u in enumerate(units) if u.wl is not None]
        dl_idx = [i for i, u in enumerate(units) if u.dl is not None]
        wslot_of, dslot_of = {}, {}
        wp = dp_ = 0
        wdone = ddone = 0
        if max_units is not None:
            units = units[:max_units]
            wl_idx = [i for i in wl_idx if i < max_units]
            dl_idx = [i for i in dl_idx if i < max_units]
        for idx, u in enumerate(units):
            while wp < len(wl_idx) and (wp - wdone) < NW and wl_idx[wp] <= idx + 8:
                ui = wl_idx[wp]
                slot = wslots[wp % NW]
                wslot_of[ui] = slot
                units[ui].wl(slot)
                wp += 1
            while dp_ < len(dl_idx) and (dp_ - ddone) < ND and dl_idx[dp_] <= idx + 4:
                ui = dl_idx[dp_]
                slot = dslots[dp_ % ND]
                dslot_of[ui] = slot
                units[ui].dl(slot)
                dp_ += 1
            u.run(wslot_of.get(idx), dslot_of.get(idx))
            if u.wl is not None:
                wdone += 1
            if u.dl is not None:
                ddone += 1

        P.wait_all("sp", out_deps)
        P.emit()
    return nc


def _pack_params(inp):
    def cols(v):
        v = np.asarray(v, np.float32).reshape(-1)
        return v.reshape(-1, 128).T
    cw31 = np.asarray(inp["conv_dw_w"][0], np.float32)
    cw31c = np.zeros((128, 256), np.float32)
    for jc in range(8):
        for g in range(4):
            for jj in range(8):
                for sft in range(4):
                    d = 8 * sft + jj
                    if d <= 30:
                        ch0 = jc * 128 + 32 * g
                        cw31c[32 * sft:32 * sft + 32, jc * 32 + g * 8 + jj] = cw31[30 - d, ch0:ch0 + 32]
    cw4 = np.asarray(inp["lru_conv_w"][0], np.float32)
    cw4c = np.concatenate([cw4[:, j * 128:(j + 1) * 128].T for j in range(10)], axis=1)
    parts = [cols(inp["b_in"][0]), cols(inp["lru_conv_b"][0]), cols(inp["lru_ba"][0]), cols(inp["lru_bx"][0]),
             cols(inp["conv_ln_g"][0]), cols(inp["conv_ln_b"][0]), cw31c,
             cols(inp["norm1_g"][0]), cols(inp["conv_dw_b"][0]), cols(inp["conv_b_out"][0]),
             cols(inp["lru_lambda"][0]), cols(inp["norm2_g"][0]), cols(inp["norm_f_g"]), cw4c]
    pp = np.ascontiguousarray(np.concatenate(parts, axis=1), dtype=np.float32)
    assert pp.shape == (128, NPP), pp.shape
    return pp


def _blockdiag(w):
    w = np.asarray(w, np.float32)
    out = np.zeros((DR, DR), np.float32)
    for h in range(16):
        out[h * 80:(h + 1) * 80, h * 80:(h + 1) * 80] = w[h]
    return out


_NC_CACHE = {}


def kernel(**inp):
    x = np.asarray(inp["x"], np.float32)
    nb = x.shape[0]
    if "nc" not in _NC_CACHE:
        _NC_CACHE["nc"] = build_program()
    nc = _NC_CACHE["nc"]
    shared = {
        "w_in": np.ascontiguousarray(inp["w_in"][0], dtype=np.float32),
        "w_co": np.ascontiguousarray(inp["conv_w_out"][0], dtype=np.float32),
        "w_lo": np.ascontiguousarray(inp["lru_w_out"][0], dtype=np.float32),
        "w_mx": np.ascontiguousarray(inp["w_mix_out"][0], dtype=np.float32),
        "w_f1": np.ascontiguousarray(inp["ffn_w1"][0], dtype=np.float32),
        "w_f3": np.ascontiguousarray(inp["ffn_w3"][0], dtype=np.float32),
        "w_f2": np.ascontiguousarray(inp["ffn_w2"][0], dtype=np.float32),
        "wa_bd": _blockdiag(inp["lru_wa"][0]),
        "wx_bd": _blockdiag(inp["lru_wx"][0]),
        "pp": _pack_params(inp),
        "ident": np.eye(128, dtype=np.float32),
        "emat": np.ascontiguousarray(np.tile(np.eye(32, dtype=np.float32), (4, 1))),
    }
    in_maps = []
    for b in range(nb):
        m = dict(shared)
        m["xT"] = np.ascontiguousarray(x[b].T)
        in_maps.append(m)
    res = run_bass_kernel_spmd(nc, in_maps, core_ids=list(range(nb)))
    out = np.stack([np.ascontiguousarray(r["outT"].T) for r in res.results], axis=0)
    return out.astype(np.float32)
```

```python
import numpy as np
from contextlib import ExitStack
import concourse.bass as bass
import concourse.mybir as mybir
from concourse.bass_utils import run_bass_kernel_spmd

F32 = mybir.dt.float32
BF16 = mybir.dt.bfloat16
AF = mybir.ActivationFunctionType
ALU = mybir.AluOpType

D = 1024
S = 2048
NT = 1024
NS = S // NT
DR = 1280
DFF = 2816
DIN = 6656
OFF_A, OFF_B, OFF_RX, OFF_RG, OFF_GA, OFF_GB = 0, 1024, 2048, 3328, 4608, 5632
EPS = 1e-6

P_BIN, P_LCB, P_BA, P_BX, P_LNG, P_LNB, P_CW31 = 0, 52, 62, 72, 82, 90, 98
NHALF = 354
P_G1, P_CDB, P_CBO, P_LAM, P_G2, P_GF, P_CW4 = 354, 362, 370, 378, 388, 396, 404
NPP = 444
D_C, D_CH = 354, 364
D_BQ = 374
NDP = 384


class Buf:
    def __init__(self, name, ap):
        self.name = name
        self.ap = ap
        self.w = []
        self.r = []
        self.sem = None


class Prog:
    ENGS = ("pe", "act", "dve", "pool", "sp")

    def __init__(self, nc, es):
        self.nc = nc
        self.es = es
        self.sems = {}
        self.semval = {}
        self.ops = {e: [] for e in self.ENGS}
        self.waited = {e: {} for e in self.ENGS}
        for e in self.ENGS:
            self.newsem("c_" + e)

    def newsem(self, key):
        self.sems[key] = self.es.enter_context(self.nc.semaphore(key))
        self.semval[key] = 0
        return key

    def bufsem(self, b):
        if b.sem is None:
            b.sem = self.newsem("s_" + b.name)
        return b.sem

    def _waits(self, eng, reads, writes):
        own = "c_" + eng
        need = {}

        def add(d, same_ok):
            k, v = d
            if k == own and (eng == "pe" or not same_ok):
                return
            if v > need.get(k, 0):
                need[k] = v
        for b in reads:
            for d in b.w:
                add(d, True)
        for b in writes:
            for d in b.w:
                add(d, True)
            for d in b.r:
                add(d, True)
        out = []
        wd = self.waited[eng]
        for k, v in need.items():
            if wd.get(k, 0) >= v:
                continue
            wd[k] = v
            out.append((k, v))
        return out

    def _mark(self, dep, reads, writes):
        for b in reads:
            b.r.append(dep)
        for b in writes:
            b.w = [dep]
            b.r = []

    def op(self, eng, fn, reads=(), writes=()):
        waits = self._waits(eng, reads, writes)
        key = "c_" + eng
        self.semval[key] += 1
        dep = (key, self.semval[key])
        self.ops[eng].append((waits, fn, (key, 1)))
        self._mark(dep, reads, writes)
        return dep

    def dma(self, eng, fn, semkey, reads=(), writes=(), n=1):
        waits = self._waits(eng, reads, writes)
        self.semval[semkey] += 16 * n
        dep = (semkey, self.semval[semkey])
        self.ops[eng].append((waits, fn, (semkey, 16)))
        self._mark(dep, reads, writes)
        return dep

    def wait_all(self, eng, deps):
        waits = []
        wd = self.waited[eng]
        for k, v in deps:
            if wd.get(k, 0) < v:
                wd[k] = v
                waits.append((k, v))
        self.ops[eng].append((waits, None, None))

    def emit(self):
        nc = self.nc
        with nc.Block() as block:
            def run(name):
                def f(e):
                    for waits, fn, incs in self.ops[name]:
                        for k, v in waits:
                            e.wait_ge(self.sems[k], v)
                        if fn is None:
                            continue
                        ins = fn(e)
                        if incs is not None:
                            if isinstance(ins, (list, tuple)):
                                for i_ in ins:
                                    i_.then_inc(self.sems[incs[0]], incs[1])
                            else:
                                ins.then_inc(self.sems[incs[0]], incs[1])
                return f
            block.tensor(run("pe"))
            block.scalar(run("act"))
            block.vector(run("dve"))
            block.gpsimd(run("pool"))
            block.sync(run("sp"))


def inherit(news, olds):
    deps = []
    for b in olds:
        deps.extend(b.w)
        deps.extend(b.r)
    for n in news:
        n.r = list(n.r) + deps


class Unit:
    def __init__(self, run, wl=None, dl=None):
        self.run = run
        self.wl = wl
        self.dl = dl


def build_program(max_units=None):
    nc = bass.Bass("TRN2", target_bir_lowering=False)
    es = ExitStack()

    def din(name, shape):
        return nc.dram_tensor(name, shape, F32, kind="ExternalInput").ap()
    xT = din("xT", [D, S])
    w_in = din("w_in", [D, DIN])
    w_co = din("w_co", [D, D])
    w_lo = din("w_lo", [DR, D])
    w_mx = din("w_mx", [D, D])
    w_f1 = din("w_f1", [D, DFF])
    w_f3 = din("w_f3", [D, DFF])
    w_f2 = din("w_f2", [DFF, D])
    wa_bd = din("wa_bd", [DR, DR])
    wx_bd = din("wx_bd", [DR, DR])
    ppd = din("pp", [128, NPP])
    identd = din("ident", [128, 128])
    ematd = din("emat", [128, 32])
    outT = nc.dram_tensor("outT", [D, S], F32, kind="ExternalOutput").ap()

    def wv(w):
        return w.rearrange("(k p) n -> p k n", p=128)
    w_in_v, w_co_v, w_lo_v, w_mx_v = wv(w_in), wv(w_co), wv(w_lo), wv(w_mx)
    w_f1_v, w_f3_v, w_f2_v, wa_v, wx_v = wv(w_f1), wv(w_f3), wv(w_f2), wv(wa_bd), wv(wx_bd)

    with es:
        P = Prog(nc, es)

        def sb(name, shape, dt):
            return es.enter_context(nc.sbuf_tensor(name, shape, dt))

        pp = Buf("pp", sb("pp_sb", [128, NPP], F32))
        dp = Buf("dp", sb("dp_sb", [128, NDP], F32))
        cst = Buf("cst", sb("cst_sb", [128, 4], F32))
        tmp10 = Buf("tmp10", sb("tmp10", [128, 16], F32))
        hstate = Buf("hstate", sb("hstate", [128, 16], F32))
        ident = Buf("ident", sb("ident_sb", [128, 128], BF16))
        ones = Buf("ones", sb("ones_sb", [128, 128], BF16))
        emat = Buf("emat", sb("emat_sb", [128, 32], BF16))
        Ust = [Buf(f"Ust{g}", sb(f"Ust{g}", [128, 1032], BF16)) for g in range(4)]
        uhalo = Buf("uhalo", sb("uhalo", [128, 8 * 32], BF16))
        zhalo = Buf("zhalo", sb("zhalo", [128, 10 * 4], BF16))
        regA = sb("regA", [128, 16384], BF16)
        regB = sb("regB", [128, 13824], BF16)
        regC = sb("regC", [128, 26 * 1024], BF16)
        NT32 = 10
        t32s = [Buf(f"t32_{i}", sb(f"t32_{i}", [128, NT], F32)) for i in range(NT32)]
        t16s = [Buf(f"t16_{i}", sb(f"t16_{i}", [128, NT], BF16)) for i in range(2)]
        rstd_bc = Buf("rstd_bc", sb("rstd_bc", [128, NT], F32))
        mu_bc = Buf("mu_bc", sb("mu_bc", [128, NT], F32))
        NW = 5
        wslots = [Buf(f"ws{i}", sb(f"ws{i}", [128, 2304], BF16)) for i in range(NW)]
        ND = 2
        dslots = [Buf(f"ds{i}", sb(f"ds{i}", [128, 1024], BF16)) for i in range(ND)]
        pss = [Buf(f"ps{i}", es.enter_context(nc.psum_tensor(f"ps{i}", [128, NT], F32))) for i in range(4)]

        ybf = [Buf(f"ybf{j}", regA[:, j * 1024:(j + 1) * 1024]) for j in range(8)]
        zrx = [Buf(f"zrx{j}", regA[:, 8192 + j * 1028:8192 + (j + 1) * 1028]) for j in range(5)]
        x1 = [Buf(f"x1_{j}", regA[:, j * 2048:(j + 1) * 2048].bitcast(F32)) for j in range(8)]
        upad = [Buf(f"upad{j}", regB[:, j * 1056:(j + 1) * 1056]) for j in range(8)]
        tgbs = [Buf(f"tgbs{j}", regA[:, 8192 + j * 1024:8192 + (j + 1) * 1024]) for j in range(8)]
        xbbf = [Buf(f"xbbf{j}", regB[:, 8448 + j * 1024:8448 + (j + 1) * 1024]) for j in range(5)]
        mb = [Buf(f"m{j}", regB[:, j * 1024:(j + 1) * 1024]) for j in range(8)]
        h2 = [Buf(f"h2_{j}", regC[:, 22528 + j * 1024:22528 + (j + 1) * 1024]) for j in range(4)] + \
             [Buf(f"h2_{j}", regB[:, 8448 + (j - 4) * 1024:8448 + (j - 3) * 1024]) for j in range(4, 8)]
        hb = [Buf(f"h{j}", regC[:, j * 1024:(j + 1) * 1024]) for j in range(8)]
        vb = [Buf(f"v{j}", regC[:, (8 + j) * 1024:(9 + j) * 1024]) for j in range(8)]
        grnn = [Buf(f"grnn{j}", regC[:, (16 + j) * 1024:(17 + j) * 1024]) for j in range(10)]
        gfb = [Buf(f"gf{k}", regC[:, k * 1024:(k + 1) * 1024]) for k in range(22)]

        state = {"t32": 0, "t16": 0, "ps": 0, "reserved": set()}

        def t32():
            b = t32s[state["t32"] % NT32]
            state["t32"] += 1
            return b

        def t16():
            b = t16s[state["t16"] % 2]
            state["t16"] += 1
            return b

        def ps_alloc():
            while True:
                i = state["ps"] % 4
                state["ps"] += 1
                if i not in state["reserved"]:
                    return pss[i]

        def ps_reserve():
            b = ps_alloc()
            state["reserved"].add(pss.index(b))
            return b

        def ps_release(b):
            state["reserved"].discard(pss.index(b))

        def ppc(c):
            return pp.ap[:, c:c + 1]

        def dpc(c):
            return dp.ap[:, c:c + 1]

        def mm_group(ps, pairs, reads):
            n = len(pairs)

            def fn(e):
                last = None
                for i, (l, r) in enumerate(pairs):
                    for t in range(2):
                        last = e.matmul(ps.ap[:, t * 512:(t + 1) * 512], l, r[:, t * 512:(t + 1) * 512],
                                        start=(i == 0), stop=(i == n - 1))
                return last
            P.op("pe", fn, reads=reads, writes=[ps])

        def mm_part(ps, pairs, reads, first, last_):
            n = len(pairs)

            def fn(e):
                last = None
                for i, (l, r) in enumerate(pairs):
                    for t in range(2):
                        last = e.matmul(ps.ap[:, t * 512:(t + 1) * 512], l, r[:, t * 512:(t + 1) * 512],
                                        start=(first and i == 0), stop=(last_ and i == n - 1))
                return last
            P.op("pe", fn, reads=reads, writes=[ps])

        def conv_group(ps, dslot, ntap, src, base, reads):
            def fn(e):
                last = None
                for t in range(2):
                    for k in range(ntap):
                        o = base + k + t * 512
                        last = e.matmul(ps.ap[:, t * 512:(t + 1) * 512], dslot.ap[:, k * 128:(k + 1) * 128],
                                        src.ap[:, o:o + 512], start=(k == 0), stop=(k == ntap - 1))
                return last
            P.op("pe", fn, reads=reads, writes=[ps])

        def stat_group(ps, src, first, last_):
            def fn(e):
                l = None
                for t in range(2):
                    l = e.matmul(ps.ap[:, t * 512:(t + 1) * 512], ones.ap[:, :], src.ap[:, t * 512:(t + 1) * 512],
                                 start=first, stop=last_)
                return l
            P.op("pe", fn, reads=[ones, src], writes=[ps])

        def act(out_b, out_ap, in_b, in_ap, func, bias=None, scale=None, extra_reads=()):
            kw = {}
            if bias is not None:
                kw["bias"] = bias
            if scale is not None:
                kw["scale"] = scale
            P.op("act", lambda e: e.activation(out_ap, in_ap, func, **kw),
                 reads=[in_b] + list(extra_reads), writes=[out_b])

        def stt(out_b, out_ap, in0_b, in0_ap, scalar, in1_b, in1_ap, op0, op1, extra_reads=()):
            P.op("dve", lambda e: e.scalar_tensor_tensor(out_ap, in0_ap, scalar, in1_ap, op0, op1),
                 reads=[in0_b, in1_b] + list(extra_reads), writes=[out_b])

        def tt(out_b, out_ap, in0_b, in0_ap, in1_b, in1_ap, op):
            P.op("dve", lambda e: e.tensor_tensor(out_ap, in0_ap, in1_ap, op),
                 reads=[in0_b, in1_b], writes=[out_b])

        def load_x(j, s):
            xb = t32()
            sem = P.bufsem(xb)
            src = xT[j * 128:(j + 1) * 128, s * NT:(s + 1) * NT]
            P.dma("sp", lambda e: e.dma_start(out=xb.ap[:, :], in_=src), sem, writes=[xb])
            return xb

        def wdma(slot, items):
            sem = P.bufsem(slot)

            def fn(e):
                res = []
                for dst, src in items:
                    res.append(e.dma_start(out=dst, in_=src))
                return res
            P.dma("pool", fn, sem, writes=[slot], n=len(items))

        def s256(slot):
            return slot.ap[:, :].rearrange("p (k n) -> p k n", n=256)

        def s128(slot):
            return slot.ap[:, :].rearrange("p (k n) -> p k n", n=128)

        def rstd_from(ps_or_buf, in_ap, scale):
            lnv = t32()
            act(lnv, lnv.ap[:, :], ps_or_buf, in_ap, AF.Ln, bias=cst.ap[:, 0:1], scale=scale, extra_reads=[cst])
            act(rstd_bc, rstd_bc.ap[:, :], lnv, lnv.ap[:, :], AF.Exp, scale=-0.5)

        P.newsem("s_par")
        P.dma("sp", lambda e: e.dma_start(out=pp.ap[:, :], in_=ppd[:, :]), "s_par", writes=[pp])
        P.newsem("s_id")
        P.dma("pool", lambda e: e.dma_start(out=ident.ap[:, :], in_=identd[:, :]), "s_id", writes=[ident])
        P.newsem("s_em")
        P.dma("pool", lambda e: e.dma_start(out=emat.ap[:, :], in_=ematd[:, :]), "s_em", writes=[emat])
        P.op("pool", lambda e: e.memset(ones.ap[:, :], 1.0), writes=[ones])
        P.op("pool", lambda e: e.memset(cst.ap[:, 0:1], EPS), writes=[cst])
        P.op("pool", lambda e: e.memset(cst.ap[:, 1:2], 1.0), writes=[cst])
        P.op("dve", lambda e: e.tensor_scalar(dp.ap[:, 0:NHALF], pp.ap[:, 0:NHALF], 0.5, None, ALU.mult),
             reads=[pp], writes=[dp])
        act(tmp10, tmp10.ap[:, 0:10], pp, pp.ap[:, P_LAM:P_LAM + 10], AF.Exp, scale=-1.0)
        act(tmp10, tmp10.ap[:, 0:10], tmp10, tmp10.ap[:, 0:10], AF.Ln, bias=cst.ap[:, 1:2], extra_reads=[cst])
        P.op("dve", lambda e: e.tensor_scalar(dp.ap[:, D_C:D_C + 10], tmp10.ap[:, 0:10], -8.0, None, ALU.mult),
             reads=[tmp10], writes=[dp])
        P.op("dve", lambda e: e.tensor_scalar(dp.ap[:, D_CH:D_CH + 10], tmp10.ap[:, 0:10], -4.0, None, ALU.mult),
             reads=[tmp10], writes=[dp])
        P.op("dve", lambda e: e.tensor_scalar(dp.ap[:, D_BQ:D_BQ + 10], pp.ap[:, P_BIN + OFF_RG // 128:P_BIN + OFF_RG // 128 + 10],
                                              0.25, None, ALU.mult), reads=[pp], writes=[dp])

        units = []
        out_deps = []

        def add(run, wl=None, dl=None):
            units.append(Unit(run, wl, dl))

        pending_out = {}
        for s in range(NS):
            def norm1_stats(s_, dest, keep=None):
                ps = ps_reserve()
                for j in range(8):
                    xb = load_x(j, s_)
                    if keep is not None:
                        keep.append(xb)
                    sq = t16()
                    act(sq, sq.ap[:, :], xb, xb.ap[:, :], AF.Square)
                    stat_group(ps, sq, j == 0, j == 7)
                lnv = t32()
                act(lnv, lnv.ap[:, :], ps, ps.ap[:, :], AF.Ln, bias=cst.ap[:, 0:1], scale=1.0 / D, extra_reads=[cst])
                act(dest, dest.ap[:, :], lnv, lnv.ap[:, :], AF.Exp, scale=-0.5)
                ps_release(ps)

            def st_norm1(ws, ds, s=s):
                if s > 0:
                    inherit(hb + vb + grnn, gfb + h2)
                    src = mu_bc
                    kept = None
                else:
                    kept = []
                    norm1_stats(s, rstd_bc, kept)
                    src = rstd_bc
                for j in range(8):
                    xb = kept[j] if kept is not None else load_x(j, s)
                    stt(hb[j], hb[j].ap, xb, xb.ap[:, :], ppc(P_G1 + j), src, src.ap[:, :],
                        ALU.mult, ALU.mult, extra_reads=[pp])
            add(st_norm1)
            if "fn" in pending_out:
                add(pending_out.pop("fn"))

            def st_front_pre(ws, ds, s=s):
                if s > 0:
                    inherit(upad + xbbf, h2 + mb)
                    inherit(ybf + zrx, x1)
            add(st_front_pre)

            def mk_glu(j, s=s):
                def wl(slot):
                    v = s256(slot)
                    wdma(slot, [(v[:, 0:8, 0:128], w_in_v[:, 0:8, OFF_A + 128 * j:OFF_A + 128 * (j + 1)]),
                                (v[:, 0:8, 128:256], w_in_v[:, 0:8, OFF_B + 128 * j:OFF_B + 128 * (j + 1)])])

                def run(slot, ds):
                    v = s256(slot)
                    psA = ps_alloc()
                    psB = ps_alloc()
                    mm_group(psA, [(v[:, i, 0:128], hb[i].ap) for i in range(8)], [slot] + hb)
                    mm_group(psB, [(v[:, i, 128:256], hb[i].ap) for i in range(8)], [slot] + hb)
                    tb = t32()
                    act(tb, tb.ap[:, :], psB, psB.ap[:, :], AF.Tanh, bias=dpc(P_BIN + OFF_B // 128 + j), scale=0.5,
                        extra_reads=[dp])
                    za = t32()
                    P.op("dve", lambda e: e.tensor_scalar(za.ap[:, :], psA.ap[:, :], ppc(P_BIN + OFF_A // 128 + j), None,
                                                          ALU.add), reads=[psA, pp], writes=[za])
                    u = upad[j]
                    stt(u, u.ap[:, 32:32 + NT], tb, tb.ap[:, :], 1.0, za, za.ap[:, :], ALU.add, ALU.mult)
                    if s == 0:
                        P.op("pool", lambda e: e.memset(u.ap[:, 0:32], 0.0), writes=[u])
                        P.op("pool", lambda e: e.tensor_copy(uhalo.ap[:, j * 32:(j + 1) * 32], u.ap[:, NT:NT + 32]),
                             reads=[u], writes=[uhalo])
                    else:
                        P.op("pool", lambda e: e.tensor_copy(u.ap[:, 0:32], uhalo.ap[:, j * 32:(j + 1) * 32]),
                             reads=[uhalo], writes=[u])
                return Unit(run, wl=wl)

            cstate = {}

            def ustack(jc):
                u = upad[jc]
                for g in range(4):
                    def fn(e, g=g):
                        res = []
                        for sh in range(4):
                            res.append(e.dma_start(out=Ust[g].ap[32 * sh:32 * sh + 32, 0:1032],
                                                   in_=u.ap[32 * g:32 * g + 32, 24 - 8 * sh:24 - 8 * sh + 1032]))
                        return res
                    P.dma("sp", fn, P.bufsem(Ust[g]), reads=[u], writes=[Ust[g]], n=4)

            def mk_conv31(j, cstate=cstate):
                def dl(dslot):
                    o = dslot.ap[:, :].rearrange("p (k n) -> p k n", n=32)
                    i0 = emat.ap[:, :].unsqueeze(1).to_broadcast([128, 32, 32])
                    i1 = dp.ap[:, P_CW31 + j * 32:P_CW31 + (j + 1) * 32].unsqueeze(2).to_broadcast([128, 32, 32])
                    P.op("pool", lambda e: e.tensor_tensor(o, i0, i1, ALU.mult), reads=[emat, dp], writes=[dslot])

                def run(ws, dslot):
                    ps = ps_alloc()

                    def cfn(e):
                        last = None
                        for t in range(2):
                            for jj in range(8):
                                for g in range(4):
                                    o_ = 8 + t * 512 - jj
                                    last = e.matmul(ps.ap[32 * g:32 * g + 32, t * 512:(t + 1) * 512],
                                                    dslot.ap[:, (g * 8 + jj) * 32:(g * 8 + jj + 1) * 32],
                                                    Ust[g].ap[:, o_:o_ + 512], start=(jj == 0), stop=(jj == 7),
                                                    tile_position=(0, 32 * g))
                        return last
                    P.op("pe", cfn, reads=[dslot] + Ust, writes=[ps])
                    if j < 7:
                        ustack(j + 1)
                    P.op("dve", lambda e: e.tensor_scalar(ybf[j].ap, ps.ap[:, :], ppc(P_CDB + j), None, ALU.add),
                         reads=[ps, pp], writes=[ybf[j]])
                return Unit(run, dl=dl)

            def st_ln_stats(ws, ds, cstate=cstate):
                pm = ps_reserve()
                pq = ps_reserve()
                for j in range(8):
                    ysq = t16()
                    act(ysq, ysq.ap[:, :], ybf[j], ybf[j].ap, AF.Square)
                    stat_group(pm, ybf[j], j == 0, j == 7)
                    stat_group(pq, ysq, j == 0, j == 7)
                act(mu_bc, mu_bc.ap[:, :], pm, pm.ap[:, :], AF.Identity, scale=1.0 / D)
                msq = t32()
                tt(msq, msq.ap[:, :], mu_bc, mu_bc.ap[:, :], mu_bc, mu_bc.ap[:, :], ALU.mult)
                var = t32()
                stt(var, var.ap[:, :], pq, pq.ap[:, :], 1.0 / D, msq, msq.ap[:, :], ALU.mult, ALU.subtract)
                ps_release(pm)
                ps_release(pq)
                rstd_from(var, var.ap[:, :], 1.0)
                inherit(mb, upad + xbbf)
                inherit(tgbs, zrx)

            def mk_ln_apply(j):
                def run(ws, ds):
                    d = t32()
                    tt(d, d.ap[:, :], ybf[j], ybf[j].ap, mu_bc, mu_bc.ap[:, :], ALU.subtract)
                    tt(d, d.ap[:, :], d, d.ap[:, :], rstd_bc, rstd_bc.ap[:, :], ALU.mult)
                    n = t32()
                    act(n, n.ap[:, :], d, d.ap[:, :], AF.Identity, bias=ppc(P_LNB + j), scale=ppc(P_LNG + j),
                        extra_reads=[pp])
                    tn = t32()
                    act(tn, tn.ap[:, :], d, d.ap[:, :], AF.Tanh, bias=dpc(P_LNB + j), scale=dpc(P_LNG + j),
                        extra_reads=[dp])
                    stt(vb[j], vb[j].ap, tn, tn.ap[:, :], 1.0, n, n.ap[:, :], ALU.add, ALU.mult)
                return Unit(run)

            def mk_r1(j, s=s):
                c = j % 5

                def wl(slot):
                    v = s256(slot)
                    wdma(slot, [(v[:, 0:8, 0:128], w_in_v[:, 0:8, OFF_RX + 128 * j:OFF_RX + 128 * (j + 1)])])

                def run(slot, ds):
                    v = s256(slot)
                    ps = ps_alloc()
                    mm_group(ps, [(v[:, i, 0:128], hb[i].ap) for i in range(8)], [slot] + hb)
                    z = zrx[c]
                    P.op("dve", lambda e: e.tensor_scalar(z.ap[:, 4:4 + NT], ps.ap[:, :], ppc(P_BIN + OFF_RX // 128 + j),
                                                          None, ALU.add), reads=[ps, pp], writes=[z])
                    if s == 0:
                        P.op("pool", lambda e: e.memset(z.ap[:, 0:4], 0.0), writes=[z])
                        P.op("pool", lambda e: e.tensor_copy(zhalo.ap[:, j * 4:(j + 1) * 4], z.ap[:, NT:NT + 4]),
                             reads=[z], writes=[zhalo])
                    else:
                        P.op("pool", lambda e: e.tensor_copy(z.ap[:, 0:4], zhalo.ap[:, j * 4:(j + 1) * 4]),
                             reads=[zhalo], writes=[z])
                return Unit(run, wl=wl)

            rstate = {}

            def mk_conv4(j, s=s):
                c = j % 5

                def dl(dslot):
                    o = dslot.ap[:, 0:512].rearrange("p (k n) -> p k n", n=128)
                    i0 = ident.ap[:, :].unsqueeze(1).to_broadcast([128, 4, 128])
                    i1 = pp.ap[:, P_CW4 + j * 4:P_CW4 + (j + 1) * 4].unsqueeze(2).to_broadcast([128, 4, 128])
                    P.op("pool", lambda e: e.tensor_tensor(o, i0, i1, ALU.mult), reads=[ident, pp], writes=[dslot])

                def run(ws, dslot):
                    ps = ps_alloc()
                    conv_group(ps, dslot, 4, zrx[c], 1, [dslot, zrx[c]])
                    P.op("dve", lambda e: e.tensor_scalar(xbbf[c].ap, ps.ap[:, :], ppc(P_LCB + j), None, ALU.add),
                         reads=[ps, pp], writes=[xbbf[c]])
                return Unit(run, dl=dl)

            def mk_gate(j, s=s, rstate=rstate):
                g0 = (j // 5) * 5
                nb = list(range(max(g0, j - 1), min(g0 + 5, j + 2)))

                def wl(slot):
                    v = s256(slot)
                    n = len(nb)
                    wdma(slot, [(v[:, 0:n, 0:128], wa_v[:, nb[0]:nb[0] + n, 128 * j:128 * (j + 1)]),
                                (v[:, 0:n, 128:256], wx_v[:, nb[0]:nb[0] + n, 128 * j:128 * (j + 1)])])

                def run(slot, ds):
                    v = s256(slot)
                    psR = ps_alloc()
                    psI = ps_alloc()
                    rd = [slot] + [xbbf[i % 5] for i in nb]
                    mm_group(psR, [(v[:, ii, 0:128], xbbf[i % 5].ap) for ii, i in enumerate(nb)], rd)
                    mm_group(psI, [(v[:, ii, 128:256], xbbf[i % 5].ap) for ii, i in enumerate(nb)], rd)
                    tr = t32()
                    act(tr, tr.ap[:, :], psR, psR.ap[:, :], AF.Tanh, bias=dpc(P_BA + j), scale=0.5, extra_reads=[dp])
                    ti = t32()
                    act(ti, ti.ap[:, :], psI, psI.ap[:, :], AF.Tanh, bias=dpc(P_BX + j), scale=0.5, extra_reads=[dp])
                    a = t32()
                    act(a, a.ap[:, :], tr, tr.ap[:, :], AF.Exp, bias=dpc(D_CH + j), scale=dpc(D_CH + j),
                        extra_reads=[dp])
                    a2 = t32()
                    act(a2, a2.ap[:, :], tr, tr.ap[:, :], AF.Exp, bias=dpc(D_C + j), scale=dpc(D_C + j),
                        extra_reads=[dp])
                    act(a2, a2.ap[:, :], a2, a2.ap[:, :], AF.Sqrt, bias=cst.ap[:, 1:2], scale=-1.0, extra_reads=[cst])
                    xbc = xbbf[j % 5]
                    stt(ti, ti.ap[:, :], ti, ti.ap[:, :], 1.0, xbc, xbc.ap, ALU.add, ALU.mult)
                    tt(ti, ti.ap[:, :], ti, ti.ap[:, :], a2, a2.ap[:, :], ALU.mult)
                    hs = tr
                    init = 0.0 if s == 0 else hstate.ap[:, j:j + 1]
                    rds = [a, ti] + ([hstate] if s > 0 else [])
                    P.op("dve", lambda e: e.tensor_tensor_scan(hs.ap[:, :], a.ap[:, :], ti.ap[:, :], init,
                                                               ALU.mult, ALU.add), reads=rds, writes=[hs])
                    if s == 0:
                        P.op("dve", lambda e: e.tensor_copy(hstate.ap[:, j:j + 1], hs.ap[:, NT - 1:NT]),
                             reads=[hs], writes=[hstate])
                    rstate[j] = hs
                return Unit(run, wl=wl)

            gstate = {}

            def mk_gelu_a(j, gstate=gstate):
                def wl(slot):
                    v = s256(slot)
                    wdma(slot, [(v[:, 0:8, 0:128], w_in_v[:, 0:8, OFF_RG + 128 * j:OFF_RG + 128 * (j + 1)])])

                def run(slot, ds):
                    v = s256(slot)
                    ps = ps_alloc()
                    mm_group(ps, [(v[:, i, 0:128], hb[i].ap) for i in range(8)], [slot] + hb)
                    c = OFF_RG // 128 + j
                    zgq = t32()
                    act(zgq, zgq.ap[:, :], ps, ps.ap[:, :], AF.Identity, bias=dpc(D_BQ + j), scale=0.25, extra_reads=[dp])
                    q = t32()
                    act(q, q.ap[:, :], ps, ps.ap[:, :], AF.Square, bias=ppc(P_BIN + c), extra_reads=[pp])
                    P.op("dve", lambda e: e.tensor_scalar(q.ap[:, :], q.ap[:, :], 0.044715, 1.0, ALU.mult, ALU.add),
                         reads=[q], writes=[q])
                    tt(q, q.ap[:, :], q, q.ap[:, :], zgq, zgq.ap[:, :], ALU.mult)
                    gstate[j] = (zgq, q)
                return Unit(run, wl=wl)

            def mk_gelu_b(j, rstate=rstate, gstate=gstate):
                def run(ws, ds):
                    zgq, q = gstate[j]
                    act(q, q.ap[:, :], q, q.ap[:, :], AF.Tanh, scale=4.0 * 0.7978845608028654)
                    stt(q, q.ap[:, :], q, q.ap[:, :], 1.0, zgq, zgq.ap[:, :], ALU.add, ALU.mult)
                    hs = rstate[j]
                    tt(grnn[j], grnn[j].ap, q, q.ap[:, :], hs, hs.ap[:, :], ALU.mult)
                return Unit(run)

            def front_seq(g0, fillers, per):
                out = [mk_conv4(g0)]
                fi = 0
                for k in range(5):
                    j = g0 + k
                    if k < 4:
                        out.append(mk_conv4(j + 1))
                    out.append(mk_gelu_a(j))
                    if per[k] >= 1 and fi < len(fillers):
                        out.append(fillers[fi]); fi += 1
                    out.append(mk_gate(j))
                    if per[k] >= 2 and fi < len(fillers):
                        out.append(fillers[fi]); fi += 1
                    if per[k] >= 3 and fi < len(fillers):
                        out.append(fillers[fi]); fi += 1
                    out.append(mk_gelu_b(j))
                out += fillers[fi:]
                return out

            G_ = [mk_glu(j) for j in range(8)]
            C_ = [mk_conv31(j) for j in range(8)]
            U0 = Unit(lambda ws, ds: ustack(0))
            fl = [G_[0], U0, G_[1], G_[2], C_[0], G_[3], C_[1], G_[4], C_[2], G_[5], C_[3],
                  G_[6], C_[4], G_[7], C_[5], C_[6], C_[7]]
            units.extend([mk_r1(j) for j in range(5)])
            units.extend(front_seq(0, fl[:11], [3, 2, 2, 2, 2]))
            units.extend([mk_r1(j) for j in range(5, 10)])
            units.extend(front_seq(5, fl[11:], [2, 1, 1, 1, 1]))
            add(st_ln_stats)

            def mk_g(j):
                def wlg(slot):
                    v = s256(slot)
                    wdma(slot, [(v[:, 0:8, 0:128], w_in_v[:, 0:8, OFF_GA + 128 * j:OFF_GA + 128 * (j + 1)]),
                                (v[:, 0:8, 128:256], w_in_v[:, 0:8, OFF_GB + 128 * j:OFF_GB + 128 * (j + 1)])])

                def rung(slot, ds):
                    v = s256(slot)
                    psa = ps_alloc()
                    psb = ps_alloc()
                    mm_group(psa, [(v[:, i, 0:128], hb[i].ap) for i in range(8)], [slot] + hb)
                    mm_group(psb, [(v[:, i, 128:256], hb[i].ap) for i in range(8)], [slot] + hb)
                    act(mb[j], mb[j].ap, psa, psa.ap[:, :], AF.Tanh, bias=dpc(P_BIN + OFF_GA // 128 + j), scale=0.5,
                        extra_reads=[dp])
                    act(tgbs[j], tgbs[j].ap, psb, psb.ap[:, :], AF.Tanh, bias=dpc(P_BIN + OFF_GB // 128 + j),
                        scale=0.5, extra_reads=[dp])
                return Unit(rung, wl=wlg)

            def mk_y(j):
                def wly(slot):
                    v = s128(slot)
                    wdma(slot, [(v[:, 0:8, :], w_co_v[:, 0:8, 128 * j:128 * (j + 1)]),
                                (v[:, 8:18, :], w_lo_v[:, 0:10, 128 * j:128 * (j + 1)])])

                def runy(slot, ds):
                    v = s128(slot)
                    psa = ps_alloc()
                    psb = ps_alloc()
                    mm_group(psa, [(v[:, i, :], vb[i].ap) for i in range(8)], [slot] + vb)
                    mm_group(psb, [(v[:, 8 + i, :], grnn[i].ap) for i in range(10)], [slot] + grnn)
                    ya = t32()
                    act(ya, ya.ap[:, :], psa, psa.ap[:, :], AF.Identity, bias=ppc(P_CBO + j), scale=0.5, extra_reads=[pp])
                    stt(ya, ya.ap[:, :], mb[j], mb[j].ap, 1.0, ya, ya.ap[:, :], ALU.add, ALU.mult)
                    t2 = t32()
                    stt(t2, t2.ap[:, :], tgbs[j], tgbs[j].ap, 1.0, psb, psb.ap[:, :], ALU.add, ALU.mult)
                    tt(mb[j], mb[j].ap, ya, ya.ap[:, :], t2, t2.ap[:, :], ALU.add)
                return Unit(runy, wl=wly)

            for j in range(8):
                units.append(mk_ln_apply(j))
                units.append(mk_g(j))
            for j in range(8):
                units.append(mk_y(j))

            xstate = {}

            def st_mix_pre(ws, ds, xstate=xstate):
                inherit(x1, zrx + ybf + tgbs)
                inherit(h2, grnn + xbbf + gfb)
                xstate["ps"] = ps_reserve()
            add(st_mix_pre)
            for j in range(8):
                def wl(slot, j=j):
                    v = s256(slot)
                    wdma(slot, [(v[:, 0:8, 0:128], w_mx_v[:, 0:8, 128 * j:128 * (j + 1)])])

                def run(slot, ds, j=j, s=s, xstate=xstate):
                    v = s256(slot)
                    ps = ps_alloc()
                    mm_group(ps, [(v[:, i, 0:128], mb[i].ap) for i in range(8)], [slot] + mb)
                    if j > 0:
                        stat_group(xstate["ps"], xstate["sq"], j - 1 == 0, False)
                    xb = load_x(j, s)
                    stt(x1[j], x1[j].ap, ps, ps.ap[:, :], 0.5, xb, xb.ap[:, :], ALU.mult, ALU.add)
                    sq = t16()
                    act(sq, sq.ap[:, :], x1[j], x1[j].ap, AF.Square)
                    xstate["sq"] = sq
                    P.op("dve", lambda e: e.tensor_scalar(h2[j].ap, x1[j].ap, ppc(P_G2 + j), None, ALU.mult),
                         reads=[x1[j], pp], writes=[h2[j]])
                add(run, wl=wl)

            def st_norm2(ws, ds, xstate=xstate):
                ps = xstate["ps"]
                stat_group(ps, xstate["sq"], False, True)
                rstd_from(ps, ps.ap[:, :], 1.0 / D)
                ps_release(ps)
                inherit(gfb, hb + vb + grnn)
            add(st_norm2)

            for k in range(22):
                def wl(slot, k=k):
                    v = s256(slot)
                    wdma(slot, [(v[:, 0:8, 0:128], w_f1_v[:, 0:8, 128 * k:128 * (k + 1)]),
                                (v[:, 0:8, 128:256], w_f3_v[:, 0:8, 128 * k:128 * (k + 1)])])

                def run(slot, ds, k=k):
                    v = s256(slot)
                    p1 = ps_alloc()
                    p3 = ps_alloc()
                    mm_group(p1, [(v[:, i, 0:128], h2[i].ap) for i in range(8)], [slot] + h2)
                    mm_group(p3, [(v[:, i, 128:256], h2[i].ap) for i in range(8)], [slot] + h2)
                    s1 = t32()
                    tt(s1, s1.ap[:, :], p1, p1.ap[:, :], rstd_bc, rstd_bc.ap[:, :], ALU.mult)
                    act(s1, s1.ap[:, :], s1, s1.ap[:, :], AF.Silu)
                    t3 = t32()
                    tt(t3, t3.ap[:, :], p3, p3.ap[:, :], rstd_bc, rstd_bc.ap[:, :], ALU.mult)
                    tt(gfb[k], gfb[k].ap, s1, s1.ap[:, :], t3, t3.ap[:, :], ALU.mult)
                add(run, wl=wl)

            fstate = {}

            def st_dn_pre(ws, ds, fstate=fstate, s=s):
                if s + 1 < NS:
                    norm1_stats(s + 1, mu_bc)
                fstate["ps"] = ps_reserve()
            add(st_dn_pre)
            for j in range(8):
                def wlA(slot, j=j):
                    v = s128(slot)
                    wdma(slot, [(v[:, 0:11, :], w_f2_v[:, 0:11, 128 * j:128 * (j + 1)])])

                def runA(slot, ds, j=j, fstate=fstate):
                    v = s128(slot)
                    ps = ps_alloc()
                    fstate["psj"] = ps
                    mm_part(ps, [(v[:, k, :], gfb[k].ap) for k in range(11)], [slot] + gfb[:11], True, False)
                add(runA, wl=wlA)

                def wl(slot, j=j):
                    v = s128(slot)
                    wdma(slot, [(v[:, 0:11, :], w_f2_v[:, 11:22, 128 * j:128 * (j + 1)])])

                def run(slot, ds, j=j, fstate=fstate):
                    v = s128(slot)
                    ps = fstate["psj"]
                    mm_part(ps, [(v[:, k, :], gfb[11 + k].ap) for k in range(11)], [slot] + gfb[11:], False, True)
                    if j > 0:
                        stat_group(fstate["ps"], fstate["sq"], j - 1 == 0, False)
                    tt(x1[j], x1[j].ap, ps, ps.ap[:, :], x1[j], x1[j].ap, ALU.add)
                    sq = t16()
                    act(sq, sq.ap[:, :], x1[j], x1[j].ap, AF.Square)
                    fstate["sq"] = sq
                add(run, wl=wl)

            def st_out(ws, ds, s=s, fstate=fstate):
                ps = fstate["ps"]
                stat_group(ps, fstate["sq"], False, True)
                rstd_from(ps, ps.ap[:, :], 1.0 / D)
                ps_release(ps)
                for j in range(8):
                    o = t32()
                    stt(o, o.ap[:, :], x1[j], x1[j].ap, ppc(P_GF + j), rstd_bc, rstd_bc.ap[:, :],
                        ALU.mult, ALU.mult, extra_reads=[pp])
                    sem = P.bufsem(o)
                    dst = outT[j * 128:(j + 1) * 128, s * NT:(s + 1) * NT]
                    out_deps.append(P.dma("sp", lambda e, o=o, dst=dst: e.dma_start(out=dst, in_=o.ap[:, :]),
                                          sem, reads=[o]))
            pending_out["fn"] = st_out

        add(pending_out.pop("fn"))

        wl_idx = [i for i, u in enumerate(units) if u.wl is not None]
        dl_idx = [i for i, u in enumerate(units) if u.dl is not None]
        wslot_of, dslot_of = {}, {}
        wp = dp_ = 0
        wdone = ddone = 0
        if max_units is not None:
            units = units[:max_units]
            wl_idx = [i for i in wl_idx if i < max_units]
            dl_idx = [i for i in dl_idx if i < max_units]
        for idx, u in enumerate(units):
            while wp < len(wl_idx) and (wp - wdone) < NW and wl_idx[wp] <= idx + 10:
                ui = wl_idx[wp]
                slot = wslots[wp % NW]
                wslot_of[ui] = slot
                units[ui].wl(slot)
                wp += 1
            while dp_ < len(dl_idx) and (dp_ - ddone) < ND and dl_idx[dp_] <= idx + 4:
                ui = dl_idx[dp_]
                slot = dslots[dp_ % ND]
                dslot_of[ui] = slot
                units[ui].dl(slot)
                dp_ += 1
            u.run(wslot_of.get(idx), dslot_of.get(idx))
            if u.wl is not None:
                wdone += 1
            if u.dl is not None:
                ddone += 1

        P.wait_all("sp", out_deps)
        P.emit()
    return nc


def _pack_params(inp):
    def cols(v):
        v = np.asarray(v, np.float32).reshape(-1)
        return v.reshape(-1, 128).T
    cw31 = np.asarray(inp["conv_dw_w"][0], np.float32)
    cw31c = np.zeros((128, 256), np.float32)
    for jc in range(8):
        for g in range(4):
            for jj in range(8):
                for sft in range(4):
                    d = 8 * sft + jj
                    if d <= 30:
                        ch0 = jc * 128 + 32 * g
                        cw31c[32 * sft:32 * sft + 32, jc * 32 + g * 8 + jj] = cw31[30 - d, ch0:ch0 + 32]
    cw4 = np.asarray(inp["lru_conv_w"][0], np.float32)
    cw4c = np.concatenate([cw4[:, j * 128:(j + 1) * 128].T for j in range(10)], axis=1)
    parts = [cols(inp["b_in"][0]), cols(inp["lru_conv_b"][0]), cols(inp["lru_ba"][0]), cols(inp["lru_bx"][0]),
             cols(inp["conv_ln_g"][0]), cols(inp["conv_ln_b"][0]), cw31c,
             cols(inp["norm1_g"][0]), cols(inp["conv_dw_b"][0]), cols(inp["conv_b_out"][0]),
             cols(inp["lru_lambda"][0]), cols(inp["norm2_g"][0]), cols(inp["norm_f_g"]), cw4c]
    pp = np.ascontiguousarray(np.concatenate(parts, axis=1), dtype=np.float32)
    assert pp.shape == (128, NPP), pp.shape
    return pp


def _blockdiag(w):
    w = np.asarray(w, np.float32)
    out = np.zeros((DR, DR), np.float32)
    for h in range(16):
        out[h * 80:(h + 1) * 80, h * 80:(h + 1) * 80] = w[h]
    return out


_NC_CACHE = {}


def kernel(**inp):
    x = np.asarray(inp["x"], np.float32)
    nb = x.shape[0]
    if "nc" not in _NC_CACHE:
        _NC_CACHE["nc"] = build_program()
    nc = _NC_CACHE["nc"]
    shared = {
        "w_in": np.ascontiguousarray(inp["w_in"][0], dtype=np.float32),
        "w_co": np.ascontiguousarray(inp["conv_w_out"][0], dtype=np.float32),
        "w_lo": np.ascontiguousarray(inp["lru_w_out"][0], dtype=np.float32),
        "w_mx": np.ascontiguousarray(inp["w_mix_out"][0], dtype=np.float32),
        "w_f1": np.ascontiguousarray(inp["ffn_w1"][0], dtype=np.float32),
        "w_f3": np.ascontiguousarray(inp["ffn_w3"][0], dtype=np.float32),
        "w_f2": np.ascontiguousarray(inp["ffn_w2"][0], dtype=np.float32),
        "wa_bd": _blockdiag(inp["lru_wa"][0]),
        "wx_bd": _blockdiag(inp["lru_wx"][0]),
        "pp": _pack_params(inp),
        "ident": np.eye(128, dtype=np.float32),
        "emat": np.ascontiguousarray(np.tile(np.eye(32, dtype=np.float32), (4, 1))),
    }
    in_maps = []
    for b in range(nb):
        m = dict(shared)
        m["xT"] = np.ascontiguousarray(x[b].T)
        in_maps.append(m)
    res = run_bass_kernel_spmd(nc, in_maps, core_ids=list(range(nb)))
    out = np.stack([np.ascontiguousarray(r["outT"].T) for r in res.results], axis=0)
    return out.astype(np.float32)
```
